# Optimizing a Trainium2 kernel written in Bass

```python
import math
import jax, jax.numpy as jnp
from jax import lax
import numpy as np

D_MODEL = 1024
BATCH = 4
SEQ = 4096
DEPTH = 2

MEM_LEN = 256
BRANCH_WIDTH = 512
N_BRANCHES = 5
GDN_HEADS = 4
GDN_HEAD_DIM = 128
GDN_CONV = 4
GDN_CHUNK = 64
CONF_WIDTH = 31
SWA_Q_HEADS = 8
SWA_KV_HEADS = 2
SWA_HEAD_DIM = 64
WINDOW = 128
REL_BUCKETS = 32
REL_MAX_DIST = 128
LRU_BLOCKS = 8
LRU_BLOCK_W = BRANCH_WIDTH // LRU_BLOCKS
LRU_CONV = 4
LRU_C = 8.0
XATTN_HEADS = 4
XATTN_HEAD_DIM = BRANCH_WIDTH // XATTN_HEADS
NORM_EPS = 1e-6

SPLIT_SIZES = (3 * BRANCH_WIDTH, BRANCH_WIDTH, GDN_HEADS, GDN_HEADS,
               2 * BRANCH_WIDTH, BRANCH_WIDTH,
               SWA_Q_HEADS * SWA_HEAD_DIM, 2 * SWA_KV_HEADS * SWA_HEAD_DIM, BRANCH_WIDTH,
               BRANCH_WIDTH, BRANCH_WIDTH,
               XATTN_HEADS * XATTN_HEAD_DIM, BRANCH_WIDTH,
               N_BRANCHES * D_MODEL)
IN_COLS = sum(SPLIT_SIZES)

kernel_name = "hybrid_gated_five_mixer_block"


def _rms_norm(x, g):
    xf = x.astype(jnp.float32)
    y = xf * lax.rsqrt(jnp.mean(xf * xf, axis=-1, keepdims=True) + NORM_EPS)
    return (y * g.astype(jnp.float32)).astype(x.dtype)


def _layer_norm(x, g, b):
    xf = x.astype(jnp.float32)
    mu = jnp.mean(xf, axis=-1, keepdims=True)
    var = jnp.mean(jnp.square(xf - mu), axis=-1, keepdims=True)
    y = (xf - mu) * lax.rsqrt(var + NORM_EPS)
    return (y * g.astype(jnp.float32) + b.astype(jnp.float32)).astype(x.dtype)


def _l2_norm(x):
    return x * lax.rsqrt(jnp.sum(x * x, axis=-1, keepdims=True) + NORM_EPS)


def _causal_dwconv(x, w, b=None):
    K, C = w.shape
    y = lax.conv_general_dilated(x, w[:, None, :].astype(x.dtype), window_strides=(1,),
                                 padding=((K - 1, 0),), dimension_numbers=('NWC', 'WIO', 'NWC'),
                                 feature_group_count=C)
    if b is not None:
        y = y + b.astype(x.dtype)
    return y


def _gated_delta_rule(q, k, v, g, beta):
    B, S, H, DK = q.shape
    DV = v.shape[-1]
    C = GDN_CHUNK
    N = S // C

    def chunks(t):
        return jnp.swapaxes(t.reshape((B, N, C, H) + t.shape[3:]), 2, 3)

    qc, kc, vc, gc, bc = (chunks(t) for t in (q, k, v, g, beta))
    G = jnp.cumsum(gc, axis=-1)
    idx = jnp.arange(C)
    tril = idx[:, None] >= idx[None, :]
    strict = idx[:, None] > idx[None, :]
    L = jnp.exp(jnp.where(tril, G[..., :, None] - G[..., None, :], -jnp.inf))
    kb = kc * bc[..., None]
    A = jnp.where(strict, jnp.einsum('bnhid,bnhjd->bnhij', kb, kc) * L, 0.0)
    rhs = jnp.concatenate([vc * bc[..., None], kb * jnp.exp(G)[..., None]], axis=-1)
    sol = lax.linalg.triangular_solve(A + jnp.eye(C, dtype=A.dtype), rhs, left_side=True, lower=True)
    u, w = sol[..., :DV], sol[..., DV:]
    attn = jnp.einsum('bnhid,bnhjd->bnhij', qc, kc) * L
    q_dec = qc * jnp.exp(G)[..., None]
    k_dec = kc * jnp.exp(G[..., -1:] - G)[..., None]
    chunk_decay = jnp.exp(G[..., -1])

    def step(state, inp):
        u_n, w_n, attn_n, qd_n, kd_n, cd_n = inp
        v_new = u_n - jnp.einsum('bhck,bhkv->bhcv', w_n, state)
        o = jnp.einsum('bhck,bhkv->bhcv', qd_n, state) + jnp.einsum('bhij,bhjv->bhiv', attn_n, v_new)
        state = state * cd_n[..., None, None] + jnp.einsum('bhck,bhcv->bhkv', kd_n, v_new)
        return state, o

    xs = tuple(jnp.moveaxis(t, 1, 0) for t in (u, w, attn, q_dec, k_dec, chunk_decay))
    s0 = jnp.zeros((B, H, DK, DV), q.dtype)
    _, o = lax.scan(step, s0, xs)
    o = jnp.moveaxis(o, 0, 1)
    return jnp.swapaxes(o, 2, 3).reshape(B, S, H, DV)


def _t5_bucket(dist):
    n = np.maximum(dist, 0)
    max_exact = REL_BUCKETS // 2
    large = max_exact + (np.log(np.maximum(n, 1) / max_exact) / np.log(REL_MAX_DIST / max_exact)
                         * (REL_BUCKETS - max_exact)).astype(np.int32)
    large = np.minimum(large, REL_BUCKETS - 1)
    return np.where(n < max_exact, n, large).astype(np.int32)


def _swa_sinks(q, k, v, sinks, band_bias):
    B, S, HQ, hd = q.shape
    HKV = k.shape[2]
    G = HQ // HKV
    W = WINDOW
    N = S // W
    qb = q.reshape(B, N, W, HKV, G, hd)

    def band(t):
        tb = t.reshape(B, N, W, HKV, hd)
        prev = jnp.pad(tb, ((0, 0), (1, 0), (0, 0), (0, 0), (0, 0)))[:, :-1]
        return jnp.concatenate([prev, tb], axis=2)

    kb, vb = band(k), band(v)
    s = jnp.einsum('bniKgd,bnjKd->bnKgij', qb, kb).astype(jnp.float32) * (hd ** -0.5)
    s = s + band_bias.astype(jnp.float32).reshape(1, 1, HKV, G, W, 2 * W)
    i = jnp.arange(W)[:, None]
    j = jnp.arange(2 * W)[None, :]
    dist = i + W - j
    valid = (dist >= 0) & (dist < W)
    valid = valid[None] & ((jnp.arange(N)[:, None, None] > 0) | (j >= W)[None])
    s = jnp.where(valid[None, :, None, None], s, -jnp.inf)
    sink = sinks.astype(jnp.float32).reshape(1, 1, HKV, G, 1, 1)
    m = jnp.maximum(jnp.max(s, axis=-1, keepdims=True), sink)
    p = jnp.exp(s - m)
    p = p / (jnp.sum(p, axis=-1, keepdims=True) + jnp.exp(sink - m))
    o = jnp.einsum('bnKgij,bnjKd->bniKgd', p.astype(v.dtype), vb)
    return o.reshape(B, S, HQ, hd)


def _rglru(x, w_a, b_a, w_x, b_x, lam):
    B, S, W = x.shape
    xf = x.astype(jnp.float32)
    xb = xf.reshape(B, S, LRU_BLOCKS, LRU_BLOCK_W)
    r = jax.nn.sigmoid(jnp.einsum('bsnc,ncd->bsnd', xb, w_a.astype(jnp.float32)).reshape(B, S, W) + b_a)
    ig = jax.nn.sigmoid(jnp.einsum('bsnc,ncd->bsnd', xb, w_x.astype(jnp.float32)).reshape(B, S, W) + b_x)
    log_a = -LRU_C * r * jax.nn.softplus(-lam.astype(jnp.float32))
    a = jnp.exp(log_a)
    bterm = jnp.sqrt(-jnp.expm1(2.0 * log_a)) * (ig * xf)

    def combine(left, right):
        a1, b1 = left
        a2, b2 = right
        return a1 * a2, a2 * b1 + b2

    _, h = lax.associative_scan(combine, (a, bterm), axis=1)
    return h.astype(x.dtype)


def _hybrid_mixer(h, mem, w_in, b_in, a_conv_w, a_log, a_dt_bias, a_norm_g,
                  b_dw_w, b_dw_b, b_ln_g, b_ln_b, c_sinks, band_bias,
                  d_conv_w, d_conv_b, d_w_a, d_b_a, d_w_x, d_b_x, d_lambda,
                  g_mem, w_mem_kv, w_br, w_out):
    B, S, _ = h.shape
    dt = h.dtype
    proj = h @ w_in + b_in
    offsets = [int(o) for o in np.cumsum(SPLIT_SIZES)[:-1]]
    (a_qkv, a_z, a_beta, a_alpha, b_glu, b_z, c_q, c_kv, c_z,
     d_x, d_z, e_q, e_z, gates) = jnp.split(proj, offsets, axis=-1)

    qkv = jax.nn.silu(_causal_dwconv(a_qkv, a_conv_w)).astype(jnp.float32)
    aq, ak, av = jnp.split(qkv, 3, axis=-1)
    aq = _l2_norm(aq.reshape(B, S, GDN_HEADS, GDN_HEAD_DIM)) * (GDN_HEAD_DIM ** -0.5)
    ak = _l2_norm(ak.reshape(B, S, GDN_HEADS, GDN_HEAD_DIM))
    av = av.reshape(B, S, GDN_HEADS, GDN_HEAD_DIM)
    beta = jax.nn.sigmoid(a_beta.astype(jnp.float32))
    g = -jnp.exp(a_log.astype(jnp.float32)) * jax.nn.softplus(a_alpha.astype(jnp.float32) + a_dt_bias)
    ao = _gated_delta_rule(aq, ak, av, g, beta)
    ao = _rms_norm(ao, a_norm_g).reshape(B, S, BRANCH_WIDTH).astype(dt)
    y_a = ao * jax.nn.silu(a_z)

    glu_a, glu_b = jnp.split(b_glu, 2, axis=-1)
    bu = glu_a * jax.nn.sigmoid(glu_b)
    bu = _causal_dwconv(bu, b_dw_w, b_dw_b)
    bu = jax.nn.silu(_layer_norm(bu, b_ln_g, b_ln_b))
    y_b = bu * jax.nn.silu(b_z)

    cq = c_q.reshape(B, S, SWA_Q_HEADS, SWA_HEAD_DIM)
    ck, cv = jnp.split(c_kv, 2, axis=-1)
    ck = ck.reshape(B, S, SWA_KV_HEADS, SWA_HEAD_DIM)
    cv = cv.reshape(B, S, SWA_KV_HEADS, SWA_HEAD_DIM)
    co = _swa_sinks(cq, ck, cv, c_sinks, band_bias).reshape(B, S, BRANCH_WIDTH)
    y_c = co * jax.nn.silu(c_z)

    dxc = _causal_dwconv(d_x, d_conv_w, d_conv_b)
    do = _rglru(dxc, d_w_a, d_b_a, d_w_x, d_b_x, d_lambda)
    y_d = do * jax.nn.silu(d_z)

    mem_kv = _rms_norm(mem, g_mem) @ w_mem_kv
    mk, mv = jnp.split(mem_kv, 2, axis=-1)
    mk = mk.reshape(B, MEM_LEN, XATTN_HEADS, XATTN_HEAD_DIM)
    mv = mv.reshape(B, MEM_LEN, XATTN_HEADS, XATTN_HEAD_DIM)
    eq = e_q.reshape(B, S, XATTN_HEADS, XATTN_HEAD_DIM)
    es = jnp.einsum('bshd,bmhd->bhsm', eq, mk).astype(jnp.float32) * (XATTN_HEAD_DIM ** -0.5)
    ep = jax.nn.softmax(es, axis=-1).astype(dt)
    eo = jnp.einsum('bhsm,bmhd->bshd', ep, mv).reshape(B, S, BRANCH_WIDTH)
    y_e = eo * jax.nn.silu(e_z)

    ys = jnp.stack([y_a, y_b.astype(dt), y_c.astype(dt), y_d.astype(dt), y_e.astype(dt)], axis=2)
    branch_out = jnp.einsum('bsnw,nwd->bsnd', ys, w_br)
    gate = jax.nn.sigmoid(gates.reshape(B, S, N_BRANCHES, D_MODEL))
    merged = jnp.sum(gate * branch_out, axis=2)
    return merged @ w_out


def setup_inputs(seed: int = 0) -> dict:
    key = jax.random.key(seed)
    ks = jax.random.split(key, 32)
    f32 = jnp.float32
    L = DEPTH
    W = BRANCH_WIDTH

    def nrm(k, shape, scale):
        return scale * jax.random.normal(k, shape, f32)

    def gain(k, shape):
        return 1.0 + 0.05 * jax.random.normal(k, shape, f32)

    x = nrm(ks[0], (BATCH, SEQ, D_MODEL), 1.0)
    mem = nrm(ks[1], (BATCH, MEM_LEN, D_MODEL), 1.0)
    g_pre = gain(ks[2], (L, D_MODEL))
    g_post = gain(ks[3], (L, D_MODEL))
    w_in = nrm(ks[4], (L, D_MODEL, IN_COLS), D_MODEL ** -0.5)
    b_in = nrm(ks[5], (L, IN_COLS), 0.02)
    a_conv_w = nrm(ks[6], (L, GDN_CONV, 3 * W), GDN_CONV ** -0.5)
    a_log = jnp.log(jax.random.uniform(ks[7], (L, GDN_HEADS), f32, 1.0, 16.0))
    dt0 = jnp.exp(jax.random.uniform(ks[8], (L, GDN_HEADS), f32, math.log(1e-3), math.log(0.1)))
    a_dt_bias = dt0 + jnp.log(-jnp.expm1(-dt0))
    a_norm_g = gain(ks[9], (L, GDN_HEAD_DIM))
    b_dw_w = nrm(ks[10], (L, CONF_WIDTH, W), CONF_WIDTH ** -0.5)
    b_dw_b = nrm(ks[11], (L, W), 0.02)
    b_ln_g = gain(ks[12], (L, W))
    b_ln_b = nrm(ks[13], (L, W), 0.02)
    c_sinks = nrm(ks[14], (L, SWA_Q_HEADS), 0.5)
    rel_bias = nrm(ks[15], (REL_BUCKETS, SWA_Q_HEADS), 0.5)
    d_conv_w = nrm(ks[16], (L, LRU_CONV, W), LRU_CONV ** -0.5)
    d_conv_b = nrm(ks[17], (L, W), 0.02)
    d_w_a = nrm(ks[18], (L, LRU_BLOCKS, LRU_BLOCK_W, LRU_BLOCK_W), LRU_BLOCK_W ** -0.5)
    d_b_a = nrm(ks[19], (L, W), 0.02)
    d_w_x = nrm(ks[20], (L, LRU_BLOCKS, LRU_BLOCK_W, LRU_BLOCK_W), LRU_BLOCK_W ** -0.5)
    d_b_x = nrm(ks[21], (L, W), 0.02)
    u = jax.random.uniform(ks[22], (L, W), f32, 0.9, 0.999)
    a0 = u ** (1.0 / LRU_C)
    d_lambda = jnp.log(a0) - jnp.log1p(-a0)
    g_mem = gain(ks[23], (L, D_MODEL))
    w_mem_kv = nrm(ks[24], (L, D_MODEL, 2 * W), D_MODEL ** -0.5)
    w_br = nrm(ks[25], (L, N_BRANCHES, W, D_MODEL), W ** -0.5)
    w_out = nrm(ks[26], (L, D_MODEL, D_MODEL), D_MODEL ** -0.5)
    return {"x": x, "mem": mem, "g_pre": g_pre, "g_post": g_post, "w_in": w_in, "b_in": b_in,
            "a_conv_w": a_conv_w, "a_log": a_log, "a_dt_bias": a_dt_bias, "a_norm_g": a_norm_g,
            "b_dw_w": b_dw_w, "b_dw_b": b_dw_b, "b_ln_g": b_ln_g, "b_ln_b": b_ln_b,
            "c_sinks": c_sinks, "rel_bias": rel_bias,
            "d_conv_w": d_conv_w, "d_conv_b": d_conv_b, "d_w_a": d_w_a, "d_b_a": d_b_a,
            "d_w_x": d_w_x, "d_b_x": d_b_x, "d_lambda": d_lambda,
            "g_mem": g_mem, "w_mem_kv": w_mem_kv, "w_br": w_br, "w_out": w_out}


def reference(x, mem, g_pre, g_post, w_in, b_in, a_conv_w, a_log, a_dt_bias, a_norm_g,
              b_dw_w, b_dw_b, b_ln_g, b_ln_b, c_sinks, rel_bias,
              d_conv_w, d_conv_b, d_w_a, d_b_a, d_w_x, d_b_x, d_lambda,
              g_mem, w_mem_kv, w_br, w_out):
    i = np.arange(WINDOW)[:, None]
    j = np.arange(2 * WINDOW)[None, :]
    bucket = _t5_bucket(i + WINDOW - j)
    band_bias = jnp.transpose(rel_bias[bucket], (2, 0, 1))
    for l in range(DEPTH):
        h = _rms_norm(x, g_pre[l])
        y = _hybrid_mixer(h, mem, w_in[l], b_in[l], a_conv_w[l], a_log[l], a_dt_bias[l], a_norm_g[l],
                          b_dw_w[l], b_dw_b[l], b_ln_g[l], b_ln_b[l], c_sinks[l], band_bias,
                          d_conv_w[l], d_conv_b[l], d_w_a[l], d_b_a[l], d_w_x[l], d_b_x[l], d_lambda[l],
                          g_mem[l], w_mem_kv[l], w_br[l], w_out[l])
        x = x + _rms_norm(y, g_post[l])
    return x
```

```python
import os
import numpy as np
from contextlib import ExitStack
import concourse.bass as bass
import concourse.mybir as mybir
from concourse.bass_utils import run_bass_kernel_spmd

F32 = mybir.dt.float32
F32R = mybir.dt.float32r
BF16 = mybir.dt.bfloat16
AF = mybir.ActivationFunctionType
ALU = mybir.AluOpType
AX = mybir.AxisListType

D_MODEL = 1024
SEQ = 4096
DEPTH = 2
MEM_LEN = 256
IN_COLS = 12040
NCOLP = 12288
EPS = 1e-6
NEG = -30000.0

ENGS = ("pe", "act", "dve", "pool", "sp")
EPOCH = 12000
WAW_SELF = bool(int(os.environ.get('WAW_SELF', 0)))
W_CONVA = int(os.environ.get('W_CONVA', 2))
W_GDN = int(os.environ.get('W_GDN', 2))
W_ATT = int(os.environ.get('W_ATT', 2))
W_MRG = int(os.environ.get('W_MRG', 3))
W_B = int(os.environ.get('W_B', 2))
NDMASEM = 24


class Reg:
    __slots__ = ("name", "w", "r")

    def __init__(self, name):
        self.name = name
        self.w = None
        self.r = []


class Prog:
    def __init__(self, nc, stack):
        self.nc = nc
        self.stack = stack
        self.q = {e: [] for e in ENGS}
        self.cnt = {e: 0 for e in ENGS}
        self.sems = {e: [] for e in ENGS}
        self.seen = {e: {} for e in ENGS}
        self.dma_sems = [stack.enter_context(nc.semaphore(f"dq{i}")) for i in range(NDMASEM)]
        self.dma_n = 0
        self.dma_tok = {}
        self.cc_sems = [stack.enter_context(nc.semaphore(f"cc{i}")) for i in range(2)]
        self.cc_n = 0
        self.cc_tok = {}
        self.eng_time = {e: 0.0 for e in ENGS}
        self.step_fin = 0.0

    def _sem(self, e, epoch):
        while len(self.sems[e]) <= epoch:
            self.sems[e].append(self.stack.enter_context(
                self.nc.semaphore(f"s_{e}_{len(self.sems[e])}")))
        return self.sems[e][epoch]

    def _need(self, e, tok, waits):
        key, val = tok[0], tok[1]
        if self.seen[e].get(key, 0) >= val:
            return
        self.seen[e][key] = val
        waits[key] = max(waits.get(key, 0), val)

    def _wl(self, waits):
        wl = []
        for key, val in waits.items():
            if key[0] == "d":
                wl.append((self.dma_sems[key[1]], val))
            elif key[0] == "c":
                wl.append((self.cc_sems[key[1]], val))
            else:
                wl.append((self._sem(key[0], key[1]), val))
        return wl

    def emit(self, e, fn, reads=(), writes=(), dma=False, selfdep=True, cost=300.0, cc=False):
        waits = {}
        ready = 0.0
        for r in reads:
            t = r.w
            if t is not None:
                ready = max(ready, t[4])
                if (selfdep or t[2] != e or dma or t[3]):
                    self._need(e, t, waits)
        for w in writes:
            t = w.w
            if t is not None:
                ready = max(ready, t[4])
                if (t[2] != e or dma or t[3] or (selfdep and WAW_SELF)):
                    self._need(e, t, waits)
            for t in w.r:
                ready = max(ready, t[4])
                if t[2] != e or dma or t[3]:
                    self._need(e, t, waits)
        cost = cost * float(os.environ.get("CS_" + ("dma" if dma else e), 1.0))
        if dma:
            t0 = max(ready + 100.0, self.eng_time[e])
            self.eng_time[e] = t0 + 60.0
            fin = t0 + cost
        else:
            t0 = max(ready + float(os.environ.get('CS_lat', 150.0)), self.eng_time[e]) if ready > self.eng_time[e] - 1e-9 and waits else max(ready, self.eng_time[e])
            fin = t0 + cost
            self.eng_time[e] = fin
        self.step_fin = max(self.step_fin, fin)
        self.busy = getattr(self, "busy", {})
        self.busy[e] = self.busy.get(e, 0.0) + cost
        self.stall = getattr(self, "stall", {})
        self.stall[e] = self.stall.get(e, 0.0) + max(0.0, t0 - max(self.eng_time[e] - (cost if not dma else 60.0), 0.0))
        if cc:
            k = self.cc_n
            self.cc_n += 1
            si = k % 2
            val = k // 2 + 1
            if k >= 2:
                self._need(e, self.cc_tok[k - 2], waits)
            tok = (("c", si), val, e, True, fin)
            self.cc_tok[k] = tok
            inc = (self.cc_sems[si], 1)
        elif dma:
            k = self.dma_n
            self.dma_n += 1
            si = k % NDMASEM
            val = 16 * (k // NDMASEM + 1)
            if k >= NDMASEM:
                self._need(e, self.dma_tok[k - NDMASEM], waits)
            tok = (("d", si), val, e, True, fin)
            self.dma_tok[k] = tok
            inc = (self.dma_sems[si], 16)
        else:
            self.cnt[e] += 1
            c = self.cnt[e]
            epoch, val = (c - 1) // EPOCH, (c - 1) % EPOCH + 1
            tok = ((e, epoch), val, e, False, fin)
            inc = (self._sem(e, epoch), 1)
        self.q[e].append((self._wl(waits), fn, inc))
        for r in reads:
            if not r.r or r.r[-1] is not tok:
                r.r.append(tok)
        for w in writes:
            w.w = tok
            w.r = []
        return tok

    def finish_wait(self, e, toks):
        waits = {}
        for t in toks:
            self._need(e, t, waits)
        self.q[e].append((self._wl(waits), None, None))

    def run(self):
        nc = self.nc
        with nc.Block() as block:
            def play(eng, items):
                for wl, fn, inc in items:
                    for s, v in wl:
                        eng.wait_ge(s, v)
                    if fn is not None:
                        fn(eng).then_inc(inc[0], inc[1])

            @block.tensor
            def _(eng):
                play(eng, self.q["pe"])

            @block.scalar
            def _(eng):
                play(eng, self.q["act"])

            @block.vector
            def _(eng):
                play(eng, self.q["dve"])

            @block.gpsimd
            def _(eng):
                play(eng, self.q["pool"])

            @block.sync
            def _(eng):
                play(eng, self.q["sp"])


class Slots:
    def __init__(self, items):
        self.free = list(items)
        self.n = len(items)

    def get(self):
        if not self.free:
            raise RuntimeError("slot pool exhausted")
        return self.free.pop(0)

    def put(self, it):
        self.free.append(it)


def _win_perm():
    p = list(range(0, 2048))
    p += list(range(2056, 3592))
    cq0 = 3592
    for c in range(4):
        p += list(range(cq0 + c * 64, cq0 + (c + 1) * 64))
        p += list(range(cq0 + (4 + c) * 64, cq0 + (5 + c) * 64))
    p += list(range(4104, 4360)) + list(range(2048, 2056)) + [-1] * 248
    p += list(range(4360, 12040))
    p = np.array(p, dtype=np.int64)
    assert p.size == NCOLP
    return p


def _t5_bucket(dist):
    n = np.maximum(dist, 0)
    max_exact = 16
    large = max_exact + (np.log(np.maximum(n, 1) / max_exact) / np.log(128 / max_exact) * (32 - max_exact)).astype(np.int32)
    large = np.minimum(large, 31)
    return np.where(n < max_exact, n, large).astype(np.int32)


def _param_layout(L):
    lay = [("ident", 128), ("triu", 128), ("masku", 128), ("masklneg", 128), ("bd01", 128), ("off01", 128),
           ("ones", 128), ("maskc", 256), ("bandb", 2048), ("flags", 6)]
    for l in range(L):
        lay += [(f"gpre{l}", 8), (f"gmem{l}", 8), (f"bfm{l}", 96), (f"bba{l}", 8), (f"bv{l}", 128),
                (f"aconv{l}", 48), (f"alog{l}", 4), (f"adt{l}", 4), (f"anorm{l}", 1),
                (f"bdw{l}", 124), (f"bdwb{l}", 4), (f"blng{l}", 4), (f"blnb{l}", 4),
                (f"sinks{l}", 8), (f"dconv{l}", 16), (f"dconvb{l}", 4), (f"dba{l}", 4), (f"dbx{l}", 4),
                (f"dlam{l}", 4)]
    off = {}
    o = 0
    for n, w in lay:
        off[n] = (o, w)
        o += w
    return off, o


def _host_params(inp, L, stage=0):
    off, tot = _param_layout(L)
    prm = np.zeros((128, tot), np.float32)
    fA, fB = (1.0, 0.0) if stage == 0 else (0.0, 1.0)
    o_, _w = off["flags"]
    big = 3.0e38 if stage == 0 else 0.0
    prm[:, o_:o_ + 6] = np.array([fA, fB, NEG if stage == 1 else 0.0, fA, -big, big], np.float32)[None, :]

    def put(name, a):
        o, w = off[name]
        prm[:, o:o + w] = np.asarray(a, np.float32).reshape(128, w)

    def fm(v, c):
        return np.asarray(v).reshape(c, 128).T

    def row(v):
        v = np.asarray(v).reshape(1, -1)
        return np.broadcast_to(v, (128, v.shape[1]))

    i = np.arange(128)
    put("ident", np.eye(128))
    put("triu", (i[:, None] <= i[None, :]))
    put("masku", np.where(i[None, :] >= i[:, None], 0.0, NEG))
    put("masklneg", np.where(i[:, None] > i[None, :], 0.0, NEG))
    blk = (i[:, None] // 64) == (i[None, :] // 64)
    put("bd01", blk)
    put("off01", ~blk)
    put("ones", np.ones((128, 128)))
    ii = np.arange(128)[:, None]
    jj = np.arange(256)[None, :]
    dist = ii + 128 - jj
    put("maskc", np.where((dist >= 0) & (dist < 128), 0.0, NEG))
    bucket = _t5_bucket(dist)
    bb = inp["rel_bias"][bucket]
    put("bandb", np.transpose(bb, (0, 2, 1)))
    perm = _win_perm()
    for l in range(L):
        put(f"gpre{l}", fm(inp["g_pre"][l], 8))
        put(f"gmem{l}", fm(inp["g_mem"][l], 8))
        b = inp["b_in"][l]
        bp = np.where(perm >= 0, b[np.maximum(perm, 0)], 0.0)
        put(f"bfm{l}", fm(bp, 96))
        put(f"bba{l}", row(b[2048:2056]))
        put(f"bv{l}", row(b[4232:4360]))
        put(f"aconv{l}", np.transpose(inp["a_conv_w"][l].reshape(4, 12, 128), (2, 1, 0)))
        put(f"alog{l}", row(inp["a_log"][l]))
        put(f"adt{l}", row(inp["a_dt_bias"][l]))
        put(f"anorm{l}", inp["a_norm_g"][l].reshape(128, 1))
        put(f"bdw{l}", np.transpose(inp["b_dw_w"][l].reshape(31, 4, 128), (2, 1, 0)))
        put(f"bdwb{l}", fm(inp["b_dw_b"][l], 4))
        put(f"blng{l}", fm(inp["b_ln_g"][l], 4))
        put(f"blnb{l}", fm(inp["b_ln_b"][l], 4))
        put(f"sinks{l}", row(inp["c_sinks"][l]))
        put(f"dconv{l}", np.transpose(inp["d_conv_w"][l].reshape(4, 4, 128), (2, 1, 0)))
        put(f"dconvb{l}", fm(inp["d_conv_b"][l], 4))
        put(f"dba{l}", fm(inp["d_b_a"][l], 4))
        put(f"dbx{l}", fm(inp["d_b_x"][l], 4))
        put(f"dlam{l}", fm(inp["d_lambda"][l], 4))
    w_in = inp["w_in"][:L]
    winp = np.zeros((L, D_MODEL, NCOLP), np.float32)
    valid = perm >= 0
    winp[:, :, valid] = w_in[:, :, perm[valid]]
    bdw = np.zeros((L, 2, 4, 128, 128), np.float32)
    for l in range(L):
        for t, nm in enumerate(("d_w_a", "d_w_x")):
            w = inp[nm][l]
            for c in range(4):
                bdw[l, t, c, 0:64, 0:64] = w[2 * c]
                bdw[l, t, c, 64:128, 64:128] = w[2 * c + 1]
    g_post = np.ascontiguousarray(inp["g_post"][:L], np.float32)
    return prm, winp, bdw, g_post


def build(T, L, taps=(), pipe=False):
    NT = T // 512
    nc = bass.Bass("TRN2", target_bir_lowering=False)
    poff, ptot = _param_layout(L)
    x_d = nc.dram_tensor("x", [T, D_MODEL], F32, kind="ExternalInput").ap()
    mem_d = nc.dram_tensor("mem", [MEM_LEN, D_MODEL], F32, kind="ExternalInput").ap()
    win_d = nc.dram_tensor("winp", [L, D_MODEL, NCOLP], F32, kind="ExternalInput").ap()
    wbr_d = nc.dram_tensor("wbr", [L, 5, 512, D_MODEL], F32, kind="ExternalInput").ap()
    wout_d = nc.dram_tensor("wout", [L, D_MODEL, D_MODEL], F32, kind="ExternalInput").ap()
    wmem_d = nc.dram_tensor("wmem", [L, D_MODEL, D_MODEL], F32, kind="ExternalInput").ap()
    prm_d = nc.dram_tensor("prm", [128, ptot], F32, kind="ExternalInput").ap()
    bdw_d = nc.dram_tensor("bdw", [L, 2, 4, 128, 128], F32, kind="ExternalInput").ap()
    gpost_d = nc.dram_tensor("gpost", [L, D_MODEL], F32, kind="ExternalInput").ap()
    out_d = nc.dram_tensor("out", [T, D_MODEL], F32, kind="ExternalOutput").ap()
    NSCR = 26 * L
    scr_d = nc.dram_tensor("wscr", [NSCR, 128, 8 * 512], BF16, kind="Internal").ap()
    scrb_d = nc.dram_tensor("wscrb", [L * 10, 128, 4 * 512], BF16, kind="Internal").ap()
    tap_d = {}
    for name, shape in taps:
        tap_d[name] = nc.dram_tensor("tap_" + name, list(shape), F32, kind="ExternalOutput").ap()

    with ExitStack() as st:
        P = Prog(nc, st)

        def sb(name, shape, dt):
            return st.enter_context(nc.sbuf_tensor("sb_" + name, list(shape), dt))

        def _fsize(ap):
            n = 1
            for d in ap.shape[1:]:
                n *= d
            return n

        def op(eng, meth, R=(), W=(), **kw):
            o = kw.get("out", kw.get("ap"))
            n = _fsize(o) if o is not None else 128
            if eng == "pe":
                if meth == "transpose":
                    c = 64.0 + 128 * 0.5
                else:
                    nn = _fsize(kw["rhs"])
                    f = 4.0 if kw["rhs"].dtype in (F32, F32R) else 1.0
                    c = 40.0 + nn * 0.52 * f
            elif eng == "act":
                c = 220.0 + n * 0.9
            elif eng == "dve":
                c = 120.0 + n * 0.75
            else:
                c = 250.0 + n * 2.0
            return P.emit(eng, lambda e: getattr(e, meth)(**kw), reads=R, writes=W, selfdep=(eng != "pe"), cost=c)

        def mm(out, lhsT, rhs, start, stop, R, W):
            return op("pe", "matmul", R, W, out=out, lhsT=lhsT, rhs=rhs, start=start, stop=stop)

        def tr(out, in_, ident, R, W):
            return op("pe", "transpose", R, W, out=out, in_=in_, identity=ident)

        def act(out, in_, func, R, W, **kw):
            return op("act", "activation", R, W, out=out, in_=in_, func=func, **kw)

        def dma(eng, out, in_, R=(), W=()):
            nb = 128 * _fsize(out) * 4
            return P.emit(eng, lambda e: e.dma_start(out=out, in_=in_), reads=R, writes=W, dma=True, cost=2500.0 + nb / 150.0)

        def ts(eng, out, in0, s1, s2, op0, op1, R, W):
            if s2 is None:
                return op(eng, "tensor_scalar", R, W, out=out, in0=in0, scalar1=s1, scalar2=None, op0=op0)
            return op(eng, "tensor_scalar", R, W, out=out, in0=in0, scalar1=s1, scalar2=s2, op0=op0, op1=op1)

        def tt(eng, out, in0, in1, o, R, W):
            return op(eng, "tensor_tensor", R, W, out=out, in0=in0, in1=in1, op=o)

        def stt(out, in0, scalar, in1, op0, op1, R, W):
            return op("dve", "scalar_tensor_tensor", R, W, out=out, in0=in0, scalar=scalar, in1=in1, op0=op0, op1=op1)

        prm = sb("prm", [128, ptot], F32)
        Rprm = Reg("prm")

        def pp(name, lo=0, hi=None):
            o, w = poff[name]
            hi = w if hi is None else hi
            return prm[:, o + lo:o + hi]

        cst = sb("cst", [128, 6, 128], BF16)
        cstr = sb("cstr", [128, 2, 128], F32)
        Rcst = Reg("cst")
        xt = sb("xt", [128, 4, D_MODEL], F32)
        Rxt = Reg("xt")
        hT = sb("hT", [128, 8, 512], BF16)
        RhT = Reg("hT")
        yT = sb("yT", [128, 5, 4, 512], BF16)
        RyT = [Reg(f"yT{n}") for n in range(5)]
        mgb = sb("mgb", [128, 8, 512], BF16)
        Rmgb = Reg("mgb")
        sm = sb("sm", [128, 256], F32)
        Rsm = {}

        def smr(name):
            if name not in Rsm:
                Rsm[name] = Reg("sm_" + name)
            return Rsm[name]

        NW = 5
        WB = Slots([(sb(f"wb{i}", [128, 8, 512], BF16), Reg(f"wb{i}")) for i in range(NW)])
        PS = Slots([(st.enter_context(nc.psum_tensor(f"ps{i}", [128, 512], F32)), Reg(f"ps{i}")) for i in range(8)])
        FS = Slots([(sb(f"f{i}", [128, 512], F32), Reg(f"f{i}")) for i in range(7)])
        HS = Slots([(sb(f"h{i}", [128, 544], BF16), Reg(f"h{i}")) for i in range(10)])
        GS = Slots([(sb(f"g{i}", [128, 4, 512], BF16), Reg(f"g{i}")) for i in range(8)])
        FR = Slots([(sb(f"fr{i}", [128, 512], F32), Reg(f"fr{i}")) for i in range(5)])
        S16 = Slots([(sb(f"s16_{i}", [128, 128], BF16), Reg(f"s16_{i}")) for i in range(4)])
        DG = Slots([(sb(f"dg{i}", [128, 4, 128], BF16), Reg(f"dg{i}")) for i in range(2)])

        haloA = sb("haloA", [128, L, 12, 4], BF16)
        haloB = sb("haloB", [128, L, 4, 32], BF16)
        haloD = sb("haloD", [128, L, 4, 4], BF16)
        Rhalo = [Reg(f"halo{l}") for l in range(L)]
        Sst = sb("Sst", [128, L, 4, 128], F32)
        Sbf = sb("Sbf", [128, L, 4, 128], BF16)
        RS = [[Reg(f"S{l}_{h}") for h in range(4)] for l in range(L)]
        hst = sb("hst", [128, L, 4], F32)
        Rhst = [Reg(f"hst{l}") for l in range(L)]
        kTc = sb("kTc", [128, L, 640], BF16)
        vtc = sb("vtc", [128, L, 5, 128], BF16)
        Rkv = [Reg(f"kv{l}") for l in range(L)]
        mkT = sb("mkT", [128, L, 4, 256], BF16)
        mvt = sb("mvt", [128, L, 2, 512], BF16)
        Rmem = [Reg(f"mem{l}") for l in range(L)]
        bdws = sb("bdws", [128, L, 2, 4, 128], BF16)
        Rbdw = Reg("bdw")
        lruc = sb("lruc", [128, L, 8], F32)
        nA = sb("nA", [128, L, 4], F32)
        Rlc = Reg("lruc")

        ident_f = pp("ident")
        ident_b = cst[:, 0, :]
        ones_b = cst[:, 1, :]
        ones_r = cstr[:, 0, :].bitcast(F32R)

        dma("sp", prm[:], prm_d, W=[Rprm])
        op("dve", "tensor_copy", [Rprm], [Rcst], out=cst[:, 0, :], in_=pp("ident"))
        op("dve", "tensor_copy", [Rprm], [Rcst], out=cst[:, 1, :], in_=pp("ones"))
        op("dve", "tensor_copy", [Rprm], [Rcst], out=cstr[:, 0, :].bitcast(F32R), in_=pp("ones"))
        bo_, bw_ = poff["bandb"]
        bandv = prm[:, bo_:bo_ + bw_].rearrange("p (h j) -> p h j", h=8)
        tt("dve", bandv, bandv, pp("maskc").unsqueeze(1).broadcast_to([128, 8, 256]), ALU.add, [Rprm], [Rprm])
        for l in range(L):
            dma("pool", bdws[:, l].rearrange("p t c m -> p (t c) m"),
                bdw_d[l].rearrange("t c p m -> p (t c) m"), W=[Rbdw])
            act(lruc[:, l, 0:4], pp(f"dlam{l}"), AF.Exp, [Rprm], [Rlc], scale=-1.0)
            act(lruc[:, l, 0:4], lruc[:, l, 0:4], AF.Ln, [Rlc], [Rlc], bias=1.0)
            ts("dve", lruc[:, l, 4:8], lruc[:, l, 0:4], -16.0, None, ALU.mult, None, [Rlc], [Rlc])
            ts("dve", lruc[:, l, 0:4], lruc[:, l, 0:4], -8.0, None, ALU.mult, None, [Rlc], [Rlc])
            act(nA[:, l, :], pp(f"alog{l}"), AF.Exp, [Rprm], [Rlc])
            ts("dve", nA[:, l, :], nA[:, l, :], -1.0, None, ALU.mult, None, [Rlc], [Rlc])
            op("dve", "memset", [], [Rhalo[l]], ap=haloA[:, l], constant=0.0)
            op("dve", "memset", [], [Rhalo[l]], ap=haloB[:, l], constant=0.0)
            op("dve", "memset", [], [Rhalo[l]], ap=haloD[:, l], constant=0.0)
            for h in range(4):
                op("dve", "memset", [], [RS[l][h]], ap=Sst[:, l, h, :], constant=0.0)
                op("dve", "memset", [], [RS[l][h]], ap=Sbf[:, l, h, :], constant=0.0)
            op("dve", "memset", [], [Rhst[l]], ap=hst[:, l, :], constant=0.0)
            op("dve", "memset", [], [Rkv[l]], ap=kTc[:, l, :], constant=0.0)
            op("dve", "memset", [], [Rkv[l]], ap=vtc[:, l], constant=0.0)

        def gget(pool):
            while not pool.free:
                P.blocked = True
                yield
            P.progress = True
            return pool.get()

        def run(g):
            idle = 0
            P.progress = False
            for _ in g:
                if P.progress:
                    idle = 0
                else:
                    idle += 1
                    if idle > 10000:
                        raise RuntimeError("build-time scheduling deadlock")
                P.progress = False

        def _step_best(active, rdy, bias=None):
            g = min(active, key=lambda x: rdy[id(x)] - (bias.get(id(x), 0.0) if bias else 0.0))
            save = P.step_fin
            P.step_fin = 0.0
            P.blocked = False
            try:
                next(g)
                if P.step_fin > 0.0:
                    rdy[id(g)] = P.step_fin
                else:
                    others = [rdy[id(x)] for x in active if x is not g]
                    rdy[id(g)] = (min(others) if others else rdy[id(g)]) + 50.0
            except StopIteration:
                active.remove(g)
                P.progress = True
            P.step_fin = max(save, P.step_fin)

        def par(*gens, prio=None):
            active = list(gens)
            rdy = {id(g): 0.0 for g in active}
            bias = {id(g): (prio[i] if prio else 0.0) for i, g in enumerate(active)}
            while active:
                _step_best(active, rdy, bias)
                yield

        def pipeline(gens, width):
            it = iter(gens)
            active = []
            rdy = {}
            done = False
            while True:
                while not done and len(active) < width:
                    try:
                        active.append(next(it))
                        P.progress = True
                    except StopIteration:
                        done = True
                if not active:
                    return
                for g in active:
                    rdy.setdefault(id(g), 0.0)
                _step_best(active, rdy)
                yield

        Rscr = {}
        scr_seen = set()

        class WStream:
            def __init__(self, srcs):
                self.srcs = srcs
                self.i = 0
                self.pend = {}

            def _issue(self, i, slot):
                wt, Rw = slot
                src, ncols, key = self.srcs[i]
                if key is None or ncols != 512:
                    dma("pool", wt[:, :, 0:ncols], src.rearrange("(kc p) n -> p kc n", p=128), W=[Rw])
                elif key not in scr_seen:
                    scr_seen.add(key)
                    Rscr[key] = Reg(f"scr{key}")
                    dma("pool", wt[:, :, 0:ncols], src.rearrange("(kc p) n -> p kc n", p=128), W=[Rw])
                    dma("sp", scr_d[key], wt[:].rearrange("p a b -> p (a b)"), R=[Rw], W=[Rscr[key]])
                else:
                    dma("sp", wt[:].rearrange("p a b -> p (a b)"), scr_d[key], R=[Rscr[key]], W=[Rw])
                self.pend[i] = slot

            def prefetch(self):
                if self.i < len(self.srcs) and self.i not in self.pend and WB.free:
                    self._issue(self.i, WB.get())

            def take(self):
                i = self.i
                self.i += 1
                if i not in self.pend:
                    slot = yield from gget(WB)
                    self._issue(i, slot)
                cur = self.pend.pop(i)
                self.prefetch()
                return cur

        def winblk(l, blk):
            return (win_d[l][:, blk * 512:(blk + 1) * 512], 512, l * 26 + blk)

        def norm_T(src, Rsrc, nsub, gname, dst, Rdst, inv_ap, Rinv):
            for s in range(nsub):
                jt, Rj = yield from gget(FS)
                act(jt[:].bitcast(BF16), src[:, s, :], AF.Square, [Rsrc], [Rj, Rinv], accum_out=inv_ap[:, s:s + 1])
                FS.put((jt, Rj))
            rs = sm[:, 8:8 + nsub]
            Rrs = smr("rs")
            act(inv_ap, inv_ap, AF.Ln, [Rinv], [Rinv], scale=1.0 / D_MODEL, bias=EPS)
            act(rs, inv_ap, AF.Exp, [Rinv], [Rrs], scale=-0.5)
            act(inv_ap, inv_ap, AF.Exp, [Rinv], [Rinv], scale=0.5)
            for s in range(nsub):
                ts("dve", src[:, s, :], src[:, s, :], rs[:, s:s + 1], None, ALU.mult, None, [Rsrc, Rrs], [Rsrc])
            for c in range(8):
                b, Rb = yield from gget(PS)
                for s in range(nsub):
                    tr(b[:, s * 128:(s + 1) * 128], src[:, s, c * 128:(c + 1) * 128], ident_f, [Rsrc, Rprm], [Rb])
                act(dst[:, c, 0:nsub * 128], b[:, 0:nsub * 128], AF.Copy, [Rb, Rprm], [Rdst], scale=pp(gname, c, c + 1))
                PS.put((b, Rb))
                yield

        def proj(wt, Rw, cofs, M, evac, ntok=512, rhs=None, Rrhs=None):
            rhs = hT if rhs is None else rhs
            Rrhs = RhT if Rrhs is None else Rrhs
            b, Rb = yield from gget(PS)
            for kc in range(8):
                mm(b[0:M, 0:ntok], wt[:, kc, cofs:cofs + M], rhs[:, kc, 0:ntok], kc == 0, kc == 7, [Rw, Rrhs], [Rb])
            yield
            evac(b, Rb)
            PS.put((b, Rb))

        mem_ready = {}
        if pipe:
            memt = sb("memt", [128, 2, D_MODEL], F32)
            memT = sb("memT", [128, 8, 256], BF16)
            Rmemt, RmemT = Reg("memt"), Reg("memT")
            minv, Rminv = sm[:, 200:202], smr("minv")
        else:
            memt, Rmemt, memT, RmemT = xt, Rxt, hT, RhT
            minv, Rminv = sm[:, 0:2], smr("inv")

        def mem_phase():
            for l in range(L):
                dma("sp", memt[:, 0:2, :], mem_d.rearrange("(s p) d -> p s d", p=128), W=[Rmemt])
                yield from norm_T(memt, Rmemt, 2, f"gmem{l}", memT, RmemT, minv, Rminv)
                ws = WStream([(wmem_d[l][:, 0:512], 512, None), (wmem_d[l][:, 512:1024], 512, None)])
                wt, Rw = yield from ws.take()
                for h in range(4):
                    def ev(b, Rb, h=h, l=l):
                        act(mkT[:, l, h, :], b[:, 0:256], AF.Copy, [Rb], [Rmem[l]])
                    yield from proj(wt, Rw, h * 128, 128, ev, ntok=256, rhs=memT, Rrhs=RmemT)
                WB.put((wt, Rw))
                wt, Rw = yield from ws.take()
                for s in range(2):
                    b, Rb = yield from gget(PS)
                    for kc in range(8):
                        mm(b[:, :], memT[:, kc, s * 128:(s + 1) * 128], wt[:, kc, :], kc == 0, kc == 7, [Rw, RmemT], [Rb])
                    yield
                    act(mvt[:, l, s, :], b[:, :], AF.Copy, [Rb], [Rmem[l]])
                    PS.put((b, Rb))
                WB.put((wt, Rw))
                mem_ready[l] = True
        if not pipe:
            run(mem_phase())

        def convA_item(l, grp, c, wt, Rw, qnT, RqnT, knT, RknT, vtok, Rvtok):
            ch = grp * 4 + c
            bfm = lambda q: pp(f"bfm{l}", q, q + 1)
            pre, Rpre = yield from gget(HS)
            op("pool", "tensor_copy", [Rhalo[l]], [Rpre], out=pre[:, 0:3], in_=haloA[:, l, ch, 0:3])

            def ev(b, Rb):
                act(pre[:, 3:515], b[:, :], AF.Identity, [Rb, Rprm], [Rpre], bias=bfm(ch))
            yield from proj(wt, Rw, c * 128, 128, ev)
            op("pool", "tensor_copy", [Rpre], [Rhalo[l]], out=haloA[:, l, ch, 0:3], in_=pre[:, 512:515])
            dg, Rdg = yield from gget(DG)
            o_, _w = poff[f"aconv{l}"]
            for k in range(4):
                ts("pool", dg[:, k, :], ident_b, prm[:, o_ + ch * 4 + k:o_ + ch * 4 + k + 1], 0.0, ALU.mult, ALU.add,
                   [Rcst, Rprm], [Rdg])
            yield
            b, Rb = yield from gget(PS)
            for k in range(4):
                mm(b[:, :], dg[:, k, :], pre[:, k:k + 512], k == 0, k == 3, [Rdg, Rpre], [Rb])
            DG.put((dg, Rdg))
            HS.put((pre, Rpre))
            yield
            if grp == 2:
                vT, RvT = yield from gget(HS)
                act(vT[:, 0:512], b[:, :], AF.Silu, [Rb], [RvT])
                PS.put((b, Rb))
                yield
                b, Rb = yield from gget(PS)
                bb = b[:].bitcast(BF16)
                for s in range(4):
                    tr(bb[:, s * 128:(s + 1) * 128], vT[:, s * 128:(s + 1) * 128], ident_b, [RvT, Rcst], [Rb])
                HS.put((vT, RvT))
                yield
                act(vtok[:, :, c * 128:(c + 1) * 128], bb[:, 0:512].rearrange("p (s d) -> p s d", s=4), AF.Copy, [Rb], [Rvtok])
                PS.put((b, Rb))
                return
            cs, Rcs = yield from gget(FS)
            act(cs[:, :], b[:, :], AF.Silu, [Rb], [Rcs])
            PS.put((b, Rb))
            sq, Rsq = yield from gget(HS)
            act(sq[:, 0:512], cs[:, :], AF.Square, [Rcs], [Rsq])
            yield
            b, Rb = yield from gget(PS)
            mm(b[:, :], ones_b, sq[:, 0:512], True, True, [Rcst, Rsq], [Rb])
            HS.put((sq, Rsq))
            yield
            rs, Rrs = yield from gget(FS)
            act(rs[:, :], b[:, :], AF.Ln, [Rb], [Rrs], bias=EPS)
            PS.put((b, Rb))
            act(rs[:, :], rs[:, :], AF.Exp, [Rrs], [Rrs], scale=-0.5)
            dst, Rdst = (qnT, RqnT) if grp == 0 else (knT, RknT)
            stt(dst[:, c, :], cs[:, :], (128 ** -0.5) if grp == 0 else 1.0, rs[:, :], ALU.mult, ALU.mult,
                [Rcs, Rrs], [Rdst])
            FS.put((cs, Rcs))
            FS.put((rs, Rrs))
            yield

        def chain_A(l, ti):
            bfm = lambda q: pp(f"bfm{l}", q, q + 1)
            qnT, RqnT = yield from gget(GS)
            knT, RknT = yield from gget(GS)
            sza, Rsza = yield from gget(GS)
            ktok, Rktok = yield from gget(GS)
            vtok, Rvtok = yield from gget(GS)
            ws = WStream([winblk(l, 8), winblk(l, 0), winblk(l, 1), winblk(l, 2), winblk(l, 3)])
            wt, Rw = yield from ws.take()
            yield from stage_C_kv(l, ti, wt, Rw)
            bba, Rbba = yield from gget(PS)
            for s in range(4):
                for kc in range(8):
                    mm(bba[:, s * 8:(s + 1) * 8], hT[:, kc, s * 128:(s + 1) * 128], wt[:, kc, 256:264], kc == 0, kc == 7,
                       [Rw, RhT], [Rbba])
            WB.put((wt, Rw))
            kv_ready[(l, ti)] = True
            yield
            bg = sm[:, 16:48].rearrange("p (s e) -> p s e", s=4)
            Rbg = smr("bg")
            tt("dve", bg, bba[:, 0:32].rearrange("p (s e) -> p s e", s=4),
               pp(f"bba{l}").unsqueeze(1).broadcast_to([128, 4, 8]), ALU.add, [Rbba, Rprm], [Rbg])
            PS.put((bba, Rbba))
            for grp in range(3):
                wt, Rw = yield from ws.take()
                yield from pipeline([convA_item(l, grp, c, wt, Rw, qnT, RqnT, knT, RknT, vtok, Rvtok) for c in range(4)], W_CONVA)
                WB.put((wt, Rw))
            wt, Rw = yield from ws.take()
            for c in range(4):
                def ev(b, Rb, c=c):
                    act(sza[:, c, :], b[:, :], AF.Silu, [Rb, Rprm], [Rsza], bias=bfm(12 + c))
                yield from proj(wt, Rw, c * 128, 128, ev)
            WB.put((wt, Rw))
            bet = sm[:, 48:64].rearrange("p (s e) -> p s e", s=4)
            gg = sm[:, 64:80].rearrange("p (s e) -> p s e", s=4)
            act(bet, bg[:, :, 0:4], AF.Sigmoid, [Rbg], [smr("bet")])
            tt("dve", gg, bg[:, :, 4:8], pp(f"adt{l}").unsqueeze(1).broadcast_to([128, 4, 4]), ALU.add, [Rbg, Rprm], [smr("gg")])
            act(gg, gg, AF.Exp, [smr("gg")], [smr("gg")])
            act(gg, gg, AF.Ln, [smr("gg")], [smr("gg")], bias=1.0)
            tt("dve", gg, gg, nA[:, l, :].unsqueeze(1).broadcast_to([128, 4, 4]), ALU.mult, [smr("gg"), Rlc], [smr("gg")])
            for s in range(4):
                b, Rb = yield from gget(PS)
                bb = b[:].bitcast(BF16)
                for h in range(4):
                    tr(bb[:, h * 128:(h + 1) * 128], knT[:, h, s * 128:(s + 1) * 128], ident_b, [RknT, Rcst], [Rb])
                yield
                act(ktok[:, s, :], bb[:, 0:512], AF.Copy, [Rb], [Rktok])
                PS.put((b, Rb))
            marks.append(("A_prologue_end", ti, l, P.cnt["pe"]))
            for s in range(4):
                yield from gdn_chunk(l, ti, s, qnT, RqnT, knT, RknT, ktok, Rktok, vtok, Rvtok, sza, Rsza, bet, gg)
                marks.append((f"A_chunk{s}_end", ti, l, P.cnt["pe"]))
            for it in ((qnT, RqnT), (knT, RknT), (ktok, Rktok), (vtok, Rvtok), (sza, Rsza)):
                GS.put(it)

        def gdn_chunk(l, ti, s, qnT, RqnT, knT, RknT, ktok, Rktok, vtok, Rvtok, sza, Rsza, bet, gg):
            tsl = slice(s * 128, (s + 1) * 128)
            Rbet, Rgg = smr("bet"), smr("gg")
            gs = sm[:, 80:88]
            Rgs = smr("gs")
            H4 = lambda ap: ap.rearrange("p (h j) -> p h j", h=4)
            bc4 = lambda ap: ap.unsqueeze(2).broadcast_to([128, 4, 128])
            bcm = lambda ap: ap.unsqueeze(1).broadcast_to([128, 4, 128])
            hsl = lambda h: slice(h * 128, (h + 1) * 128)
            bG, RbG = yield from gget(PS)
            mm(bG[:, 0:4], pp("triu"), gg[:, s, :], True, True, [Rprm, Rgg], [RbG])
            mm(bG[:, 4:8], pp("ones"), gg[:, s, :], True, True, [Rprm, Rgg], [RbG])
            yield
            op("dve", "tensor_copy", [RbG], [Rgs], out=gs, in_=bG[:, 0:8])
            PS.put((bG, RbG))
            ex = sm[:, 88:104]
            Rex = smr("ex")
            act(ex[:, 0:4], gs[:, 0:4], AF.Exp, [Rgs], [Rex])
            tt("dve", ex[:, 0:4], ex[:, 0:4], bet[:, s, :], ALU.mult, [Rex, Rbet], [Rex])
            tt("dve", ex[:, 12:16], gs[:, 4:8], gs[:, 0:4], ALU.subtract, [Rgs], [Rex])
            act(ex[:, 4:8], ex[:, 12:16], AF.Exp, [Rex], [Rex])
            act(ex[:, 8:12], gs[:, 4:8], AF.Exp, [Rgs], [Rex])
            rg, Rrg = yield from gget(FS)
            tt("dve", H4(rg[:, :]), bcm(ident_f), bc4(gs[:, 0:4]), ALU.mult, [Rprm, Rgs], [Rrg])
            bGr, RbGr = yield from gget(PS)
            mm(bGr[:, :], pp("ones"), rg[:, :], True, True, [Rprm, Rrg], [RbGr])
            FS.put((rg, Rrg))
            bK, RbK = yield from gget(PS)
            bQ, RbQ = yield from gget(PS)
            for h in range(4):
                mm(bK[:, hsl(h)], knT[:, h, tsl], knT[:, h, tsl], True, True, [RknT], [RbK])
            for h in range(4):
                mm(bQ[:, hsl(h)], knT[:, h, tsl], qnT[:, h, tsl], True, True, [RknT, RqnT], [RbQ])
            yield
            dd, Rdd = yield from gget(FS)
            e1, Re1 = yield from gget(FS)
            e2, Re2 = yield from gget(FS)
            tt("dve", H4(dd[:, :]), H4(bGr[:, :]), bc4(gs[:, 0:4]), ALU.subtract, [RbGr, Rgs], [Rdd])
            tt("dve", H4(e2[:, :]), H4(dd[:, :]), bcm(pp("masku")), ALU.add, [Rdd, Rprm], [Re2])
            act(e2[:, :], e2[:, :], AF.Exp, [Re2], [Re2])
            tt("dve", H4(e1[:, :]), H4(dd[:, :]), bcm(pp("masklneg")), ALU.subtract, [Rdd, Rprm], [Re1])
            act(e1[:, :], e1[:, :], AF.Exp, [Re1], [Re1], scale=-1.0)
            act(dd[:, :], bGr[:, :], AF.Exp, [RbGr], [Rdd])
            PS.put((bGr, RbGr))
            qd, Rqd = yield from gget(HS)
            tt("dve", H4(qd[:, 0:512]), qnT[:, :, tsl], H4(dd[:, :]), ALU.mult, [RqnT, Rdd], [Rqd])
            yield
            tt("dve", e1[:, :], bK[:, :], e1[:, :], ALU.mult, [RbK, Re1], [Re1])
            PS.put((bK, RbK))
            tt("dve", H4(dd[:, :]), H4(e1[:, :]), bc4(bet[:, s, :]), ALU.mult, [Re1, Rbet], [Rdd])
            at, Rat = yield from gget(HS)
            tt("dve", at[:, 0:512], bQ[:, :], e2[:, :], ALU.mult, [RbQ, Re2], [Rat])
            PS.put((bQ, RbQ))
            FS.put((e1, Re1))
            FS.put((e2, Re2))
            ad, Rad = yield from gget(FR)
            ao, Rao = yield from gget(FR)
            tt("dve", H4(ad[:, :].bitcast(F32R)), H4(dd[:, :]), bcm(pp("bd01")), ALU.mult, [Rdd, Rprm], [Rad])
            tt("dve", H4(ao[:, :].bitcast(F32R)), H4(dd[:, :]), bcm(pp("off01")), ALU.mult, [Rdd, Rprm], [Rao])
            FS.put((dd, Rdd))
            bT, RbT = yield from gget(PS)
            for h in range(4):
                tr(bT[:, hsl(h)], ad[:, hsl(h)], ident_f, [Rad, Rprm], [RbT])
            yield
            bm, Rbm = yield from gget(FR)
            pm, Rpm = yield from gget(FR)
            op("dve", "tensor_copy", [RbT], [Rbm], out=bm[:, :].bitcast(F32R), in_=bT[:, :])
            tt("dve", H4(pm[:, :].bitcast(F32R)), bcm(ident_f), H4(bT[:, :]), ALU.subtract, [Rprm, RbT], [Rpm])
            PS.put((bT, RbT))
            adr, bmr, pmr = ad[:, :].bitcast(F32R), bm[:, :].bitcast(F32R), pm[:, :].bitcast(F32R)
            bA, RbA = yield from gget(PS)
            bB, RbB = yield from gget(PS)
            bP, RbP = yield from gget(PS)
            for k in range(1, 6):
                for h in range(4):
                    mm(bA[:, hsl(h)], bmr[:, hsl(h)], adr[:, hsl(h)], True, True, [Rbm, Rad], [RbA])
                if k < 5:
                    for h in range(4):
                        mm(bB[:, hsl(h)], adr[:, hsl(h)], bmr[:, hsl(h)], True, True, [Rbm, Rad], [RbB])
                yield
                op("dve", "tensor_copy", [RbA], [Rad], out=adr, in_=bA[:, :])
                if k < 5:
                    op("dve", "tensor_copy", [RbB], [Rbm], out=bmr, in_=bB[:, :])
                for h in range(4):
                    mm(bP[:, hsl(h)], adr[:, hsl(h)], pmr[:, hsl(h)], True, True, [Rad, Rpm], [RbP])
                yield
                tt("dve", pmr, pm[:, :], bP[:, :], ALU.add, [Rpm, RbP], [Rpm])
            FR.put((ad, Rad))
            for h in range(4):
                tr(bA[:, hsl(h)], pm[:, hsl(h)], ident_f, [Rpm, Rprm], [RbA])
            for h in range(4):
                mm(bB[:, hsl(h)], ao[:, hsl(h)].bitcast(F32R), pmr[:, hsl(h)], True, True, [Rao, Rpm], [RbB])
            yield
            op("dve", "tensor_copy", [RbA], [Rbm], out=bmr, in_=bA[:, :])
            ym, Rym = yield from gget(FR)
            op("dve", "tensor_copy", [RbB], [Rym], out=ym[:, :].bitcast(F32R), in_=bB[:, :])
            FR.put((ao, Rao))
            for h in range(4):
                mm(bP[:, hsl(h)], bmr[:, hsl(h)], ym[:, hsl(h)].bitcast(F32R), True, True, [Rbm, Rym], [RbP])
            yield
            ttm, Rttm = yield from gget(HS)
            tt("dve", ttm[:, 0:512], pm[:, :], bP[:, :], ALU.subtract, [Rpm, RbP], [Rttm])
            for it in ((bm, Rbm), (pm, Rpm), (ym, Rym)):
                FR.put(it)
            rv, Rrv = yield from gget(HS)
            rk, Rrk = yield from gget(HS)
            kd, Rkd = yield from gget(HS)
            tt("pool", H4(rv[:, 0:512]), H4(vtok[:, s, :]), bc4(bet[:, s, :]), ALU.mult, [Rvtok, Rbet], [Rrv])
            tt("pool", H4(rk[:, 0:512]), H4(ktok[:, s, :]), bc4(ex[:, 0:4]), ALU.mult, [Rktok, Rex], [Rrk])
            tt("pool", H4(kd[:, 0:512]), H4(ktok[:, s, :]), bc4(ex[:, 4:8]), ALU.mult, [Rktok, Rex], [Rkd])
            for h in range(4):
                mm(bA[:, hsl(h)], ttm[:, hsl(h)], rv[:, hsl(h)], True, True, [Rttm, Rrv], [RbA])
            for h in range(4):
                mm(bB[:, hsl(h)], rk[:, hsl(h)], ttm[:, hsl(h)], True, True, [Rttm, Rrk], [RbB])
            yield
            u, Ru = yield from gget(FS)
            wT, RwT = yield from gget(HS)
            act(u[:, :], bA[:, :], AF.Copy, [RbA], [Ru])
            act(wT[:, 0:512], bB[:, :], AF.Copy, [RbB], [RwT])
            for it in ((ttm, Rttm), (rv, Rrv), (rk, Rrk)):
                HS.put(it)
            for h in range(4):
                mm(bP[:, hsl(h)], wT[:, hsl(h)], Sbf[:, l, h, :], True, True, [RwT, RS[l][h]], [RbP])
            yield
            vn, Rvn = yield from gget(HS)
            tt("dve", vn[:, 0:512], u[:, :], bP[:, :], ALU.subtract, [Ru, RbP], [Rvn])
            FS.put((u, Ru))
            for h in range(4):
                mm(bA[:, hsl(h)], qd[:, hsl(h)], Sbf[:, l, h, :], True, False, [Rqd, RS[l][h]], [RbA])
                mm(bA[:, hsl(h)], at[:, hsl(h)], vn[:, hsl(h)], False, True, [Rat, Rvn], [RbA])
            for h in range(4):
                mm(bB[:, hsl(h)], kd[:, hsl(h)], vn[:, hsl(h)], True, True, [Rkd, Rvn], [RbB])
            yield
            Sall = Sst[:, l].rearrange("p h d -> p (h d)")
            tt("dve", H4(Sall), H4(Sall), bc4(ex[:, 8:12]), ALU.mult, [Rex] + RS[l], RS[l])
            tt("dve", Sall, Sall, bB[:, :], ALU.add, RS[l] + [RbB], RS[l])
            act(Sbf[:, l].rearrange("p h d -> p (h d)"), Sall, AF.Copy, RS[l], RS[l])
            PS.put((bB, RbB))
            PS.put((bP, RbP))
            for it in ((wT, RwT), (vn, Rvn), (qd, Rqd), (at, Rat), (kd, Rkd)):
                HS.put(it)
            ssq = sm[:, 104:108]
            Rssq = smr("ssq")
            sq, Rsq = yield from gget(FS)
            act(sq[:, :], bA[:, :], AF.Square, [RbA], [Rsq])
            op("dve", "tensor_reduce", [Rsq], [Rssq], out=ssq, in_=H4(sq[:, :]), axis=AX.X, op=ALU.add)
            FS.put((sq, Rsq))
            act(ssq, ssq, AF.Ln, [Rssq], [Rssq], scale=1.0 / 128, bias=EPS)
            act(ssq, ssq, AF.Exp, [Rssq], [Rssq], scale=-0.5)
            on, Ron = yield from gget(HS)
            tt("dve", H4(on[:, 0:512]), H4(bA[:, :]), bc4(ssq), ALU.mult, [RbA, Rssq], [Ron])
            PS.put((bA, RbA))
            b, Rb = yield from gget(PS)
            bb = b[:].bitcast(BF16)
            for h in range(4):
                tr(bb[:, hsl(h)], on[:, hsl(h)], ident_b, [Ron, Rcst], [Rb])
            HS.put((on, Ron))
            yield
            stt(yT[:, 0, :, tsl], H4(bb[:, 0:512]), pp(f"anorm{l}"), sza[:, :, tsl], ALU.mult, ALU.mult,
                [Rb, Rprm, Rsza], [RyT[0]])
            PS.put((b, Rb))

        kv_ready = {}

        def attn_head(h, hh, hd, bO, RbO, Rq, Rk, Rv, stc):
            Rst = smr(f"stc{h}")
            nk = hh["nk"]
            bS, RbS = yield from gget(PS)
            mm(bS[:, 0:nk], hh["q"], hh["k"], True, True, [Rq, Rk], [RbS])
            yield
            mx, m_, negm, rsum, es, rden = (stc[:, i, h:h + 1] for i in range(6))
            p, Rp = yield from gget(HS)
            if hh["bias"] is not None:
                sc, Rsc = yield from gget(FS)
                stt(sc[:, 0:nk], bS[:, 0:nk], hh["scale"], hh["bias"], ALU.mult, ALU.add, [RbS, Rprm], [Rsc])
                PS.put((bS, RbS))
                if hh.get("premask") is not None:
                    ts("dve", sc[:, 0:128], sc[:, 0:128], hh["premask"], None, ALU.add, None, [Rsc, Rprm], [Rsc])
                op("dve", "tensor_reduce", [Rsc], [Rst], out=mx, in_=sc[:, 0:nk], axis=AX.X, op=ALU.max)
                tt("dve", m_, mx, hh["sink"], ALU.max, [Rst, Rprm], [Rst])
                ts("dve", negm, m_, -1.0, None, ALU.mult, None, [Rst], [Rst])
                act(p[:, 0:nk], sc[:, 0:nk], AF.Exp, [Rsc, Rst], [Rp, Rst], bias=negm, accum_out=rsum)
                FS.put((sc, Rsc))
                act(es, hh["sink"], AF.Exp, [Rprm, Rst], [Rst], bias=negm)
                tt("dve", rden, rsum, es, ALU.add, [Rst], [Rst])
            else:
                op("dve", "tensor_reduce", [RbS], [Rst], out=mx, in_=bS[:, 0:nk], axis=AX.X, op=ALU.max)
                ts("dve", negm, mx, -hh["scale"], None, ALU.mult, None, [Rst], [Rst])
                act(p[:, 0:nk], bS[:, 0:nk], AF.Exp, [RbS, Rst], [Rp, Rst], bias=negm, scale=hh["scale"], accum_out=rden)
                PS.put((bS, RbS))
            op("dve", "reciprocal", [Rst], [Rst], out=rden, in_=rden)
            yield
            nkc = nk // 128
            bT, RbT = yield from gget(PS)
            bTb = bT[:].bitcast(BF16)
            for kc in range(nkc):
                tr(bTb[:, kc * 128:(kc + 1) * 128], p[:, kc * 128:(kc + 1) * 128], ident_b, [Rp, Rcst], [RbT])
            HS.put((p, Rp))
            yield
            pT, RpT = yield from gget(HS)
            act(pT[:, 0:nk], bTb[:, 0:nk], AF.Copy, [RbT], [RpT])
            PS.put((bT, RbT))
            for kc in range(nkc):
                mm(bO[:, h * hd:(h + 1) * hd], pT[:, kc * 128:(kc + 1) * 128], hh["v"][kc], kc == 0, kc == nkc - 1,
                   [RpT, Rv], [RbO])
            HS.put((pT, RpT))
            yield

        def attention(heads, hd, Rq, Rk, Rv, ydst, Ry, sz, Rsz, tsl):
            nh = len(heads)
            bO, RbO = yield from gget(PS)
            stc = sm[:, 112:112 + 6 * 8].rearrange("p (k h) -> p k h", k=6)
            yield from pipeline([attn_head(h, hh, hd, bO, RbO, Rq, Rk, Rv, stc) for h, hh in enumerate(heads)], W_ATT)
            on, Ron = yield from gget(HS)
            rd = stc[:, 5, 0:nh]
            tt("dve", on[:, 0:512].rearrange("p (h d) -> p h d", h=nh), bO[:, :].rearrange("p (h d) -> p h d", h=nh),
               rd.unsqueeze(2).broadcast_to([128, nh, hd]), ALU.mult, [RbO] + [smr(f"stc{h}") for h in range(nh)], [Ron])
            PS.put((bO, RbO))
            b, Rb = yield from gget(PS)
            bb = b[:].bitcast(BF16)
            for c in range(4):
                tr(bb[:, c * 128:(c + 1) * 128], on[:, c * 128:(c + 1) * 128], ident_b, [Ron, Rcst], [Rb])
            HS.put((on, Ron))
            yield
            for c in range(4):
                tt("dve", ydst[:, c, tsl], bb[:, c * 128:(c + 1) * 128], sz[:, c, tsl], ALU.mult, [Rb, Rsz], [Ry])
            PS.put((b, Rb))

        def stage_C_kv(l, ti, wt, Rw):
            def ev(b, Rb):
                act(kTc[:, l, 128:640], b[:, :], AF.Identity, [Rb, Rprm], [Rkv[l]], bias=pp(f"bfm{l}", 32, 33))
            yield from proj(wt, Rw, 0, 128, ev)
            for s in range(4):
                b, Rb = yield from gget(PS)
                for kc in range(8):
                    mm(b[:, 0:128], hT[:, kc, s * 128:(s + 1) * 128], wt[:, kc, 128:256], kc == 0, kc == 7, [Rw, RhT], [Rb])
                yield
                tt("dve", vtc[:, l, 1 + s, :], b[:, 0:128], pp(f"bv{l}"), ALU.add, [Rb, Rprm], [Rkv[l]])
                PS.put((b, Rb))

        def gated_fm(l, ws, chbase, dst, Rdst, func):
            wt, Rw = yield from ws.take()
            for c in range(4):
                def ev(b, Rb, c=c):
                    act(dst[:, c, :], b[:, :], func, [Rb, Rprm], [Rdst], bias=pp(f"bfm{l}", chbase + c, chbase + c + 1))
                yield from proj(wt, Rw, c * 128, 128, ev)
            WB.put((wt, Rw))

        def stage_C(l, ti, ws):
            qT, RqT = yield from gget(GS)
            szc, Rszc = yield from gget(GS)
            yield from gated_fm(l, ws, 28, qT, RqT, AF.Identity)
            yield from gated_fm(l, ws, 36, szc, Rszc, AF.Silu)
            while not kv_ready.get((l, ti)):
                yield
            so, _sw = poff[f"sinks{l}"]
            for s in range(4):
                first = (ti == 0 and s == 0)
                heads = []
                for h in range(8):
                    c, base = h % 4, (h // 4) * 64
                    kvh = h // 4
                    if first:
                        k_ap, nk, bias = kTc[base:base + 64, l, 128:256], 128, bandv[:, h, 128:256]
                        v = [vtc[:, l, 1, kvh * 64:(kvh + 1) * 64]]
                    else:
                        k_ap, nk, bias = kTc[base:base + 64, l, s * 128:s * 128 + 256], 256, bandv[:, h, :]
                        v = [vtc[:, l, s, kvh * 64:(kvh + 1) * 64], vtc[:, l, s + 1, kvh * 64:(kvh + 1) * 64]]
                    pm_ = None
                    if pipe and ti == 1 and s == 0:
                        fo_, _ = poff["flags"]
                        pm_ = prm[:, fo_ + 2:fo_ + 3]
                    heads.append(dict(q=qT[base:base + 64, c, s * 128:(s + 1) * 128], k=k_ap, nk=nk, bias=bias,
                                      scale=0.125, sink=prm[:, so + h:so + h + 1], v=v, premask=pm_))
                yield from attention(heads, 64, RqT, Rkv[l], Rkv[l], yT[:, 2], RyT[2], szc, Rszc, slice(s * 128, (s + 1) * 128))
            op("pool", "tensor_copy", [Rkv[l]], [Rkv[l]], out=kTc[:, l, 0:128], in_=kTc[:, l, 512:640])
            op("pool", "tensor_copy", [Rkv[l]], [Rkv[l]], out=vtc[:, l, 0, :], in_=vtc[:, l, 4, :])
            GS.put((qT, RqT))
            GS.put((szc, Rszc))

        def stage_E(l, ti, ws):
            eq, Req = yield from gget(GS)
            sze, Rsze = yield from gget(GS)
            yield from gated_fm(l, ws, 48, eq, Req, AF.Identity)
            yield from gated_fm(l, ws, 52, sze, Rsze, AF.Silu)
            while not mem_ready.get(l):
                yield
            for s in range(4):
                tsl = slice(s * 128, (s + 1) * 128)
                heads = [dict(q=eq[:, h, tsl], k=mkT[:, l, h, :], nk=256, bias=None, scale=128 ** -0.5, sink=None,
                              v=[mvt[:, l, 0, h * 128:(h + 1) * 128], mvt[:, l, 1, h * 128:(h + 1) * 128]])
                         for h in range(4)]
                yield from attention(heads, 128, Req, Rmem[l], Rmem[l], yT[:, 4], RyT[4], sze, Rsze, tsl)
            GS.put((eq, Req))
            GS.put((sze, Rsze))

        def convB_item(l, c, wa, Rwa, wb, Rwb, cvv, Rcv):
            bwo, _ = poff[f"bdw{l}"]
            sg, Rsg = yield from gget(HS)

            def evb(b, Rb):
                act(sg[:, 0:512], b[:, :], AF.Sigmoid, [Rb, Rprm], [Rsg], bias=pp(f"bfm{l}", 20 + c, 21 + c))
            yield from proj(wb, Rwb, c * 128, 128, evb)
            pre, Rpre = yield from gget(HS)
            op("pool", "tensor_copy", [Rhalo[l]], [Rpre], out=pre[:, 0:30], in_=haloB[:, l, c, 0:30])

            def eva(b, Rb):
                stt(pre[:, 30:542], b[:, :], pp(f"bfm{l}", 16 + c, 17 + c), sg[:, 0:512], ALU.add, ALU.mult,
                    [Rb, Rprm, Rsg], [Rpre])
            yield from proj(wa, Rwa, c * 128, 128, eva)
            HS.put((sg, Rsg))
            op("pool", "tensor_copy", [Rpre], [Rhalo[l]], out=haloB[:, l, c, 0:30], in_=pre[:, 512:542])
            b, Rb = yield from gget(PS)
            for k in range(31):
                dgk, Rdgk = yield from gget(S16)
                ts("pool", dgk[:, :], ident_b, prm[:, bwo + c * 31 + k:bwo + c * 31 + k + 1], 0.0, ALU.mult, ALU.add,
                   [Rcst, Rprm], [Rdgk])
                mm(b[:, :], dgk[:, :], pre[:, k:k + 512], k == 0, k == 30, [Rdgk, Rpre], [Rb])
                S16.put((dgk, Rdgk))
                if k % 4 == 3:
                    yield
            HS.put((pre, Rpre))
            yield
            act(cvv[c], b[:, :], AF.Identity, [Rb, Rprm], [Rcv[c]], bias=pp(f"bdwb{l}", c, c + 1))
            PS.put((b, Rb))

        def stage_B(l, ti, ws):
            szb, Rszb = yield from gget(GS)
            g0, Rg0 = yield from gget(GS)
            g1, Rg1 = yield from gget(GS)
            yield from gated_fm(l, ws, 24, szb, Rszb, AF.Silu)
            wa, Rwa = yield from ws.take()
            wb, Rwb = yield from ws.take()
            g0f = g0[:].rearrange("p a b -> p (a b)").bitcast(F32)
            g1f = g1[:].rearrange("p a b -> p (a b)").bitcast(F32)
            cvv = [g0f[:, 0:512], g0f[:, 512:1024], g1f[:, 0:512], g1f[:, 512:1024]]
            Rcv = [Rg0, Rg0, Rg1, Rg1]
            yield from pipeline([convB_item(l, c, wa, Rwa, wb, Rwb, cvv, Rcv) for c in range(4)], W_B)
            WB.put((wa, Rwa))
            WB.put((wb, Rwb))
            bM, RbM = yield from gget(PS)
            bQ, RbQ = yield from gget(PS)
            for c in range(4):
                mm(bM[:, :], pp("ones"), cvv[c], c == 0, c == 3, [Rprm, Rcv[c]], [RbM])
            for c in range(4):
                sq, Rsq = yield from gget(FS)
                act(sq[:, :], cvv[c], AF.Square, [Rcv[c]], [Rsq])
                mm(bQ[:, :], pp("ones"), sq[:, :], c == 0, c == 3, [Rprm, Rsq], [RbQ])
                FS.put((sq, Rsq))
                yield
            mean, Rmean = yield from gget(FS)
            rstd, Rrstd = yield from gget(FS)
            act(mean[:, :], bM[:, :], AF.Copy, [RbM], [Rmean], scale=1.0 / 512)
            act(rstd[:, :], bM[:, :], AF.Square, [RbM], [Rrstd], scale=1.0 / 512)
            PS.put((bM, RbM))
            stt(rstd[:, :], bQ[:, :], 1.0 / 512, rstd[:, :], ALU.mult, ALU.subtract, [RbQ, Rrstd], [Rrstd])
            PS.put((bQ, RbQ))
            ts("dve", rstd[:, :], rstd[:, :], 0.0, EPS, ALU.max, ALU.add, [Rrstd], [Rrstd])
            act(rstd[:, :], rstd[:, :], AF.Ln, [Rrstd], [Rrstd])
            act(rstd[:, :], rstd[:, :], AF.Exp, [Rrstd], [Rrstd], scale=-0.5)
            yield
            for c in range(4):
                tt("dve", cvv[c], cvv[c], mean[:, :], ALU.subtract, [Rcv[c], Rmean], [Rcv[c]])
                tt("pool", cvv[c], cvv[c], rstd[:, :], ALU.mult, [Rcv[c], Rrstd], [Rcv[c]])
                bn, Rbn = yield from gget(HS)
                act(bn[:, 0:512], cvv[c], AF.Silu, [Rcv[c], Rprm], [Rbn], scale=pp(f"blng{l}", c, c + 1),
                    bias=pp(f"blnb{l}", c, c + 1))
                tt("dve", yT[:, 1, c, :], bn[:, 0:512], szb[:, c, :], ALU.mult, [Rbn, Rszb], [RyT[1]])
                HS.put((bn, Rbn))
                yield
            FS.put((mean, Rmean))
            FS.put((rstd, Rrstd))
            for it in ((szb, Rszb), (g0, Rg0), (g1, Rg1)):
                GS.put(it)

        def lru_item(l, c, wt, Rw, szd, Rszd):
            dwo, _ = poff[f"dconv{l}"]
            pre, Rpre = yield from gget(HS)
            op("pool", "tensor_copy", [Rhalo[l]], [Rpre], out=pre[:, 0:3], in_=haloD[:, l, c, 0:3])

            def ev(b, Rb):
                act(pre[:, 3:515], b[:, :], AF.Identity, [Rb, Rprm], [Rpre], bias=pp(f"bfm{l}", 40 + c, 41 + c))
            yield from proj(wt, Rw, c * 128, 128, ev)
            op("pool", "tensor_copy", [Rpre], [Rhalo[l]], out=haloD[:, l, c, 0:3], in_=pre[:, 512:515])
            dg, Rdg = yield from gget(DG)
            for k in range(4):
                ts("pool", dg[:, k, :], ident_b, prm[:, dwo + c * 4 + k:dwo + c * 4 + k + 1], 0.0, ALU.mult, ALU.add,
                   [Rcst, Rprm], [Rdg])
            yield
            b, Rb = yield from gget(PS)
            for k in range(4):
                mm(b[:, :], dg[:, k, :], pre[:, k:k + 512], k == 0, k == 3, [Rdg, Rpre], [Rb])
            DG.put((dg, Rdg))
            HS.put((pre, Rpre))
            yield
            while len(FS.free) < 4:
                yield
            dx, Rdx = FS.get()
            r, Rr = FS.get()
            ig, Rig = FS.get()
            a, Ra = FS.get()
            dxb, Rdxb = yield from gget(HS)
            act(dx[:, :], b[:, :], AF.Identity, [Rb, Rprm], [Rdx], bias=pp(f"dconvb{l}", c, c + 1))
            PS.put((b, Rb))
            op("pool", "tensor_copy", [Rdx], [Rdxb], out=dxb[:, 0:512], in_=dx[:, :])
            yield
            bR, RbR = yield from gget(PS)
            mm(bR[:, :], bdws[:, l, 0, c, :], dxb[:, 0:512], True, True, [Rbdw, Rdxb], [RbR])
            bI, RbI = yield from gget(PS)
            mm(bI[:, :], bdws[:, l, 1, c, :], dxb[:, 0:512], True, True, [Rbdw, Rdxb], [RbI])
            HS.put((dxb, Rdxb))
            yield
            act(r[:, :], bR[:, :], AF.Sigmoid, [RbR, Rprm], [Rr], bias=pp(f"dba{l}", c, c + 1))
            act(ig[:, :], bI[:, :], AF.Sigmoid, [RbI, Rprm], [Rig], bias=pp(f"dbx{l}", c, c + 1))
            PS.put((bR, RbR))
            PS.put((bI, RbI))
            act(a[:, :], r[:, :], AF.Exp, [Rr, Rlc], [Ra], scale=lruc[:, l, c:c + 1])
            act(r[:, :], r[:, :], AF.Exp, [Rr, Rlc], [Rr], scale=lruc[:, l, 4 + c:5 + c])
            act(r[:, :], r[:, :], AF.Sqrt, [Rr], [Rr], scale=-1.0, bias=1.0)
            tt("dve", ig[:, :], ig[:, :], dx[:, :], ALU.mult, [Rig, Rdx], [Rig])
            tt("pool", ig[:, :], ig[:, :], r[:, :], ALU.mult, [Rig, Rr], [Rig])
            FS.put((dx, Rdx))
            yield
            op("dve", "tensor_tensor_scan", [Ra, Rig, Rhst[l]], [Rr], out=r[:, :], data0=a[:, :], data1=ig[:, :],
               initial=hst[:, l, c:c + 1], op0=ALU.mult, op1=ALU.add)
            op("dve", "tensor_copy", [Rr], [Rhst[l]], out=hst[:, l, c:c + 1], in_=r[:, 511:512])
            tt("dve", yT[:, 3, c, :], r[:, :], szd[:, c, :], ALU.mult, [Rr, Rszd], [RyT[3]])
            for it in ((r, Rr), (ig, Rig), (a, Ra)):
                FS.put(it)
            yield

        def stage_D(l, ti, ws):
            szd, Rszd = yield from gget(GS)
            yield from gated_fm(l, ws, 44, szd, Rszd, AF.Silu)
            wt, Rw = yield from ws.take()
            for c in range(4):
                yield from lru_item(l, c, wt, Rw, szd, Rszd)
            WB.put((wt, Rw))
            GS.put((szd, Rszd))

        def chain_rest(l, ti):
            ws = WStream([winblk(l, 12), winblk(l, 13),
                          winblk(l, 7), winblk(l, 9)])
            ws.prefetch()
            yield from stage_E(l, ti, ws)
            marks.append(("E_end", ti, l, P.cnt["pe"]))
            yield from stage_C(l, ti, ws)
            marks.append(("C_end", ti, l, P.cnt["pe"]))

        def chain_bd(l, ti):
            ws = WStream([winblk(l, 11), winblk(l, 10),
                          winblk(l, 6), winblk(l, 4), winblk(l, 5)])
            yield from stage_D(l, ti, ws)
            marks.append(("D_end", ti, l, P.cnt["pe"]))
            yield from stage_B(l, ti, ws)
            marks.append(("B_end", ti, l, P.cnt["pe"]))

        def merge_item(l, n, j, jj, wg, Rwg, wr, Rwr, mgf, Rmgs):
            gsb, Rgsb = yield from gget(HS)

            def ev(b, Rb):
                act(gsb[:, 0:512], b[:, :], AF.Sigmoid, [Rb, Rprm], [Rgsb], bias=pp(f"bfm{l}", 56 + n * 8 + j, 57 + n * 8 + j))
            yield from proj(wg, Rwg, jj * 128, 128, ev)
            b, Rb = yield from gget(PS)
            for kc in range(4):
                mm(b[:, :], wr[:, kc, jj * 128:(jj + 1) * 128], yT[:, n, kc, :], kc == 0, kc == 3, [Rwr, RyT[n]], [Rb])
            yield
            if n == 0:
                tt("dve", mgf[j], b[:, :], gsb[:, 0:512], ALU.mult, [Rb, Rgsb], [Rmgs[j]])
            else:
                tmp, Rtmp = yield from gget(FS)
                tt("dve", tmp[:, :], b[:, :], gsb[:, 0:512], ALU.mult, [Rb, Rgsb], [Rtmp])
                tt("pool", mgf[j], mgf[j], tmp[:, :], ALU.add, [Rmgs[j], Rtmp], [Rmgs[j]])
                FS.put((tmp, Rtmp))
            PS.put((b, Rb))
            HS.put((gsb, Rgsb))
            if n == 4:
                act(mgb[:, j, :], mgf[j], AF.Copy, [Rmgs[j]], [Rmgb])
            yield

        def stage_merge(l, ti):
            mgs = []
            for i in range(4):
                mgs.append((yield from gget(GS)))
            mgf, Rmgs = [], []
            for i in range(4):
                f = mgs[i][0][:].rearrange("p a b -> p (a b)").bitcast(F32)
                mgf += [f[:, 0:512], f[:, 512:1024]]
                Rmgs += [Reg(f"mgs{2 * i}"), Reg(f"mgs{2 * i + 1}")]
                for r_ in Rmgs[-2:]:
                    r_.w, r_.r = mgs[i][1].w, list(mgs[i][1].r)
            ws = WStream([winblk(l, 14 + q) for q in range(10)])
            ws.prefetch()
            for n in range(5):
                for half in range(2):
                    wr, Rwr = yield from gget(GS)
                    kb = ("b", l * 10 + n * 2 + half)
                    if kb not in scr_seen:
                        scr_seen.add(kb)
                        Rscr[kb] = Reg(f"scrb{kb[1]}")
                        dma("pool", wr[:, :, :], wbr_d[l, n][:, half * 512:(half + 1) * 512].rearrange("(kc p) d -> p kc d", p=128),
                            W=[Rwr])
                        dma("sp", scrb_d[kb[1]], wr[:].rearrange("p a b -> p (a b)"), R=[Rwr], W=[Rscr[kb]])
                    else:
                        dma("sp", wr[:].rearrange("p a b -> p (a b)"), scrb_d[kb[1]], R=[Rscr[kb]], W=[Rwr])
                    wg, Rwg = yield from ws.take()
                    yield from pipeline([merge_item(l, n, half * 4 + jj, jj, wg, Rwg, wr, Rwr, mgf, Rmgs) for jj in range(4)], W_MRG)
                    WB.put((wg, Rwg))
                    GS.put((wr, Rwr))
            for i in range(4):
                R_ = mgs[i][1]
                R_.w = Rmgs[2 * i + 1].w
                R_.r = list(Rmgs[2 * i].r) + list(Rmgs[2 * i + 1].r) + ([Rmgs[2 * i].w] if Rmgs[2 * i].w else [])
                GS.put(mgs[i])

        def out_item(l, s, wo, gp):
            inv = sm[:, 0:4]
            Rinv = smr("inv")
            bs = []
            for half in range(2):
                b, Rb = yield from gget(PS)
                wt, Rw = wo[half]
                for kc in range(8):
                    mm(b[:, :], mgb[:, kc, s * 128:(s + 1) * 128], wt[:, kc, :], kc == 0, kc == 7, [Rmgb, Rw], [Rb])
                bs.append((b, Rb))
                yield
            ss = sm[:, 160 + 4 * s:164 + 4 * s]
            Rss = smr(f"oss{s}")
            for half in range(2):
                jt, Rj = yield from gget(HS)
                act(jt[:, 0:512], bs[half][0][:, :], AF.Square, [bs[half][1]], [Rj, Rss], accum_out=ss[:, half:half + 1])
                HS.put((jt, Rj))
            tt("dve", ss[:, 2:3], ss[:, 0:1], ss[:, 1:2], ALU.add, [Rss], [Rss])
            act(ss[:, 2:3], ss[:, 2:3], AF.Ln, [Rss], [Rss], scale=1.0 / D_MODEL, bias=EPS)
            act(ss[:, 3:4], ss[:, 2:3], AF.Exp, [Rss], [Rss], scale=-0.5)
            yield
            for half in range(2):
                b, Rb = bs[half]
                tmp, Rtmp = yield from gget(FS)
                stt(tmp[:, :], b[:, :], ss[:, 3:4], gp[half][0][:, :], ALU.mult, ALU.mult, [Rb, Rss, gp[half][1]], [Rtmp])
                PS.put((b, Rb))
                xs = xt[:, s, half * 512:(half + 1) * 512]
                stt(xs, xs, inv[:, s:s + 1], tmp[:, :], ALU.mult, ALU.add, [Rxts[s], Rinv, Rtmp], [Rxts[s]])
                FS.put((tmp, Rtmp))
                yield

        def stage_out(l, ti):
            gp = []
            for half in range(2):
                g_, Rg_ = yield from gget(FS)
                dma("sp", g_[:, :], gpost_d[l:l + 1, half * 512:(half + 1) * 512].broadcast_to([128, 512]), W=[Rg_])
                gp.append((g_, Rg_))
            ws = WStream([(wout_d[l][:, 0:512], 512, l * 26 + 24), (wout_d[l][:, 512:1024], 512, l * 26 + 25)])
            wo = []
            for half in range(2):
                wo.append((yield from ws.take()))
            for s in range(4):
                Rxts[s].w, Rxts[s].r = Rxt.w, list(Rxt.r)
            yield from pipeline([out_item(l, s, wo, gp) for s in range(4)], 2)
            Rxt.w = Rxts[3].w
            Rxt.r = [t for s in range(4) for t in Rxts[s].r] + [Rxts[s].w for s in range(3)]
            for it in wo:
                WB.put(it)
            for it in gp:
                FS.put(it)

        Rxts = [Reg(f"xts{s}") for s in range(4)]
        marks = []

        out_toks = []

        def one_layer(l, ti, extra=()):
            marks.append(("norm", ti, l, P.cnt["pe"]))
            run(norm_T(xt, Rxt, 4, f"gpre{l}", hT, RhT, sm[:, 0:4], smr("inv")))
            marks.append(("branches", ti, l, P.cnt["pe"]))
            run(par(chain_A(l, ti), chain_rest(l, ti), chain_bd(l, ti), *extra))
            marks.append(("merge", ti, l, P.cnt["pe"]))
            run(stage_merge(l, ti))
            marks.append(("out", ti, l, P.cnt["pe"]))
            run(stage_out(l, ti))

        if not pipe:
            for ti in range(NT):
                dma("sp", xt[:], x_d[ti * 512:(ti + 1) * 512, :].rearrange("(s p) d -> p s d", p=128), W=[Rxt])
                for l in range(L):
                    one_layer(l, ti)
                    if ti == NT - 1 and l == 0 and "yT" in tap_d:
                        dma("pool", tap_d["yT"], yT[:], R=RyT)
                out_toks.append(dma("sp", out_d[ti * 512:(ti + 1) * 512, :].rearrange("(s p) d -> p s d", p=128), xt[:], R=[Rxt]))
        else:
            assert L == 1
            send_d = nc.dram_tensor("pp_send", [512, D_MODEL], F32)
            recv_d = nc.dram_tensor("pp_recv", [1024, D_MODEL], F32)
            Rsend, Rrecv = Reg("pp_send"), Reg("pp_recv")
            fo, _fw = poff["flags"]
            fA, fB = prm[:, fo:fo + 1], prm[:, fo + 1:fo + 2]
            klo, khi = prm[:, fo + 4:fo + 5], prm[:, fo + 5:fo + 6]
            gsl = None
            for step in range(NT + 1):
                ti_in = min(step, NT - 1)
                xsrc = x_d[ti_in * 512:(ti_in + 1) * 512, :].rearrange("(s p) d -> p s d", p=128)
                if step == 0:
                    dma("sp", xt[:], xsrc, W=[Rxt])
                else:
                    dma("sp", xt[:], recv_d.ap()[0:512, :].rearrange("(s p) d -> p s d", p=128), R=[Rrecv], W=[Rxt])
                    for s_ in range(4):
                        g_, Rg_ = gsl[s_]
                        gf = g_[:].rearrange("p a b -> p (a b)").bitcast(F32)
                        ts("dve", xt[:, s_, :], xt[:, s_, :], fB, None, ALU.mult, None, [Rxt, Rprm], [Rxt])
                        stt(xt[:, s_, :], gf, fA, xt[:, s_, :], ALU.mult, ALU.add, [Rg_, Rprm, Rxt], [Rxt])
                    for it in gsl:
                        GS.put(it)
                one_layer(0, step, extra=([mem_phase()] if step == 0 else ()))
                if step == 0:
                    def wipe(ap, R):
                        ts("dve", ap, ap, klo, khi, ALU.max, ALU.min, list(R) + [Rprm], list(R))
                    wipe(Sst[:, 0].rearrange("p h d -> p (h d)"), RS[0])
                    wipe(Sbf[:, 0].rearrange("p h d -> p (h d)"), RS[0])
                    wipe(hst[:, 0, :], [Rhst[0]])
                    wipe(haloA[:, 0].rearrange("p c k -> p (c k)"), [Rhalo[0]])
                    wipe(haloB[:, 0].rearrange("p c k -> p (c k)"), [Rhalo[0]])
                    wipe(haloD[:, 0].rearrange("p c k -> p (c k)"), [Rhalo[0]])
                    wipe(kTc[:, 0, 0:128], [Rkv[0]])
                    wipe(vtc[:, 0, 0, :], [Rkv[0]])
                if step < NT:
                    dma("sp", send_d.ap().rearrange("(s p) d -> p s d", p=128), xt[:], R=[Rxt], W=[Rsend])
                    if os.environ.get("PIPE_NOCC"):
                        dma("sp", recv_d.ap()[0:512, :], send_d.ap(), R=[Rsend], W=[Rrecv])
                    else:
                        P.emit("pool", lambda e: e.collective_compute(
                            "AllGather", ALU.bypass, replica_groups=[[0, 1], [2, 3], [4, 5], [6, 7]],
                            ins=[send_d.ap().opt()], outs=[recv_d.ap().opt()]),
                            reads=[Rsend], writes=[Rrecv], cc=True, cost=40000.0)
                extra = [Rsend] if step < NT else []
                if step >= 1:
                    to = step - 1
                    out_toks.append(dma("sp", out_d[to * 512:(to + 1) * 512, :].rearrange("(s p) d -> p s d", p=128), xt[:],
                                        R=[Rxt] + extra))
                if step < NT:
                    tn = min(step + 1, NT - 1)
                    xn = x_d[tn * 512:(tn + 1) * 512, :].rearrange("(s p) d -> p s d", p=128)
                    gsl = [GS.get() for _ in range(4)]
                    for s_ in range(4):
                        g_, Rg_ = gsl[s_]
                        dma("sp", g_[:].rearrange("p a b -> p (a b)").bitcast(F32), xn[:, s_, :], R=extra, W=[Rg_])
        P.finish_wait("sp", out_toks)
        P.run()
        build.stats = dict(cnt=dict(P.cnt), dmas=P.dma_n, marks=marks, model=dict(P.eng_time), busy=dict(P.busy), stall=dict(P.stall))
    return nc


_NC_CACHE = {}


_PER_LAYER = ("g_pre", "g_post", "w_in", "b_in", "a_conv_w", "a_log", "a_dt_bias", "a_norm_g", "b_dw_w", "b_dw_b",
              "b_ln_g", "b_ln_b", "c_sinks", "d_conv_w", "d_conv_b", "d_w_a", "d_b_a", "d_w_x", "d_b_x", "d_lambda",
              "g_mem", "w_mem_kv", "w_br", "w_out")


def kernel(**inputs):
    inp = {k: np.asarray(v) for k, v in inputs.items()}
    B, T, _ = inp["x"].shape
    L = inp["g_pre"].shape[0]
    assert L == 2 and 2 * B <= 8
    key = (T, "pipe")
    if key not in _NC_CACHE:
        _NC_CACHE[key] = build(T, 1, pipe=True)
    nc = _NC_CACHE[key]
    per_stage = []
    for l in range(L):
        inp_l = {k: (v[l:l + 1] if k in _PER_LAYER else v) for k, v in inp.items()}
        prm, winp, bdw, g_post = _host_params(inp_l, 1, stage=l)
        per_stage.append(dict(winp=winp, wbr=np.ascontiguousarray(inp["w_br"][l:l + 1], np.float32),
                              wout=np.ascontiguousarray(inp["w_out"][l:l + 1], np.float32),
                              wmem=np.ascontiguousarray(inp["w_mem_kv"][l:l + 1], np.float32),
                              prm=prm, bdw=bdw, gpost=g_post))
    zx = np.zeros((T, D_MODEL), np.float32)
    in_maps = []
    for b in range(B):
        for stage in range(2):
            m = dict(per_stage[stage])
            m["x"] = np.ascontiguousarray(inp["x"][b], np.float32)
            m["mem"] = np.ascontiguousarray(inp["mem"][b], np.float32)
            in_maps.append(m)
    res = run_bass_kernel_spmd(nc, in_maps, core_ids=list(range(2 * B)))
    kernel.last_all = [np.asarray(r["out"]) for r in res.results]
    return np.stack([np.asarray(res.results[2 * b + 1]["out"]) for b in range(B)], axis=0).astype(np.float32)
```

```python
import os
import numpy as np
from contextlib import ExitStack
import concourse.bass as bass
import concourse.mybir as mybir
from concourse.bass_utils import run_bass_kernel_spmd

F32 = mybir.dt.float32
F32R = mybir.dt.float32r
BF16 = mybir.dt.bfloat16
AF = mybir.ActivationFunctionType
ALU = mybir.AluOpType
AX = mybir.AxisListType

D_MODEL = 1024
SEQ = 4096
DEPTH = 2
MEM_LEN = 256
IN_COLS = 12040
NCOLP = 12288
EPS = 1e-6
NEG = -30000.0

ENGS = ("pe", "act", "dve", "pool", "sp")
EPOCH = 12000
FINE = bool(int(os.environ.get('FINE', 1)))
WAW_SELF = bool(int(os.environ.get('WAW_SELF', 0)))
W_CONVA = int(os.environ.get('W_CONVA', 2))
W_GDN = int(os.environ.get('W_GDN', 2))
W_ATT = int(os.environ.get('W_ATT', 3))
W_MRG = int(os.environ.get('W_MRG', 3))
W_B = int(os.environ.get('W_B', 2))
NDMASEM = 24


class Reg:
    __slots__ = ("name", "w", "r")

    def __init__(self, name):
        self.name = name
        self.w = None
        self.r = []


class Prog:
    def __init__(self, nc, stack):
        self.nc = nc
        self.stack = stack
        self.q = {e: [] for e in ENGS}
        self.cnt = {e: 0 for e in ENGS}
        self.sems = {e: [] for e in ENGS}
        self.seen = {e: {} for e in ENGS}
        self.dma_sems = [stack.enter_context(nc.semaphore(f"dq{i}")) for i in range(NDMASEM)]
        self.dma_n = 0
        self.dma_tok = {}
        self.cc_sems = [stack.enter_context(nc.semaphore(f"cc{i}")) for i in range(2)]
        self.cc_n = 0
        self.cc_tok = {}
        self.eng_time = {e: 0.0 for e in ENGS}
        self.step_fin = 0.0

    def _sem(self, e, epoch):
        while len(self.sems[e]) <= epoch:
            self.sems[e].append(self.stack.enter_context(
                self.nc.semaphore(f"s_{e}_{len(self.sems[e])}")))
        return self.sems[e][epoch]

    def _need(self, e, tok, waits):
        key, val = tok[0], tok[1]
        if self.seen[e].get(key, 0) >= val:
            return
        self.seen[e][key] = val
        waits[key] = max(waits.get(key, 0), val)

    def _wl(self, waits):
        wl = []
        for key, val in waits.items():
            if key[0] == "d":
                wl.append((self.dma_sems[key[1]], val))
            elif key[0] == "c":
                wl.append((self.cc_sems[key[1]], val))
            else:
                wl.append((self._sem(key[0], key[1]), val))
        return wl

    def emit(self, e, fn, reads=(), writes=(), dma=False, selfdep=True, cost=300.0, cc=False):
        waits = {}
        ready = 0.0
        for r in reads:
            t = r.w
            if t is not None:
                ready = max(ready, t[4])
                if (selfdep or t[2] != e or dma or t[3]):
                    self._need(e, t, waits)
        for w in writes:
            t = w.w
            if t is not None:
                ready = max(ready, t[4])
                if (t[2] != e or dma or t[3] or (selfdep and WAW_SELF)):
                    self._need(e, t, waits)
            for t in w.r:
                ready = max(ready, t[4])
                if t[2] != e or dma or t[3]:
                    self._need(e, t, waits)
        cost = cost * float(os.environ.get("CS_" + ("dma" if dma else e), 1.0))
        if dma:
            t0 = max(ready + 100.0, self.eng_time[e])
            self.eng_time[e] = t0 + 60.0
            fin = t0 + cost
        else:
            t0 = max(ready + float(os.environ.get('CS_lat', 150.0)), self.eng_time[e]) if ready > self.eng_time[e] - 1e-9 and waits else max(ready, self.eng_time[e])
            fin = t0 + cost
            self.eng_time[e] = fin
        self.step_fin = max(self.step_fin, fin)
        self.busy = getattr(self, "busy", {})
        self.busy[e] = self.busy.get(e, 0.0) + cost
        self.stall = getattr(self, "stall", {})
        self.stall[e] = self.stall.get(e, 0.0) + max(0.0, t0 - max(self.eng_time[e] - (cost if not dma else 60.0), 0.0))
        if cc:
            k = self.cc_n
            self.cc_n += 1
            si = k % 2
            val = k // 2 + 1
            if k >= 2:
                self._need(e, self.cc_tok[k - 2], waits)
            tok = (("c", si), val, e, True, fin)
            self.cc_tok[k] = tok
            inc = (self.cc_sems[si], 1)
        elif dma:
            k = self.dma_n
            self.dma_n += 1
            si = k % NDMASEM
            val = 16 * (k // NDMASEM + 1)
            if k >= NDMASEM:
                self._need(e, self.dma_tok[k - NDMASEM], waits)
            tok = (("d", si), val, e, True, fin)
            self.dma_tok[k] = tok
            inc = (self.dma_sems[si], 16)
        else:
            self.cnt[e] += 1
            c = self.cnt[e]
            epoch, val = (c - 1) // EPOCH, (c - 1) % EPOCH + 1
            tok = ((e, epoch), val, e, False, fin)
            inc = (self._sem(e, epoch), 1)
        self.q[e].append((self._wl(waits), fn, inc))
        for r in reads:
            if not r.r or r.r[-1] is not tok:
                r.r.append(tok)
        for w in writes:
            w.w = tok
            w.r = []
        return tok

    def finish_wait(self, e, toks):
        waits = {}
        for t in toks:
            self._need(e, t, waits)
        self.q[e].append((self._wl(waits), None, None))

    def run(self):
        nc = self.nc
        with nc.Block() as block:
            def play(eng, items):
                for wl, fn, inc in items:
                    for s, v in wl:
                        eng.wait_ge(s, v)
                    if fn is not None:
                        fn(eng).then_inc(inc[0], inc[1])

            @block.tensor
            def _(eng):
                play(eng, self.q["pe"])

            @block.scalar
            def _(eng):
                play(eng, self.q["act"])

            @block.vector
            def _(eng):
                play(eng, self.q["dve"])

            @block.gpsimd
            def _(eng):
                play(eng, self.q["pool"])

            @block.sync
            def _(eng):
                play(eng, self.q["sp"])


class Slots:
    def __init__(self, items):
        self.free = list(items)
        self.n = len(items)

    def get(self):
        if not self.free:
            raise RuntimeError("slot pool exhausted")
        return self.free.pop(0)

    def put(self, it):
        self.free.append(it)


def _win_perm():
    p = list(range(0, 2048))
    p += list(range(2056, 3592))
    cq0 = 3592
    for c in range(4):
        p += list(range(cq0 + c * 64, cq0 + (c + 1) * 64))
        p += list(range(cq0 + (4 + c) * 64, cq0 + (5 + c) * 64))
    p += list(range(4104, 4360)) + list(range(2048, 2056)) + [-1] * 248
    p += list(range(4360, 12040))
    p = np.array(p, dtype=np.int64)
    assert p.size == NCOLP
    return p


def _t5_bucket(dist):
    n = np.maximum(dist, 0)
    max_exact = 16
    large = max_exact + (np.log(np.maximum(n, 1) / max_exact) / np.log(128 / max_exact) * (32 - max_exact)).astype(np.int32)
    large = np.minimum(large, 31)
    return np.where(n < max_exact, n, large).astype(np.int32)


def _param_layout(L):
    lay = [("ident", 128), ("triu", 128), ("masku", 128), ("masklneg", 128), ("bd01", 128), ("off01", 128),
           ("ones", 128), ("maskc", 256), ("bandb", 2048), ("flags", 6)]
    for l in range(L):
        lay += [(f"gpre{l}", 8), (f"gmem{l}", 8), (f"bfm{l}", 96), (f"bba{l}", 8), (f"bv{l}", 128),
                (f"aconv{l}", 48), (f"alog{l}", 4), (f"adt{l}", 4), (f"anorm{l}", 1),
                (f"bdw{l}", 124), (f"bdwb{l}", 4), (f"blng{l}", 4), (f"blnb{l}", 4),
                (f"sinks{l}", 8), (f"dconv{l}", 16), (f"dconvb{l}", 4), (f"dba{l}", 4), (f"dbx{l}", 4),
                (f"dlam{l}", 4)]
    off = {}
    o = 0
    for n, w in lay:
        off[n] = (o, w)
        o += w
    return off, o


def _host_params(inp, L, stage=0):
    off, tot = _param_layout(L)
    prm = np.zeros((128, tot), np.float32)
    fA, fB = (1.0, 0.0) if stage == 0 else (0.0, 1.0)
    o_, _w = off["flags"]
    big = 3.0e38 if stage == 0 else 0.0
    prm[:, o_:o_ + 6] = np.array([fA, fB, NEG if stage == 1 else 0.0, fA, -big, big], np.float32)[None, :]

    def put(name, a):
        o, w = off[name]
        prm[:, o:o + w] = np.asarray(a, np.float32).reshape(128, w)

    def fm(v, c):
        return np.asarray(v).reshape(c, 128).T

    def row(v):
        v = np.asarray(v).reshape(1, -1)
        return np.broadcast_to(v, (128, v.shape[1]))

    i = np.arange(128)
    put("ident", np.eye(128))
    put("triu", (i[:, None] <= i[None, :]))
    put("masku", np.where(i[None, :] >= i[:, None], 0.0, NEG))
    put("masklneg", np.where(i[:, None] > i[None, :], 0.0, NEG))
    blk = (i[:, None] // 64) == (i[None, :] // 64)
    put("bd01", blk)
    put("off01", ~blk)
    put("ones", np.ones((128, 128)))
    ii = np.arange(128)[:, None]
    jj = np.arange(256)[None, :]
    dist = ii + 128 - jj
    put("maskc", np.where((dist >= 0) & (dist < 128), 0.0, NEG))
    bucket = _t5_bucket(dist)
    bb = inp["rel_bias"][bucket]
    put("bandb", np.transpose(bb, (0, 2, 1)))
    perm = _win_perm()
    for l in range(L):
        put(f"gpre{l}", fm(inp["g_pre"][l], 8))
        put(f"gmem{l}", fm(inp["g_mem"][l], 8))
        b = inp["b_in"][l]
        bp = np.where(perm >= 0, b[np.maximum(perm, 0)], 0.0)
        put(f"bfm{l}", fm(bp, 96))
        put(f"bba{l}", row(b[2048:2056]))
        put(f"bv{l}", row(b[4232:4360]))
        put(f"aconv{l}", np.transpose(inp["a_conv_w"][l].reshape(4, 12, 128), (2, 1, 0)))
        put(f"alog{l}", row(inp["a_log"][l]))
        put(f"adt{l}", row(inp["a_dt_bias"][l]))
        put(f"anorm{l}", inp["a_norm_g"][l].reshape(128, 1))
        put(f"bdw{l}", np.transpose(inp["b_dw_w"][l].reshape(31, 4, 128), (2, 1, 0)))
        put(f"bdwb{l}", fm(inp["b_dw_b"][l], 4))
        put(f"blng{l}", fm(inp["b_ln_g"][l], 4))
        put(f"blnb{l}", fm(inp["b_ln_b"][l], 4))
        put(f"sinks{l}", row(inp["c_sinks"][l]))
        put(f"dconv{l}", np.transpose(inp["d_conv_w"][l].reshape(4, 4, 128), (2, 1, 0)))
        put(f"dconvb{l}", fm(inp["d_conv_b"][l], 4))
        put(f"dba{l}", fm(inp["d_b_a"][l], 4))
        put(f"dbx{l}", fm(inp["d_b_x"][l], 4))
        put(f"dlam{l}", fm(inp["d_lambda"][l], 4))
    w_in = inp["w_in"][:L]
    winp = np.zeros((L, D_MODEL, NCOLP), np.float32)
    valid = perm >= 0
    winp[:, :, valid] = w_in[:, :, perm[valid]]
    bdw = np.zeros((L, 2, 4, 128, 128), np.float32)
    for l in range(L):
        for t, nm in enumerate(("d_w_a", "d_w_x")):
            w = inp[nm][l]
            for c in range(4):
                bdw[l, t, c, 0:64, 0:64] = w[2 * c]
                bdw[l, t, c, 64:128, 64:128] = w[2 * c + 1]
    g_post = np.ascontiguousarray(inp["g_post"][:L], np.float32)
    return prm, winp, bdw, g_post


def build(T, L, taps=(), pipe=False):
    NT = T // 512
    nc = bass.Bass("TRN2", target_bir_lowering=False)
    poff, ptot = _param_layout(L)
    x_d = nc.dram_tensor("x", [T, D_MODEL], F32, kind="ExternalInput").ap()
    mem_d = nc.dram_tensor("mem", [MEM_LEN, D_MODEL], F32, kind="ExternalInput").ap()
    win_d = nc.dram_tensor("winp", [L, D_MODEL, NCOLP], F32, kind="ExternalInput").ap()
    wbr_d = nc.dram_tensor("wbr", [L, 5, 512, D_MODEL], F32, kind="ExternalInput").ap()
    wout_d = nc.dram_tensor("wout", [L, D_MODEL, D_MODEL], F32, kind="ExternalInput").ap()
    wmem_d = nc.dram_tensor("wmem", [L, D_MODEL, D_MODEL], F32, kind="ExternalInput").ap()
    prm_d = nc.dram_tensor("prm", [128, ptot], F32, kind="ExternalInput").ap()
    bdw_d = nc.dram_tensor("bdw", [L, 2, 4, 128, 128], F32, kind="ExternalInput").ap()
    gpost_d = nc.dram_tensor("gpost", [L, D_MODEL], F32, kind="ExternalInput").ap()
    out_d = nc.dram_tensor("out", [T, D_MODEL], F32, kind="ExternalOutput").ap()
    NSCR = 26 * L
    scr_d = nc.dram_tensor("wscr", [NSCR, 128, 8 * 512], BF16, kind="Internal").ap()
    scrb_d = nc.dram_tensor("wscrb", [L * 10, 128, 4 * 512], BF16, kind="Internal").ap()
    tap_d = {}
    for name, shape in taps:
        tap_d[name] = nc.dram_tensor("tap_" + name, list(shape), F32, kind="ExternalOutput").ap()

    with ExitStack() as st:
        P = Prog(nc, st)

        def sb(name, shape, dt):
            return st.enter_context(nc.sbuf_tensor("sb_" + name, list(shape), dt))

        def _fsize(ap):
            n = 1
            for d in ap.shape[1:]:
                n *= d
            return n

        def op(eng, meth, R=(), W=(), **kw):
            o = kw.get("out", kw.get("ap"))
            n = _fsize(o) if o is not None else 128
            if eng == "pe":
                if meth == "transpose":
                    c = 64.0 + 128 * 0.5
                else:
                    nn = _fsize(kw["rhs"])
                    f = 4.0 if kw["rhs"].dtype in (F32, F32R) else 1.0
                    c = 40.0 + nn * 0.52 * f
            elif eng == "act":
                c = 220.0 + n * 0.9
            elif eng == "dve":
                c = 120.0 + n * 0.75
            else:
                c = 250.0 + n * 2.0
            return P.emit(eng, lambda e: getattr(e, meth)(**kw), reads=R, writes=W, selfdep=(eng != "pe"), cost=c)

        def mm(out, lhsT, rhs, start, stop, R, W):
            return op("pe", "matmul", R, W, out=out, lhsT=lhsT, rhs=rhs, start=start, stop=stop)

        def tr(out, in_, ident, R, W):
            return op("pe", "transpose", R, W, out=out, in_=in_, identity=ident)

        def act(out, in_, func, R, W, **kw):
            return op("act", "activation", R, W, out=out, in_=in_, func=func, **kw)

        def dma(eng, out, in_, R=(), W=()):
            nb = 128 * _fsize(out) * 4
            return P.emit(eng, lambda e: e.dma_start(out=out, in_=in_), reads=R, writes=W, dma=True, cost=2500.0 + nb / 150.0)

        def ts(eng, out, in0, s1, s2, op0, op1, R, W):
            if s2 is None:
                return op(eng, "tensor_scalar", R, W, out=out, in0=in0, scalar1=s1, scalar2=None, op0=op0)
            return op(eng, "tensor_scalar", R, W, out=out, in0=in0, scalar1=s1, scalar2=s2, op0=op0, op1=op1)

        def tt(eng, out, in0, in1, o, R, W):
            return op(eng, "tensor_tensor", R, W, out=out, in0=in0, in1=in1, op=o)

        def stt(out, in0, scalar, in1, op0, op1, R, W):
            return op("dve", "scalar_tensor_tensor", R, W, out=out, in0=in0, scalar=scalar, in1=in1, op0=op0, op1=op1)

        prm = sb("prm", [128, ptot], F32)
        Rprm = Reg("prm")

        def pp(name, lo=0, hi=None):
            o, w = poff[name]
            hi = w if hi is None else hi
            return prm[:, o + lo:o + hi]

        cst = sb("cst", [128, 6, 128], BF16)
        cstr = sb("cstr", [128, 2, 128], F32)
        Rcst = Reg("cst")
        xt = sb("xt", [128, 4, D_MODEL], F32)
        Rxt = Reg("xt")
        hT = sb("hT", [128, 8, 512], BF16)
        RhT = Reg("hT")
        yT = sb("yT", [128, 5, 4, 512], BF16)
        RyT = [Reg(f"yT{n}") for n in range(5)]
        mgb = sb("mgb", [128, 8, 512], BF16)
        Rmgb = Reg("mgb")
        sm = sb("sm", [128, 256], F32)
        Rsm = {}

        def smr(name):
            if name not in Rsm:
                Rsm[name] = Reg("sm_" + name)
            return Rsm[name]

        NW = 5
        WB = Slots([(sb(f"wb{i}", [128, 8, 512], BF16), Reg(f"wb{i}")) for i in range(NW)])
        PS = Slots([(st.enter_context(nc.psum_tensor(f"ps{i}", [128, 512], F32)), Reg(f"ps{i}")) for i in range(8)])
        FS = Slots([(sb(f"f{i}", [128, 512], F32), Reg(f"f{i}")) for i in range(7)])
        HS = Slots([(sb(f"h{i}", [128, 544], BF16), Reg(f"h{i}")) for i in range(10)])
        GS = Slots([(sb(f"g{i}", [128, 4, 512], BF16), Reg(f"g{i}")) for i in range(8)])
        FR = Slots([(sb(f"fr{i}", [128, 512], F32), Reg(f"fr{i}")) for i in range(5)])
        S16 = Slots([(sb(f"s16_{i}", [128, 128], BF16), Reg(f"s16_{i}")) for i in range(4)])
        DG = Slots([(sb(f"dg{i}", [128, 4, 128], BF16), Reg(f"dg{i}")) for i in range(2)])

        haloA = sb("haloA", [128, L, 12, 4], BF16)
        haloB = sb("haloB", [128, L, 4, 32], BF16)
        haloD = sb("haloD", [128, L, 4, 4], BF16)
        Rhalo = [Reg(f"halo{l}") for l in range(L)]
        Sst = sb("Sst", [128, L, 4, 128], F32)
        Sbf = sb("Sbf", [128, L, 4, 128], BF16)
        RS = [[Reg(f"S{l}_{h}") for h in range(4)] for l in range(L)]
        hst = sb("hst", [128, L, 4], F32)
        Rhst = [Reg(f"hst{l}") for l in range(L)]
        kTc = sb("kTc", [128, L, 640], BF16)
        vtc = sb("vtc", [128, L, 5, 128], BF16)
        Rkv = [Reg(f"kv{l}") for l in range(L)]
        mkT = sb("mkT", [128, L, 4, 256], BF16)
        mvt = sb("mvt", [128, L, 2, 512], BF16)
        Rmem = [Reg(f"mem{l}") for l in range(L)]
        bdws = sb("bdws", [128, L, 2, 4, 128], BF16)
        Rbdw = Reg("bdw")
        lruc = sb("lruc", [128, L, 8], F32)
        nA = sb("nA", [128, L, 4], F32)
        Rlc = Reg("lruc")

        ident_f = pp("ident")
        ident_b = cst[:, 0, :]
        ones_b = cst[:, 1, :]
        ones_r = cstr[:, 0, :].bitcast(F32R)

        dma("sp", prm[:], prm_d, W=[Rprm])
        op("dve", "tensor_copy", [Rprm], [Rcst], out=cst[:, 0, :], in_=pp("ident"))
        op("dve", "tensor_copy", [Rprm], [Rcst], out=cst[:, 1, :], in_=pp("ones"))
        op("dve", "tensor_copy", [Rprm], [Rcst], out=cstr[:, 0, :].bitcast(F32R), in_=pp("ones"))
        bo_, bw_ = poff["bandb"]
        bandv = prm[:, bo_:bo_ + bw_].rearrange("p (h j) -> p h j", h=8)
        tt("dve", bandv, bandv, pp("maskc").unsqueeze(1).broadcast_to([128, 8, 256]), ALU.add, [Rprm], [Rprm])
        for l in range(L):
            dma("pool", bdws[:, l].rearrange("p t c m -> p (t c) m"),
                bdw_d[l].rearrange("t c p m -> p (t c) m"), W=[Rbdw])
            act(lruc[:, l, 0:4], pp(f"dlam{l}"), AF.Exp, [Rprm], [Rlc], scale=-1.0)
            act(lruc[:, l, 0:4], lruc[:, l, 0:4], AF.Ln, [Rlc], [Rlc], bias=1.0)
            ts("dve", lruc[:, l, 4:8], lruc[:, l, 0:4], -16.0, None, ALU.mult, None, [Rlc], [Rlc])
            ts("dve", lruc[:, l, 0:4], lruc[:, l, 0:4], -8.0, None, ALU.mult, None, [Rlc], [Rlc])
            act(nA[:, l, :], pp(f"alog{l}"), AF.Exp, [Rprm], [Rlc])
            ts("dve", nA[:, l, :], nA[:, l, :], -1.0, None, ALU.mult, None, [Rlc], [Rlc])
            op("dve", "memset", [], [Rhalo[l]], ap=haloA[:, l], constant=0.0)
            op("dve", "memset", [], [Rhalo[l]], ap=haloB[:, l], constant=0.0)
            op("dve", "memset", [], [Rhalo[l]], ap=haloD[:, l], constant=0.0)
            for h in range(4):
                op("dve", "memset", [], [RS[l][h]], ap=Sst[:, l, h, :], constant=0.0)
                op("dve", "memset", [], [RS[l][h]], ap=Sbf[:, l, h, :], constant=0.0)
            op("dve", "memset", [], [Rhst[l]], ap=hst[:, l, :], constant=0.0)
            op("dve", "memset", [], [Rkv[l]], ap=kTc[:, l, :], constant=0.0)
            op("dve", "memset", [], [Rkv[l]], ap=vtc[:, l], constant=0.0)

        def gget(pool):
            while not pool.free:
                P.blocked = True
                yield
            P.progress = True
            return pool.get()

        def run(g):
            idle = 0
            P.progress = False
            for _ in g:
                if P.progress:
                    idle = 0
                else:
                    idle += 1
                    if idle > 10000:
                        raise RuntimeError("build-time scheduling deadlock")
                P.progress = False

        def _step_best(active, rdy, bias=None):
            g = min(active, key=lambda x: rdy[id(x)] - (bias.get(id(x), 0.0) if bias else 0.0))
            save = P.step_fin
            P.step_fin = 0.0
            P.blocked = False
            try:
                next(g)
                if P.step_fin > 0.0:
                    rdy[id(g)] = P.step_fin
                else:
                    others = [rdy[id(x)] for x in active if x is not g]
                    rdy[id(g)] = (min(others) if others else rdy[id(g)]) + 50.0
            except StopIteration:
                active.remove(g)
                P.progress = True
            P.step_fin = max(save, P.step_fin)

        def par(*gens, prio=None):
            active = list(gens)
            rdy = {id(g): 0.0 for g in active}
            bias = {id(g): (prio[i] if prio else 0.0) for i, g in enumerate(active)}
            while active:
                _step_best(active, rdy, bias)
                yield

        def pipeline(gens, width):
            it = iter(gens)
            active = []
            rdy = {}
            done = False
            while True:
                while not done and len(active) < width:
                    try:
                        active.append(next(it))
                        P.progress = True
                    except StopIteration:
                        done = True
                if not active:
                    return
                for g in active:
                    rdy.setdefault(id(g), 0.0)
                _step_best(active, rdy)
                yield

        Rscr = {}
        scr_seen = set()

        class WStream:
            def __init__(self, srcs):
                self.srcs = srcs
                self.i = 0
                self.pend = {}

            def _issue(self, i, slot):
                wt, Rw = slot
                src, ncols, key = self.srcs[i]
                if key is None or ncols != 512:
                    dma("pool", wt[:, :, 0:ncols], src.rearrange("(kc p) n -> p kc n", p=128), W=[Rw])
                elif key not in scr_seen:
                    scr_seen.add(key)
                    Rscr[key] = Reg(f"scr{key}")
                    dma("pool", wt[:, :, 0:ncols], src.rearrange("(kc p) n -> p kc n", p=128), W=[Rw])
                    dma("sp", scr_d[key], wt[:].rearrange("p a b -> p (a b)"), R=[Rw], W=[Rscr[key]])
                else:
                    dma("sp", wt[:].rearrange("p a b -> p (a b)"), scr_d[key], R=[Rscr[key]], W=[Rw])
                self.pend[i] = slot

            def prefetch(self):
                if self.i < len(self.srcs) and self.i not in self.pend and WB.free:
                    self._issue(self.i, WB.get())

            def take(self):
                i = self.i
                self.i += 1
                if i not in self.pend:
                    slot = yield from gget(WB)
                    self._issue(i, slot)
                cur = self.pend.pop(i)
                self.prefetch()
                return cur

        def winblk(l, blk):
            return (win_d[l][:, blk * 512:(blk + 1) * 512], 512, l * 26 + blk)

        def norm_T(src, Rsrc, nsub, gname, dst, Rdst, inv_ap, Rinv):
            for s in range(nsub):
                jt, Rj = yield from gget(FS)
                act(jt[:].bitcast(BF16), src[:, s, :], AF.Square, [Rsrc], [Rj, Rinv], accum_out=inv_ap[:, s:s + 1])
                FS.put((jt, Rj))
            rs = sm[:, 8:8 + nsub]
            Rrs = smr("rs")
            act(inv_ap, inv_ap, AF.Ln, [Rinv], [Rinv], scale=1.0 / D_MODEL, bias=EPS)
            act(rs, inv_ap, AF.Exp, [Rinv], [Rrs], scale=-0.5)
            act(inv_ap, inv_ap, AF.Exp, [Rinv], [Rinv], scale=0.5)
            for s in range(nsub):
                ts("dve", src[:, s, :], src[:, s, :], rs[:, s:s + 1], None, ALU.mult, None, [Rsrc, Rrs], [Rsrc])
            for c in range(8):
                b, Rb = yield from gget(PS)
                for s in range(nsub):
                    tr(b[:, s * 128:(s + 1) * 128], src[:, s, c * 128:(c + 1) * 128], ident_f, [Rsrc, Rprm], [Rb])
                act(dst[:, c, 0:nsub * 128], b[:, 0:nsub * 128], AF.Copy, [Rb, Rprm], [Rdst], scale=pp(gname, c, c + 1))
                PS.put((b, Rb))
                yield

        def proj(wt, Rw, cofs, M, evac, ntok=512, rhs=None, Rrhs=None):
            rhs = hT if rhs is None else rhs
            Rrhs = RhT if Rrhs is None else Rrhs
            b, Rb = yield from gget(PS)
            for kc in range(8):
                mm(b[0:M, 0:ntok], wt[:, kc, cofs:cofs + M], rhs[:, kc, 0:ntok], kc == 0, kc == 7, [Rw, Rrhs], [Rb])
            yield
            evac(b, Rb)
            PS.put((b, Rb))

        mem_ready = {}
        if pipe:
            memt = sb("memt", [128, 2, D_MODEL], F32)
            memT = sb("memT", [128, 8, 256], BF16)
            Rmemt, RmemT = Reg("memt"), Reg("memT")
            minv, Rminv = sm[:, 200:202], smr("minv")
        else:
            memt, Rmemt, memT, RmemT = xt, Rxt, hT, RhT
            minv, Rminv = sm[:, 0:2], smr("inv")

        def mem_phase():
            for l in range(L):
                dma("sp", memt[:, 0:2, :], mem_d.rearrange("(s p) d -> p s d", p=128), W=[Rmemt])
                yield from norm_T(memt, Rmemt, 2, f"gmem{l}", memT, RmemT, minv, Rminv)
                ws = WStream([(wmem_d[l][:, 0:512], 512, None), (wmem_d[l][:, 512:1024], 512, None)])
                wt, Rw = yield from ws.take()
                for h in range(4):
                    def ev(b, Rb, h=h, l=l):
                        act(mkT[:, l, h, :], b[:, 0:256], AF.Copy, [Rb], [Rmem[l]])
                    yield from proj(wt, Rw, h * 128, 128, ev, ntok=256, rhs=memT, Rrhs=RmemT)
                WB.put((wt, Rw))
                wt, Rw = yield from ws.take()
                for s in range(2):
                    b, Rb = yield from gget(PS)
                    for kc in range(8):
                        mm(b[:, :], memT[:, kc, s * 128:(s + 1) * 128], wt[:, kc, :], kc == 0, kc == 7, [Rw, RmemT], [Rb])
                    yield
                    act(mvt[:, l, s, :], b[:, :], AF.Copy, [Rb], [Rmem[l]])
                    PS.put((b, Rb))
                WB.put((wt, Rw))
                mem_ready[l] = True
        if not pipe:
            run(mem_phase())

        def convA_item(l, grp, c, wt, Rw, qnT, RqnT, knT, RknT, vtok, Rvtok):
            ch = grp * 4 + c
            bfm = lambda q: pp(f"bfm{l}", q, q + 1)
            pre, Rpre = yield from gget(HS)
            op("pool", "tensor_copy", [Rhalo[l]], [Rpre], out=pre[:, 0:3], in_=haloA[:, l, ch, 0:3])

            def ev(b, Rb):
                act(pre[:, 3:515], b[:, :], AF.Identity, [Rb, Rprm], [Rpre], bias=bfm(ch))
            yield from proj(wt, Rw, c * 128, 128, ev)
            op("pool", "tensor_copy", [Rpre], [Rhalo[l]], out=haloA[:, l, ch, 0:3], in_=pre[:, 512:515])
            dg, Rdg = yield from gget(DG)
            o_, _w = poff[f"aconv{l}"]
            for k in range(4):
                ts("pool", dg[:, k, :], ident_b, prm[:, o_ + ch * 4 + k:o_ + ch * 4 + k + 1], 0.0, ALU.mult, ALU.add,
                   [Rcst, Rprm], [Rdg])
            yield
            b, Rb = yield from gget(PS)
            for k in range(4):
                mm(b[:, :], dg[:, k, :], pre[:, k:k + 512], k == 0, k == 3, [Rdg, Rpre], [Rb])
            DG.put((dg, Rdg))
            HS.put((pre, Rpre))
            yield
            if grp == 2:
                vT, RvT = yield from gget(HS)
                act(vT[:, 0:512], b[:, :], AF.Silu, [Rb], [RvT])
                PS.put((b, Rb))
                yield
                b, Rb = yield from gget(PS)
                bb = b[:].bitcast(BF16)
                for s in range(4):
                    tr(bb[:, s * 128:(s + 1) * 128], vT[:, s * 128:(s + 1) * 128], ident_b, [RvT, Rcst], [Rb])
                HS.put((vT, RvT))
                yield
                act(vtok[:, :, c * 128:(c + 1) * 128], bb[:, 0:512].rearrange("p (s d) -> p s d", s=4), AF.Copy, [Rb], [Rvtok])
                PS.put((b, Rb))
                return
            cs, Rcs = yield from gget(FS)
            act(cs[:, :], b[:, :], AF.Silu, [Rb], [Rcs])
            PS.put((b, Rb))
            sq, Rsq = yield from gget(HS)
            act(sq[:, 0:512], cs[:, :], AF.Square, [Rcs], [Rsq])
            yield
            b, Rb = yield from gget(PS)
            mm(b[:, :], ones_b, sq[:, 0:512], True, True, [Rcst, Rsq], [Rb])
            HS.put((sq, Rsq))
            yield
            rs, Rrs = yield from gget(FS)
            act(rs[:, :], b[:, :], AF.Ln, [Rb], [Rrs], bias=EPS)
            PS.put((b, Rb))
            act(rs[:, :], rs[:, :], AF.Exp, [Rrs], [Rrs], scale=-0.5)
            dst, Rdst = (qnT, RqnT) if grp == 0 else (knT, RknT)
            stt(dst[:, c, :], cs[:, :], (128 ** -0.5) if grp == 0 else 1.0, rs[:, :], ALU.mult, ALU.mult,
                [Rcs, Rrs], [Rdst])
            FS.put((cs, Rcs))
            FS.put((rs, Rrs))
            yield

        def chain_A(l, ti):
            bfm = lambda q: pp(f"bfm{l}", q, q + 1)
            qnT, RqnT = yield from gget(GS)
            knT, RknT = yield from gget(GS)
            sza, Rsza = yield from gget(GS)
            ktok, Rktok = yield from gget(GS)
            vtok, Rvtok = yield from gget(GS)
            ws = WStream([winblk(l, 8), winblk(l, 0), winblk(l, 1), winblk(l, 2), winblk(l, 3)])
            wt, Rw = yield from ws.take()
            yield from stage_C_kv(l, ti, wt, Rw)
            bba, Rbba = yield from gget(PS)
            for s in range(4):
                for kc in range(8):
                    mm(bba[:, s * 8:(s + 1) * 8], hT[:, kc, s * 128:(s + 1) * 128], wt[:, kc, 256:264], kc == 0, kc == 7,
                       [Rw, RhT], [Rbba])
            WB.put((wt, Rw))
            kv_ready[(l, ti)] = True
            yield
            bg = sm[:, 16:48].rearrange("p (s e) -> p s e", s=4)
            Rbg = smr("bg")
            tt("dve", bg, bba[:, 0:32].rearrange("p (s e) -> p s e", s=4),
               pp(f"bba{l}").unsqueeze(1).broadcast_to([128, 4, 8]), ALU.add, [Rbba, Rprm], [Rbg])
            PS.put((bba, Rbba))
            for grp in range(3):
                wt, Rw = yield from ws.take()
                yield from pipeline([convA_item(l, grp, c, wt, Rw, qnT, RqnT, knT, RknT, vtok, Rvtok) for c in range(4)], W_CONVA)
                WB.put((wt, Rw))
            wt, Rw = yield from ws.take()
            for c in range(4):
                def ev(b, Rb, c=c):
                    act(sza[:, c, :], b[:, :], AF.Silu, [Rb, Rprm], [Rsza], bias=bfm(12 + c))
                yield from proj(wt, Rw, c * 128, 128, ev)
            WB.put((wt, Rw))
            bet = sm[:, 48:64].rearrange("p (s e) -> p s e", s=4)
            gg = sm[:, 64:80].rearrange("p (s e) -> p s e", s=4)
            act(bet, bg[:, :, 0:4], AF.Sigmoid, [Rbg], [smr("bet")])
            tt("dve", gg, bg[:, :, 4:8], pp(f"adt{l}").unsqueeze(1).broadcast_to([128, 4, 4]), ALU.add, [Rbg, Rprm], [smr("gg")])
            act(gg, gg, AF.Exp, [smr("gg")], [smr("gg")])
            act(gg, gg, AF.Ln, [smr("gg")], [smr("gg")], bias=1.0)
            tt("dve", gg, gg, nA[:, l, :].unsqueeze(1).broadcast_to([128, 4, 4]), ALU.mult, [smr("gg"), Rlc], [smr("gg")])
            for s in range(4):
                b, Rb = yield from gget(PS)
                bb = b[:].bitcast(BF16)
                for h in range(4):
                    tr(bb[:, h * 128:(h + 1) * 128], knT[:, h, s * 128:(s + 1) * 128], ident_b, [RknT, Rcst], [Rb])
                yield
                act(ktok[:, s, :], bb[:, 0:512], AF.Copy, [Rb], [Rktok])
                PS.put((b, Rb))
            marks.append(("A_prologue_end", ti, l, P.cnt["pe"]))
            for s in range(4):
                yield from gdn_chunk(l, ti, s, qnT, RqnT, knT, RknT, ktok, Rktok, vtok, Rvtok, sza, Rsza, bet, gg)
                marks.append((f"A_chunk{s}_end", ti, l, P.cnt["pe"]))
            for it in ((qnT, RqnT), (knT, RknT), (ktok, Rktok), (vtok, Rvtok), (sza, Rsza)):
                GS.put(it)

        def gdn_chunk(l, ti, s, qnT, RqnT, knT, RknT, ktok, Rktok, vtok, Rvtok, sza, Rsza, bet, gg):
            tsl = slice(s * 128, (s + 1) * 128)
            Rbet, Rgg = smr("bet"), smr("gg")
            gs = sm[:, 80:88]
            Rgs = smr("gs")
            H4 = lambda ap: ap.rearrange("p (h j) -> p h j", h=4)
            bc4 = lambda ap: ap.unsqueeze(2).broadcast_to([128, 4, 128])
            bcm = lambda ap: ap.unsqueeze(1).broadcast_to([128, 4, 128])
            hsl = lambda h: slice(h * 128, (h + 1) * 128)
            bG, RbG = yield from gget(PS)
            mm(bG[:, 0:4], pp("triu"), gg[:, s, :], True, True, [Rprm, Rgg], [RbG])
            mm(bG[:, 4:8], pp("ones"), gg[:, s, :], True, True, [Rprm, Rgg], [RbG])
            yield
            op("dve", "tensor_copy", [RbG], [Rgs], out=gs, in_=bG[:, 0:8])
            PS.put((bG, RbG))
            ex = sm[:, 88:104]
            Rex = smr("ex")
            act(ex[:, 0:4], gs[:, 0:4], AF.Exp, [Rgs], [Rex])
            tt("dve", ex[:, 0:4], ex[:, 0:4], bet[:, s, :], ALU.mult, [Rex, Rbet], [Rex])
            tt("dve", ex[:, 12:16], gs[:, 4:8], gs[:, 0:4], ALU.subtract, [Rgs], [Rex])
            act(ex[:, 4:8], ex[:, 12:16], AF.Exp, [Rex], [Rex])
            act(ex[:, 8:12], gs[:, 4:8], AF.Exp, [Rgs], [Rex])
            rg, Rrg = yield from gget(FS)
            tt("dve", H4(rg[:, :]), bcm(ident_f), bc4(gs[:, 0:4]), ALU.mult, [Rprm, Rgs], [Rrg])
            bGr, RbGr = yield from gget(PS)
            mm(bGr[:, :], pp("ones"), rg[:, :], True, True, [Rprm, Rrg], [RbGr])
            FS.put((rg, Rrg))
            bK, RbK = yield from gget(PS)
            bQ, RbQ = yield from gget(PS)
            for h in range(4):
                mm(bK[:, hsl(h)], knT[:, h, tsl], knT[:, h, tsl], True, True, [RknT], [RbK])
            for h in range(4):
                mm(bQ[:, hsl(h)], knT[:, h, tsl], qnT[:, h, tsl], True, True, [RknT, RqnT], [RbQ])
            yield
            dd, Rdd = yield from gget(FS)
            e1, Re1 = yield from gget(FS)
            e2, Re2 = yield from gget(FS)
            tt("dve", H4(dd[:, :]), H4(bGr[:, :]), bc4(gs[:, 0:4]), ALU.subtract, [RbGr, Rgs], [Rdd])
            tt("dve", H4(e2[:, :]), H4(dd[:, :]), bcm(pp("masku")), ALU.add, [Rdd, Rprm], [Re2])
            act(e2[:, :], e2[:, :], AF.Exp, [Re2], [Re2])
            tt("dve", H4(e1[:, :]), H4(dd[:, :]), bcm(pp("masklneg")), ALU.subtract, [Rdd, Rprm], [Re1])
            act(e1[:, :], e1[:, :], AF.Exp, [Re1], [Re1], scale=-1.0)
            act(dd[:, :], bGr[:, :], AF.Exp, [RbGr], [Rdd])
            PS.put((bGr, RbGr))
            qd, Rqd = yield from gget(HS)
            tt("dve", H4(qd[:, 0:512]), qnT[:, :, tsl], H4(dd[:, :]), ALU.mult, [RqnT, Rdd], [Rqd])
            yield
            tt("dve", e1[:, :], bK[:, :], e1[:, :], ALU.mult, [RbK, Re1], [Re1])
            PS.put((bK, RbK))
            tt("dve", H4(dd[:, :]), H4(e1[:, :]), bc4(bet[:, s, :]), ALU.mult, [Re1, Rbet], [Rdd])
            at, Rat = yield from gget(HS)
            tt("dve", at[:, 0:512], bQ[:, :], e2[:, :], ALU.mult, [RbQ, Re2], [Rat])
            PS.put((bQ, RbQ))
            FS.put((e1, Re1))
            FS.put((e2, Re2))
            ad, Rad = yield from gget(FR)
            ao, Rao = yield from gget(FR)
            tt("dve", H4(ad[:, :].bitcast(F32R)), H4(dd[:, :]), bcm(pp("bd01")), ALU.mult, [Rdd, Rprm], [Rad])
            tt("dve", H4(ao[:, :].bitcast(F32R)), H4(dd[:, :]), bcm(pp("off01")), ALU.mult, [Rdd, Rprm], [Rao])
            FS.put((dd, Rdd))
            bT, RbT = yield from gget(PS)
            for h in range(4):
                tr(bT[:, hsl(h)], ad[:, hsl(h)], ident_f, [Rad, Rprm], [RbT])
            yield
            bm, Rbm = yield from gget(FR)
            pm, Rpm = yield from gget(FR)
            op("dve", "tensor_copy", [RbT], [Rbm], out=bm[:, :].bitcast(F32R), in_=bT[:, :])
            tt("dve", H4(pm[:, :].bitcast(F32R)), bcm(ident_f), H4(bT[:, :]), ALU.subtract, [Rprm, RbT], [Rpm])
            PS.put((bT, RbT))
            adr, bmr, pmr = ad[:, :].bitcast(F32R), bm[:, :].bitcast(F32R), pm[:, :].bitcast(F32R)
            bA, RbA = yield from gget(PS)
            bB, RbB = yield from gget(PS)
            bP, RbP = yield from gget(PS)
            for k in range(1, 6):
                for h in range(4):
                    mm(bA[:, hsl(h)], bmr[:, hsl(h)], adr[:, hsl(h)], True, True, [Rbm, Rad], [RbA])
                if k < 5:
                    for h in range(4):
                        mm(bB[:, hsl(h)], adr[:, hsl(h)], bmr[:, hsl(h)], True, True, [Rbm, Rad], [RbB])
                yield
                op("dve", "tensor_copy", [RbA], [Rad], out=adr, in_=bA[:, :])
                if k < 5:
                    op("dve", "tensor_copy", [RbB], [Rbm], out=bmr, in_=bB[:, :])
                for h in range(4):
                    mm(bP[:, hsl(h)], adr[:, hsl(h)], pmr[:, hsl(h)], True, True, [Rad, Rpm], [RbP])
                yield
                tt("dve", pmr, pm[:, :], bP[:, :], ALU.add, [Rpm, RbP], [Rpm])
            FR.put((ad, Rad))
            for h in range(4):
                tr(bA[:, hsl(h)], pm[:, hsl(h)], ident_f, [Rpm, Rprm], [RbA])
            for h in range(4):
                mm(bB[:, hsl(h)], ao[:, hsl(h)].bitcast(F32R), pmr[:, hsl(h)], True, True, [Rao, Rpm], [RbB])
            yield
            op("dve", "tensor_copy", [RbA], [Rbm], out=bmr, in_=bA[:, :])
            ym, Rym = yield from gget(FR)
            op("dve", "tensor_copy", [RbB], [Rym], out=ym[:, :].bitcast(F32R), in_=bB[:, :])
            FR.put((ao, Rao))
            for h in range(4):
                mm(bP[:, hsl(h)], bmr[:, hsl(h)], ym[:, hsl(h)].bitcast(F32R), True, True, [Rbm, Rym], [RbP])
            yield
            ttm, Rttm = yield from gget(HS)
            tt("dve", ttm[:, 0:512], pm[:, :], bP[:, :], ALU.subtract, [Rpm, RbP], [Rttm])
            for it in ((bm, Rbm), (pm, Rpm), (ym, Rym)):
                FR.put(it)
            rv, Rrv = yield from gget(HS)
            rk, Rrk = yield from gget(HS)
            kd, Rkd = yield from gget(HS)
            tt("pool", H4(rv[:, 0:512]), H4(vtok[:, s, :]), bc4(bet[:, s, :]), ALU.mult, [Rvtok, Rbet], [Rrv])
            tt("pool", H4(rk[:, 0:512]), H4(ktok[:, s, :]), bc4(ex[:, 0:4]), ALU.mult, [Rktok, Rex], [Rrk])
            tt("pool", H4(kd[:, 0:512]), H4(ktok[:, s, :]), bc4(ex[:, 4:8]), ALU.mult, [Rktok, Rex], [Rkd])
            for h in range(4):
                mm(bA[:, hsl(h)], ttm[:, hsl(h)], rv[:, hsl(h)], True, True, [Rttm, Rrv], [RbA])
            for h in range(4):
                mm(bB[:, hsl(h)], rk[:, hsl(h)], ttm[:, hsl(h)], True, True, [Rttm, Rrk], [RbB])
            yield
            u, Ru = yield from gget(FS)
            wT, RwT = yield from gget(HS)
            act(u[:, :], bA[:, :], AF.Copy, [RbA], [Ru])
            act(wT[:, 0:512], bB[:, :], AF.Copy, [RbB], [RwT])
            for it in ((ttm, Rttm), (rv, Rrv), (rk, Rrk)):
                HS.put(it)
            for h in range(4):
                mm(bP[:, hsl(h)], wT[:, hsl(h)], Sbf[:, l, h, :], True, True, [RwT, RS[l][h]], [RbP])
            yield
            vn, Rvn = yield from gget(HS)
            tt("dve", vn[:, 0:512], u[:, :], bP[:, :], ALU.subtract, [Ru, RbP], [Rvn])
            FS.put((u, Ru))
            for h in range(4):
                mm(bA[:, hsl(h)], qd[:, hsl(h)], Sbf[:, l, h, :], True, False, [Rqd, RS[l][h]], [RbA])
                mm(bA[:, hsl(h)], at[:, hsl(h)], vn[:, hsl(h)], False, True, [Rat, Rvn], [RbA])
            for h in range(4):
                mm(bB[:, hsl(h)], kd[:, hsl(h)], vn[:, hsl(h)], True, True, [Rkd, Rvn], [RbB])
            yield
            Sall = Sst[:, l].rearrange("p h d -> p (h d)")
            tt("dve", H4(Sall), H4(Sall), bc4(ex[:, 8:12]), ALU.mult, [Rex] + RS[l], RS[l])
            tt("dve", Sall, Sall, bB[:, :], ALU.add, RS[l] + [RbB], RS[l])
            act(Sbf[:, l].rearrange("p h d -> p (h d)"), Sall, AF.Copy, RS[l], RS[l])
            PS.put((bB, RbB))
            PS.put((bP, RbP))
            for it in ((wT, RwT), (vn, Rvn), (qd, Rqd), (at, Rat), (kd, Rkd)):
                HS.put(it)
            ssq = sm[:, 104:108]
            Rssq = smr("ssq")
            sq, Rsq = yield from gget(FS)
            act(sq[:, :], bA[:, :], AF.Square, [RbA], [Rsq])
            op("dve", "tensor_reduce", [Rsq], [Rssq], out=ssq, in_=H4(sq[:, :]), axis=AX.X, op=ALU.add)
            FS.put((sq, Rsq))
            act(ssq, ssq, AF.Ln, [Rssq], [Rssq], scale=1.0 / 128, bias=EPS)
            act(ssq, ssq, AF.Exp, [Rssq], [Rssq], scale=-0.5)
            on, Ron = yield from gget(HS)
            tt("dve", H4(on[:, 0:512]), H4(bA[:, :]), bc4(ssq), ALU.mult, [RbA, Rssq], [Ron])
            PS.put((bA, RbA))
            b, Rb = yield from gget(PS)
            bb = b[:].bitcast(BF16)
            for h in range(4):
                tr(bb[:, hsl(h)], on[:, hsl(h)], ident_b, [Ron, Rcst], [Rb])
            HS.put((on, Ron))
            yield
            stt(yT[:, 0, :, tsl], H4(bb[:, 0:512]), pp(f"anorm{l}"), sza[:, :, tsl], ALU.mult, ALU.mult,
                [Rb, Rprm, Rsza], [RyT[0]])
            PS.put((b, Rb))

        kv_ready = {}

        def attn_head(h, hh, hd, bO, RbO, Rq, Rk, Rv, stc):
            Rst = smr(f"stc{h}")
            nk = hh["nk"]
            bS, RbS = yield from gget(PS)
            mm(bS[:, 0:nk], hh["q"], hh["k"], True, True, [Rq, Rk], [RbS])
            yield
            mx, m_, negm, rsum, es, rden = (stc[:, i, h:h + 1] for i in range(6))
            p, Rp = yield from gget(HS)
            if hh["bias"] is not None:
                sc, Rsc = yield from gget(FS)
                stt(sc[:, 0:nk], bS[:, 0:nk], hh["scale"], hh["bias"], ALU.mult, ALU.add, [RbS, Rprm], [Rsc])
                PS.put((bS, RbS))
                if hh.get("premask") is not None:
                    ts("dve", sc[:, 0:128], sc[:, 0:128], hh["premask"], None, ALU.add, None, [Rsc, Rprm], [Rsc])
                if FINE:
                    yield
                op("dve", "tensor_reduce", [Rsc], [Rst], out=mx, in_=sc[:, 0:nk], axis=AX.X, op=ALU.max)
                if FINE:
                    yield
                tt("dve", m_, mx, hh["sink"], ALU.max, [Rst, Rprm], [Rst])
                if FINE:
                    yield
                ts("dve", negm, m_, -1.0, None, ALU.mult, None, [Rst], [Rst])
                if FINE:
                    yield
                act(p[:, 0:nk], sc[:, 0:nk], AF.Exp, [Rsc, Rst], [Rp, Rst], bias=negm, accum_out=rsum)
                FS.put((sc, Rsc))
                act(es, hh["sink"], AF.Exp, [Rprm, Rst], [Rst], bias=negm)
                if FINE:
                    yield
                tt("dve", rden, rsum, es, ALU.add, [Rst], [Rst])
                if FINE:
                    yield
            else:
                op("dve", "tensor_reduce", [RbS], [Rst], out=mx, in_=bS[:, 0:nk], axis=AX.X, op=ALU.max)
                if FINE:
                    yield
                ts("dve", negm, mx, -hh["scale"], None, ALU.mult, None, [Rst], [Rst])
                if FINE:
                    yield
                act(p[:, 0:nk], bS[:, 0:nk], AF.Exp, [RbS, Rst], [Rp, Rst], bias=negm, scale=hh["scale"], accum_out=rden)
                PS.put((bS, RbS))
                if FINE:
                    yield
            op("dve", "reciprocal", [Rst], [Rst], out=rden, in_=rden)
            yield
            nkc = nk // 128
            bT, RbT = yield from gget(PS)
            bTb = bT[:].bitcast(BF16)
            for kc in range(nkc):
                tr(bTb[:, kc * 128:(kc + 1) * 128], p[:, kc * 128:(kc + 1) * 128], ident_b, [Rp, Rcst], [RbT])
            HS.put((p, Rp))
            yield
            pT, RpT = yield from gget(HS)
            act(pT[:, 0:nk], bTb[:, 0:nk], AF.Copy, [RbT], [RpT])
            PS.put((bT, RbT))
            for kc in range(nkc):
                mm(bO[:, h * hd:(h + 1) * hd], pT[:, kc * 128:(kc + 1) * 128], hh["v"][kc], kc == 0, kc == nkc - 1,
                   [RpT, Rv], [RbO])
            HS.put((pT, RpT))
            yield

        def attention(heads, hd, Rq, Rk, Rv, ydst, Ry, sz, Rsz, tsl):
            nh = len(heads)
            bO, RbO = yield from gget(PS)
            stc = sm[:, 112:112 + 6 * 8].rearrange("p (k h) -> p k h", k=6)
            yield from pipeline([attn_head(h, hh, hd, bO, RbO, Rq, Rk, Rv, stc) for h, hh in enumerate(heads)], W_ATT)
            on, Ron = yield from gget(HS)
            rd = stc[:, 5, 0:nh]
            tt("dve", on[:, 0:512].rearrange("p (h d) -> p h d", h=nh), bO[:, :].rearrange("p (h d) -> p h d", h=nh),
               rd.unsqueeze(2).broadcast_to([128, nh, hd]), ALU.mult, [RbO] + [smr(f"stc{h}") for h in range(nh)], [Ron])
            PS.put((bO, RbO))
            b, Rb = yield from gget(PS)
            bb = b[:].bitcast(BF16)
            for c in range(4):
                tr(bb[:, c * 128:(c + 1) * 128], on[:, c * 128:(c + 1) * 128], ident_b, [Ron, Rcst], [Rb])
            HS.put((on, Ron))
            yield
            for c in range(4):
                tt("dve", ydst[:, c, tsl], bb[:, c * 128:(c + 1) * 128], sz[:, c, tsl], ALU.mult, [Rb, Rsz], [Ry])
            PS.put((b, Rb))

        def stage_C_kv(l, ti, wt, Rw):
            def ev(b, Rb):
                act(kTc[:, l, 128:640], b[:, :], AF.Identity, [Rb, Rprm], [Rkv[l]], bias=pp(f"bfm{l}", 32, 33))
            yield from proj(wt, Rw, 0, 128, ev)
            for s in range(4):
                b, Rb = yield from gget(PS)
                for kc in range(8):
                    mm(b[:, 0:128], hT[:, kc, s * 128:(s + 1) * 128], wt[:, kc, 128:256], kc == 0, kc == 7, [Rw, RhT], [Rb])
                yield
                tt("dve", vtc[:, l, 1 + s, :], b[:, 0:128], pp(f"bv{l}"), ALU.add, [Rb, Rprm], [Rkv[l]])
                PS.put((b, Rb))

        def gated_fm(l, ws, chbase, dst, Rdst, func):
            wt, Rw = yield from ws.take()
            for c in range(4):
                def ev(b, Rb, c=c):
                    act(dst[:, c, :], b[:, :], func, [Rb, Rprm], [Rdst], bias=pp(f"bfm{l}", chbase + c, chbase + c + 1))
                yield from proj(wt, Rw, c * 128, 128, ev)
            WB.put((wt, Rw))

        def stage_C(l, ti, ws):
            qT, RqT = yield from gget(GS)
            szc, Rszc = yield from gget(GS)
            yield from gated_fm(l, ws, 28, qT, RqT, AF.Identity)
            yield from gated_fm(l, ws, 36, szc, Rszc, AF.Silu)
            while not kv_ready.get((l, ti)):
                yield
            so, _sw = poff[f"sinks{l}"]
            for s in range(4):
                first = (ti == 0 and s == 0)
                heads = []
                for h in range(8):
                    c, base = h % 4, (h // 4) * 64
                    kvh = h // 4
                    if first:
                        k_ap, nk, bias = kTc[base:base + 64, l, 128:256], 128, bandv[:, h, 128:256]
                        v = [vtc[:, l, 1, kvh * 64:(kvh + 1) * 64]]
                    else:
                        k_ap, nk, bias = kTc[base:base + 64, l, s * 128:s * 128 + 256], 256, bandv[:, h, :]
                        v = [vtc[:, l, s, kvh * 64:(kvh + 1) * 64], vtc[:, l, s + 1, kvh * 64:(kvh + 1) * 64]]
                    pm_ = None
                    if pipe and ti == 1 and s == 0:
                        fo_, _ = poff["flags"]
                        pm_ = prm[:, fo_ + 2:fo_ + 3]
                    heads.append(dict(q=qT[base:base + 64, c, s * 128:(s + 1) * 128], k=k_ap, nk=nk, bias=bias,
                                      scale=0.125, sink=prm[:, so + h:so + h + 1], v=v, premask=pm_))
                yield from attention(heads, 64, RqT, Rkv[l], Rkv[l], yT[:, 2], RyT[2], szc, Rszc, slice(s * 128, (s + 1) * 128))
            op("pool", "tensor_copy", [Rkv[l]], [Rkv[l]], out=kTc[:, l, 0:128], in_=kTc[:, l, 512:640])
            op("pool", "tensor_copy", [Rkv[l]], [Rkv[l]], out=vtc[:, l, 0, :], in_=vtc[:, l, 4, :])
            GS.put((qT, RqT))
            GS.put((szc, Rszc))

        def stage_E(l, ti, ws):
            eq, Req = yield from gget(GS)
            sze, Rsze = yield from gget(GS)
            yield from gated_fm(l, ws, 48, eq, Req, AF.Identity)
            yield from gated_fm(l, ws, 52, sze, Rsze, AF.Silu)
            while not mem_ready.get(l):
                yield
            for s in range(4):
                tsl = slice(s * 128, (s + 1) * 128)
                heads = [dict(q=eq[:, h, tsl], k=mkT[:, l, h, :], nk=256, bias=None, scale=128 ** -0.5, sink=None,
                              v=[mvt[:, l, 0, h * 128:(h + 1) * 128], mvt[:, l, 1, h * 128:(h + 1) * 128]])
                         for h in range(4)]
                yield from attention(heads, 128, Req, Rmem[l], Rmem[l], yT[:, 4], RyT[4], sze, Rsze, tsl)
            GS.put((eq, Req))
            GS.put((sze, Rsze))

        def convB_item(l, c, wa, Rwa, wb, Rwb, cvv, Rcv):
            bwo, _ = poff[f"bdw{l}"]
            sg, Rsg = yield from gget(HS)

            def evb(b, Rb):
                act(sg[:, 0:512], b[:, :], AF.Sigmoid, [Rb, Rprm], [Rsg], bias=pp(f"bfm{l}", 20 + c, 21 + c))
            yield from proj(wb, Rwb, c * 128, 128, evb)
            pre, Rpre = yield from gget(HS)
            op("pool", "tensor_copy", [Rhalo[l]], [Rpre], out=pre[:, 0:30], in_=haloB[:, l, c, 0:30])

            def eva(b, Rb):
                stt(pre[:, 30:542], b[:, :], pp(f"bfm{l}", 16 + c, 17 + c), sg[:, 0:512], ALU.add, ALU.mult,
                    [Rb, Rprm, Rsg], [Rpre])
            yield from proj(wa, Rwa, c * 128, 128, eva)
            HS.put((sg, Rsg))
            op("pool", "tensor_copy", [Rpre], [Rhalo[l]], out=haloB[:, l, c, 0:30], in_=pre[:, 512:542])
            b, Rb = yield from gget(PS)
            for k in range(31):
                dgk, Rdgk = yield from gget(S16)
                ts("pool", dgk[:, :], ident_b, prm[:, bwo + c * 31 + k:bwo + c * 31 + k + 1], 0.0, ALU.mult, ALU.add,
                   [Rcst, Rprm], [Rdgk])
                mm(b[:, :], dgk[:, :], pre[:, k:k + 512], k == 0, k == 30, [Rdgk, Rpre], [Rb])
                S16.put((dgk, Rdgk))
                if k % 4 == 3:
                    yield
            HS.put((pre, Rpre))
            yield
            act(cvv[c], b[:, :], AF.Identity, [Rb, Rprm], [Rcv[c]], bias=pp(f"bdwb{l}", c, c + 1))
            PS.put((b, Rb))

        def stage_B(l, ti, ws):
            szb, Rszb = yield from gget(GS)
            g0, Rg0 = yield from gget(GS)
            g1, Rg1 = yield from gget(GS)
            yield from gated_fm(l, ws, 24, szb, Rszb, AF.Silu)
            wa, Rwa = yield from ws.take()
            wb, Rwb = yield from ws.take()
            g0f = g0[:].rearrange("p a b -> p (a b)").bitcast(F32)
            g1f = g1[:].rearrange("p a b -> p (a b)").bitcast(F32)
            cvv = [g0f[:, 0:512], g0f[:, 512:1024], g1f[:, 0:512], g1f[:, 512:1024]]
            Rcv = [Rg0, Rg0, Rg1, Rg1]
            yield from pipeline([convB_item(l, c, wa, Rwa, wb, Rwb, cvv, Rcv) for c in range(4)], W_B)
            WB.put((wa, Rwa))
            WB.put((wb, Rwb))
            bM, RbM = yield from gget(PS)
            bQ, RbQ = yield from gget(PS)
            for c in range(4):
                mm(bM[:, :], pp("ones"), cvv[c], c == 0, c == 3, [Rprm, Rcv[c]], [RbM])
            for c in range(4):
                sq, Rsq = yield from gget(FS)
                act(sq[:, :], cvv[c], AF.Square, [Rcv[c]], [Rsq])
                mm(bQ[:, :], pp("ones"), sq[:, :], c == 0, c == 3, [Rprm, Rsq], [RbQ])
                FS.put((sq, Rsq))
                yield
            mean, Rmean = yield from gget(FS)
            rstd, Rrstd = yield from gget(FS)
            act(mean[:, :], bM[:, :], AF.Copy, [RbM], [Rmean], scale=1.0 / 512)
            act(rstd[:, :], bM[:, :], AF.Square, [RbM], [Rrstd], scale=1.0 / 512)
            PS.put((bM, RbM))
            stt(rstd[:, :], bQ[:, :], 1.0 / 512, rstd[:, :], ALU.mult, ALU.subtract, [RbQ, Rrstd], [Rrstd])
            PS.put((bQ, RbQ))
            ts("dve", rstd[:, :], rstd[:, :], 0.0, EPS, ALU.max, ALU.add, [Rrstd], [Rrstd])
            act(rstd[:, :], rstd[:, :], AF.Ln, [Rrstd], [Rrstd])
            act(rstd[:, :], rstd[:, :], AF.Exp, [Rrstd], [Rrstd], scale=-0.5)
            yield
            for c in range(4):
                tt("dve", cvv[c], cvv[c], mean[:, :], ALU.subtract, [Rcv[c], Rmean], [Rcv[c]])
                tt("pool", cvv[c], cvv[c], rstd[:, :], ALU.mult, [Rcv[c], Rrstd], [Rcv[c]])
                bn, Rbn = yield from gget(HS)
                act(bn[:, 0:512], cvv[c], AF.Silu, [Rcv[c], Rprm], [Rbn], scale=pp(f"blng{l}", c, c + 1),
                    bias=pp(f"blnb{l}", c, c + 1))
                tt("dve", yT[:, 1, c, :], bn[:, 0:512], szb[:, c, :], ALU.mult, [Rbn, Rszb], [RyT[1]])
                HS.put((bn, Rbn))
                yield
            FS.put((mean, Rmean))
            FS.put((rstd, Rrstd))
            for it in ((szb, Rszb), (g0, Rg0), (g1, Rg1)):
                GS.put(it)

        def lru_item(l, c, wt, Rw, szd, Rszd):
            dwo, _ = poff[f"dconv{l}"]
            pre, Rpre = yield from gget(HS)
            op("pool", "tensor_copy", [Rhalo[l]], [Rpre], out=pre[:, 0:3], in_=haloD[:, l, c, 0:3])

            def ev(b, Rb):
                act(pre[:, 3:515], b[:, :], AF.Identity, [Rb, Rprm], [Rpre], bias=pp(f"bfm{l}", 40 + c, 41 + c))
            yield from proj(wt, Rw, c * 128, 128, ev)
            op("pool", "tensor_copy", [Rpre], [Rhalo[l]], out=haloD[:, l, c, 0:3], in_=pre[:, 512:515])
            dg, Rdg = yield from gget(DG)
            for k in range(4):
                ts("pool", dg[:, k, :], ident_b, prm[:, dwo + c * 4 + k:dwo + c * 4 + k + 1], 0.0, ALU.mult, ALU.add,
                   [Rcst, Rprm], [Rdg])
            yield
            b, Rb = yield from gget(PS)
            for k in range(4):
                mm(b[:, :], dg[:, k, :], pre[:, k:k + 512], k == 0, k == 3, [Rdg, Rpre], [Rb])
            DG.put((dg, Rdg))
            HS.put((pre, Rpre))
            yield
            while len(FS.free) < 4:
                yield
            dx, Rdx = FS.get()
            r, Rr = FS.get()
            ig, Rig = FS.get()
            a, Ra = FS.get()
            dxb, Rdxb = yield from gget(HS)
            act(dx[:, :], b[:, :], AF.Identity, [Rb, Rprm], [Rdx], bias=pp(f"dconvb{l}", c, c + 1))
            PS.put((b, Rb))
            op("pool", "tensor_copy", [Rdx], [Rdxb], out=dxb[:, 0:512], in_=dx[:, :])
            yield
            bR, RbR = yield from gget(PS)
            mm(bR[:, :], bdws[:, l, 0, c, :], dxb[:, 0:512], True, True, [Rbdw, Rdxb], [RbR])
            bI, RbI = yield from gget(PS)
            mm(bI[:, :], bdws[:, l, 1, c, :], dxb[:, 0:512], True, True, [Rbdw, Rdxb], [RbI])
            HS.put((dxb, Rdxb))
            yield
            act(r[:, :], bR[:, :], AF.Sigmoid, [RbR, Rprm], [Rr], bias=pp(f"dba{l}", c, c + 1))
            act(ig[:, :], bI[:, :], AF.Sigmoid, [RbI, Rprm], [Rig], bias=pp(f"dbx{l}", c, c + 1))
            PS.put((bR, RbR))
            PS.put((bI, RbI))
            act(a[:, :], r[:, :], AF.Exp, [Rr, Rlc], [Ra], scale=lruc[:, l, c:c + 1])
            act(r[:, :], r[:, :], AF.Exp, [Rr, Rlc], [Rr], scale=lruc[:, l, 4 + c:5 + c])
            act(r[:, :], r[:, :], AF.Sqrt, [Rr], [Rr], scale=-1.0, bias=1.0)
            tt("dve", ig[:, :], ig[:, :], dx[:, :], ALU.mult, [Rig, Rdx], [Rig])
            tt("pool", ig[:, :], ig[:, :], r[:, :], ALU.mult, [Rig, Rr], [Rig])
            FS.put((dx, Rdx))
            yield
            op("dve", "tensor_tensor_scan", [Ra, Rig, Rhst[l]], [Rr], out=r[:, :], data0=a[:, :], data1=ig[:, :],
               initial=hst[:, l, c:c + 1], op0=ALU.mult, op1=ALU.add)
            op("dve", "tensor_copy", [Rr], [Rhst[l]], out=hst[:, l, c:c + 1], in_=r[:, 511:512])
            tt("dve", yT[:, 3, c, :], r[:, :], szd[:, c, :], ALU.mult, [Rr, Rszd], [RyT[3]])
            for it in ((r, Rr), (ig, Rig), (a, Ra)):
                FS.put(it)
            yield

        def stage_D(l, ti, ws):
            szd, Rszd = yield from gget(GS)
            yield from gated_fm(l, ws, 44, szd, Rszd, AF.Silu)
            wt, Rw = yield from ws.take()
            for c in range(4):
                yield from lru_item(l, c, wt, Rw, szd, Rszd)
            WB.put((wt, Rw))
            GS.put((szd, Rszd))

        def chain_rest(l, ti):
            ws = WStream([winblk(l, 12), winblk(l, 13),
                          winblk(l, 7), winblk(l, 9)])
            ws.prefetch()
            yield from stage_E(l, ti, ws)
            marks.append(("E_end", ti, l, P.cnt["pe"]))
            yield from stage_C(l, ti, ws)
            marks.append(("C_end", ti, l, P.cnt["pe"]))

        def chain_bd(l, ti):
            ws = WStream([winblk(l, 11), winblk(l, 10),
                          winblk(l, 6), winblk(l, 4), winblk(l, 5)])
            yield from stage_D(l, ti, ws)
            marks.append(("D_end", ti, l, P.cnt["pe"]))
            yield from stage_B(l, ti, ws)
            marks.append(("B_end", ti, l, P.cnt["pe"]))

        def merge_item(l, n, j, jj, wg, Rwg, wr, Rwr, mgf, Rmgs):
            gsb, Rgsb = yield from gget(HS)

            def ev(b, Rb):
                act(gsb[:, 0:512], b[:, :], AF.Sigmoid, [Rb, Rprm], [Rgsb], bias=pp(f"bfm{l}", 56 + n * 8 + j, 57 + n * 8 + j))
            yield from proj(wg, Rwg, jj * 128, 128, ev)
            b, Rb = yield from gget(PS)
            for kc in range(4):
                mm(b[:, :], wr[:, kc, jj * 128:(jj + 1) * 128], yT[:, n, kc, :], kc == 0, kc == 3, [Rwr, RyT[n]], [Rb])
            yield
            if n == 0:
                tt("dve", mgf[j], b[:, :], gsb[:, 0:512], ALU.mult, [Rb, Rgsb], [Rmgs[j]])
            else:
                tmp, Rtmp = yield from gget(FS)
                tt("dve", tmp[:, :], b[:, :], gsb[:, 0:512], ALU.mult, [Rb, Rgsb], [Rtmp])
                tt("pool", mgf[j], mgf[j], tmp[:, :], ALU.add, [Rmgs[j], Rtmp], [Rmgs[j]])
                FS.put((tmp, Rtmp))
            PS.put((b, Rb))
            HS.put((gsb, Rgsb))
            if n == 4:
                act(mgb[:, j, :], mgf[j], AF.Copy, [Rmgs[j]], [Rmgb])
            yield

        def stage_merge(l, ti):
            mgs = []
            for i in range(4):
                mgs.append((yield from gget(GS)))
            mgf, Rmgs = [], []
            for i in range(4):
                f = mgs[i][0][:].rearrange("p a b -> p (a b)").bitcast(F32)
                mgf += [f[:, 0:512], f[:, 512:1024]]
                Rmgs += [Reg(f"mgs{2 * i}"), Reg(f"mgs{2 * i + 1}")]
                for r_ in Rmgs[-2:]:
                    r_.w, r_.r = mgs[i][1].w, list(mgs[i][1].r)
            ws = WStream([winblk(l, 14 + q) for q in range(10)])
            ws.prefetch()
            for n in range(5):
                for half in range(2):
                    wr, Rwr = yield from gget(GS)
                    kb = ("b", l * 10 + n * 2 + half)
                    if kb not in scr_seen:
                        scr_seen.add(kb)
                        Rscr[kb] = Reg(f"scrb{kb[1]}")
                        dma("pool", wr[:, :, :], wbr_d[l, n][:, half * 512:(half + 1) * 512].rearrange("(kc p) d -> p kc d", p=128),
                            W=[Rwr])
                        dma("sp", scrb_d[kb[1]], wr[:].rearrange("p a b -> p (a b)"), R=[Rwr], W=[Rscr[kb]])
                    else:
                        dma("sp", wr[:].rearrange("p a b -> p (a b)"), scrb_d[kb[1]], R=[Rscr[kb]], W=[Rwr])
                    wg, Rwg = yield from ws.take()
                    yield from pipeline([merge_item(l, n, half * 4 + jj, jj, wg, Rwg, wr, Rwr, mgf, Rmgs) for jj in range(4)], W_MRG)
                    WB.put((wg, Rwg))
                    GS.put((wr, Rwr))
            for i in range(4):
                R_ = mgs[i][1]
                R_.w = Rmgs[2 * i + 1].w
                R_.r = list(Rmgs[2 * i].r) + list(Rmgs[2 * i + 1].r) + ([Rmgs[2 * i].w] if Rmgs[2 * i].w else [])
                GS.put(mgs[i])

        def out_item(l, s, wo, gp):
            inv = sm[:, 0:4]
            Rinv = smr("inv")
            bs = []
            for half in range(2):
                b, Rb = yield from gget(PS)
                wt, Rw = wo[half]
                for kc in range(8):
                    mm(b[:, :], mgb[:, kc, s * 128:(s + 1) * 128], wt[:, kc, :], kc == 0, kc == 7, [Rmgb, Rw], [Rb])
                bs.append((b, Rb))
                yield
            ss = sm[:, 160 + 4 * s:164 + 4 * s]
            Rss = smr(f"oss{s}")
            for half in range(2):
                jt, Rj = yield from gget(HS)
                act(jt[:, 0:512], bs[half][0][:, :], AF.Square, [bs[half][1]], [Rj, Rss], accum_out=ss[:, half:half + 1])
                HS.put((jt, Rj))
            tt("dve", ss[:, 2:3], ss[:, 0:1], ss[:, 1:2], ALU.add, [Rss], [Rss])
            act(ss[:, 2:3], ss[:, 2:3], AF.Ln, [Rss], [Rss], scale=1.0 / D_MODEL, bias=EPS)
            act(ss[:, 3:4], ss[:, 2:3], AF.Exp, [Rss], [Rss], scale=-0.5)
            yield
            for half in range(2):
                b, Rb = bs[half]
                tmp, Rtmp = yield from gget(FS)
                stt(tmp[:, :], b[:, :], ss[:, 3:4], gp[half][0][:, :], ALU.mult, ALU.mult, [Rb, Rss, gp[half][1]], [Rtmp])
                PS.put((b, Rb))
                xs = xt[:, s, half * 512:(half + 1) * 512]
                stt(xs, xs, inv[:, s:s + 1], tmp[:, :], ALU.mult, ALU.add, [Rxts[s], Rinv, Rtmp], [Rxts[s]])
                FS.put((tmp, Rtmp))
                yield

        def stage_out(l, ti):
            gp = []
            for half in range(2):
                g_, Rg_ = yield from gget(FS)
                dma("sp", g_[:, :], gpost_d[l:l + 1, half * 512:(half + 1) * 512].broadcast_to([128, 512]), W=[Rg_])
                gp.append((g_, Rg_))
            ws = WStream([(wout_d[l][:, 0:512], 512, l * 26 + 24), (wout_d[l][:, 512:1024], 512, l * 26 + 25)])
            wo = []
            for half in range(2):
                wo.append((yield from ws.take()))
            for s in range(4):
                Rxts[s].w, Rxts[s].r = Rxt.w, list(Rxt.r)
            yield from pipeline([out_item(l, s, wo, gp) for s in range(4)], 2)
            Rxt.w = Rxts[3].w
            Rxt.r = [t for s in range(4) for t in Rxts[s].r] + [Rxts[s].w for s in range(3)]
            for it in wo:
                WB.put(it)
            for it in gp:
                FS.put(it)

        Rxts = [Reg(f"xts{s}") for s in range(4)]
        marks = []

        out_toks = []

        def one_layer(l, ti, extra=()):
            marks.append(("norm", ti, l, P.cnt["pe"]))
            run(norm_T(xt, Rxt, 4, f"gpre{l}", hT, RhT, sm[:, 0:4], smr("inv")))
            marks.append(("branches", ti, l, P.cnt["pe"]))
            run(par(chain_A(l, ti), chain_rest(l, ti), chain_bd(l, ti), *extra))
            marks.append(("merge", ti, l, P.cnt["pe"]))
            run(stage_merge(l, ti))
            marks.append(("out", ti, l, P.cnt["pe"]))
            run(stage_out(l, ti))

        if not pipe:
            for ti in range(NT):
                dma("sp", xt[:], x_d[ti * 512:(ti + 1) * 512, :].rearrange("(s p) d -> p s d", p=128), W=[Rxt])
                for l in range(L):
                    one_layer(l, ti)
                    if ti == NT - 1 and l == 0 and "yT" in tap_d:
                        dma("pool", tap_d["yT"], yT[:], R=RyT)
                out_toks.append(dma("sp", out_d[ti * 512:(ti + 1) * 512, :].rearrange("(s p) d -> p s d", p=128), xt[:], R=[Rxt]))
        else:
            assert L == 1
            send_d = nc.dram_tensor("pp_send", [512, D_MODEL], F32)
            recv_d = nc.dram_tensor("pp_recv", [1024, D_MODEL], F32)
            Rsend, Rrecv = Reg("pp_send"), Reg("pp_recv")
            fo, _fw = poff["flags"]
            fA, fB = prm[:, fo:fo + 1], prm[:, fo + 1:fo + 2]
            klo, khi = prm[:, fo + 4:fo + 5], prm[:, fo + 5:fo + 6]
            gsl = None
            for step in range(NT + 1):
                ti_in = min(step, NT - 1)
                xsrc = x_d[ti_in * 512:(ti_in + 1) * 512, :].rearrange("(s p) d -> p s d", p=128)
                if step == 0:
                    dma("sp", xt[:], xsrc, W=[Rxt])
                else:
                    dma("sp", xt[:], recv_d.ap()[0:512, :].rearrange("(s p) d -> p s d", p=128), R=[Rrecv], W=[Rxt])
                    for s_ in range(4):
                        g_, Rg_ = gsl[s_]
                        gf = g_[:].rearrange("p a b -> p (a b)").bitcast(F32)
                        ts("dve", xt[:, s_, :], xt[:, s_, :], fB, None, ALU.mult, None, [Rxt, Rprm], [Rxt])
                        stt(xt[:, s_, :], gf, fA, xt[:, s_, :], ALU.mult, ALU.add, [Rg_, Rprm, Rxt], [Rxt])
                    for it in gsl:
                        GS.put(it)
                one_layer(0, step, extra=([mem_phase()] if step == 0 else ()))
                if step == 0:
                    def wipe(ap, R):
                        ts("dve", ap, ap, klo, khi, ALU.max, ALU.min, list(R) + [Rprm], list(R))
                    wipe(Sst[:, 0].rearrange("p h d -> p (h d)"), RS[0])
                    wipe(Sbf[:, 0].rearrange("p h d -> p (h d)"), RS[0])
                    wipe(hst[:, 0, :], [Rhst[0]])
                    wipe(haloA[:, 0].rearrange("p c k -> p (c k)"), [Rhalo[0]])
                    wipe(haloB[:, 0].rearrange("p c k -> p (c k)"), [Rhalo[0]])
                    wipe(haloD[:, 0].rearrange("p c k -> p (c k)"), [Rhalo[0]])
                    wipe(kTc[:, 0, 0:128], [Rkv[0]])
                    wipe(vtc[:, 0, 0, :], [Rkv[0]])
                if step < NT:
                    dma("sp", send_d.ap().rearrange("(s p) d -> p s d", p=128), xt[:], R=[Rxt], W=[Rsend])
                    if os.environ.get("PIPE_NOCC"):
                        dma("sp", recv_d.ap()[0:512, :], send_d.ap(), R=[Rsend], W=[Rrecv])
                    else:
                        P.emit("pool", lambda e: e.collective_compute(
                            "AllGather", ALU.bypass, replica_groups=[[0, 1], [2, 3], [4, 5], [6, 7]],
                            ins=[send_d.ap().opt()], outs=[recv_d.ap().opt()]),
                            reads=[Rsend], writes=[Rrecv], cc=True, cost=40000.0)
                extra = [Rsend] if step < NT else []
                if step >= 1:
                    to = step - 1
                    out_toks.append(dma("sp", out_d[to * 512:(to + 1) * 512, :].rearrange("(s p) d -> p s d", p=128), xt[:],
                                        R=[Rxt] + extra))
                if step < NT:
                    tn = min(step + 1, NT - 1)
                    xn = x_d[tn * 512:(tn + 1) * 512, :].rearrange("(s p) d -> p s d", p=128)
                    gsl = [GS.get() for _ in range(4)]
                    for s_ in range(4):
                        g_, Rg_ = gsl[s_]
                        dma("sp", g_[:].rearrange("p a b -> p (a b)").bitcast(F32), xn[:, s_, :], R=extra, W=[Rg_])
        P.finish_wait("sp", out_toks)
        P.run()
        build.stats = dict(cnt=dict(P.cnt), dmas=P.dma_n, marks=marks, model=dict(P.eng_time), busy=dict(P.busy), stall=dict(P.stall))
    return nc


_NC_CACHE = {}


_PER_LAYER = ("g_pre", "g_post", "w_in", "b_in", "a_conv_w", "a_log", "a_dt_bias", "a_norm_g", "b_dw_w", "b_dw_b",
              "b_ln_g", "b_ln_b", "c_sinks", "d_conv_w", "d_conv_b", "d_w_a", "d_b_a", "d_w_x", "d_b_x", "d_lambda",
              "g_mem", "w_mem_kv", "w_br", "w_out")


def kernel(**inputs):
    inp = {k: np.asarray(v) for k, v in inputs.items()}
    B, T, _ = inp["x"].shape
    L = inp["g_pre"].shape[0]
    assert L == 2 and 2 * B <= 8
    key = (T, "pipe")
    if key not in _NC_CACHE:
        _NC_CACHE[key] = build(T, 1, pipe=True)
    nc = _NC_CACHE[key]
    per_stage = []
    for l in range(L):
        inp_l = {k: (v[l:l + 1] if k in _PER_LAYER else v) for k, v in inp.items()}
        prm, winp, bdw, g_post = _host_params(inp_l, 1, stage=l)
        per_stage.append(dict(winp=winp, wbr=np.ascontiguousarray(inp["w_br"][l:l + 1], np.float32),
                              wout=np.ascontiguousarray(inp["w_out"][l:l + 1], np.float32),
                              wmem=np.ascontiguousarray(inp["w_mem_kv"][l:l + 1], np.float32),
                              prm=prm, bdw=bdw, gpost=g_post))
    zx = np.zeros((T, D_MODEL), np.float32)
    in_maps = []
    for b in range(B):
        for stage in range(2):
            m = dict(per_stage[stage])
            m["x"] = np.ascontiguousarray(inp["x"][b], np.float32)
            m["mem"] = np.ascontiguousarray(inp["mem"][b], np.float32)
            in_maps.append(m)
    res = run_bass_kernel_spmd(nc, in_maps, core_ids=list(range(2 * B)))
    kernel.last_all = [np.asarray(r["out"]) for r in res.results]
    return np.stack([np.asarray(res.results[2 * b + 1]["out"]) for b in range(B)], axis=0).astype(np.float32)
```

```python
import os
import numpy as np
from contextlib import ExitStack
import concourse.bass as bass
import concourse.mybir as mybir
from concourse.bass_utils import run_bass_kernel_spmd

F32 = mybir.dt.float32
F32R = mybir.dt.float32r
BF16 = mybir.dt.bfloat16
AF = mybir.ActivationFunctionType
ALU = mybir.AluOpType
AX = mybir.AxisListType

D_MODEL = 1024
SEQ = 4096
DEPTH = 2
MEM_LEN = 256
IN_COLS = 12040
NCOLP = 12288
EPS = 1e-6
NEG = -30000.0

ENGS = ("pe", "act", "dve", "pool", "sp")
EPOCH = 12000
FINE = bool(int(os.environ.get('FINE', 1)))
WAW_SELF = bool(int(os.environ.get('WAW_SELF', 0)))
W_CONVA = int(os.environ.get('W_CONVA', 2))
W_GDN = int(os.environ.get('W_GDN', 2))
W_ATT = int(os.environ.get('W_ATT', 3))
W_MRG = int(os.environ.get('W_MRG', 3))
W_B = int(os.environ.get('W_B', 2))
NDMASEM = 24


class Reg:
    __slots__ = ("name", "w", "r")

    def __init__(self, name):
        self.name = name
        self.w = None
        self.r = []


class Prog:
    def __init__(self, nc, stack):
        self.nc = nc
        self.stack = stack
        self.q = {e: [] for e in ENGS}
        self.cnt = {e: 0 for e in ENGS}
        self.sems = {e: [] for e in ENGS}
        self.seen = {e: {} for e in ENGS}
        self.dma_sems = [stack.enter_context(nc.semaphore(f"dq{i}")) for i in range(NDMASEM)]
        self.dma_n = 0
        self.dma_tok = {}
        self.cc_sems = [stack.enter_context(nc.semaphore(f"cc{i}")) for i in range(2)]
        self.cc_n = 0
        self.cc_tok = {}
        self.eng_time = {e: 0.0 for e in ENGS}
        self.step_fin = 0.0

    def _sem(self, e, epoch):
        while len(self.sems[e]) <= epoch:
            self.sems[e].append(self.stack.enter_context(
                self.nc.semaphore(f"s_{e}_{len(self.sems[e])}")))
        return self.sems[e][epoch]

    def _need(self, e, tok, waits):
        key, val = tok[0], tok[1]
        if self.seen[e].get(key, 0) >= val:
            return
        self.seen[e][key] = val
        waits[key] = max(waits.get(key, 0), val)

    def _wl(self, waits):
        wl = []
        for key, val in waits.items():
            if key[0] == "d":
                wl.append((self.dma_sems[key[1]], val))
            elif key[0] == "c":
                wl.append((self.cc_sems[key[1]], val))
            else:
                wl.append((self._sem(key[0], key[1]), val))
        return wl

    def emit(self, e, fn, reads=(), writes=(), dma=False, selfdep=True, cost=300.0, cc=False):
        waits = {}
        ready = 0.0
        for r in reads:
            t = r.w
            if t is not None:
                ready = max(ready, t[4])
                if (selfdep or t[2] != e or dma or t[3]):
                    self._need(e, t, waits)
        for w in writes:
            t = w.w
            if t is not None:
                ready = max(ready, t[4])
                if (t[2] != e or dma or t[3] or (selfdep and WAW_SELF)):
                    self._need(e, t, waits)
            for t in w.r:
                ready = max(ready, t[4])
                if t[2] != e or dma or t[3]:
                    self._need(e, t, waits)
        cost = cost * float(os.environ.get("CS_" + ("dma" if dma else e), 1.0))
        if dma:
            t0 = max(ready + 100.0, self.eng_time[e])
            self.eng_time[e] = t0 + 60.0
            fin = t0 + cost
        else:
            t0 = max(ready + float(os.environ.get('CS_lat', 150.0)), self.eng_time[e]) if ready > self.eng_time[e] - 1e-9 and waits else max(ready, self.eng_time[e])
            fin = t0 + cost
            self.eng_time[e] = fin
        self.step_fin = max(self.step_fin, fin)
        self.busy = getattr(self, "busy", {})
        self.busy[e] = self.busy.get(e, 0.0) + cost
        self.stall = getattr(self, "stall", {})
        self.stall[e] = self.stall.get(e, 0.0) + max(0.0, t0 - max(self.eng_time[e] - (cost if not dma else 60.0), 0.0))
        if cc:
            k = self.cc_n
            self.cc_n += 1
            si = k % 2
            val = k // 2 + 1
            if k >= 2:
                self._need(e, self.cc_tok[k - 2], waits)
            tok = (("c", si), val, e, True, fin)
            self.cc_tok[k] = tok
            inc = (self.cc_sems[si], 1)
        elif dma:
            k = self.dma_n
            self.dma_n += 1
            si = k % NDMASEM
            val = 16 * (k // NDMASEM + 1)
            if k >= NDMASEM:
                self._need(e, self.dma_tok[k - NDMASEM], waits)
            tok = (("d", si), val, e, True, fin)
            self.dma_tok[k] = tok
            inc = (self.dma_sems[si], 16)
        else:
            self.cnt[e] += 1
            c = self.cnt[e]
            epoch, val = (c - 1) // EPOCH, (c - 1) % EPOCH + 1
            tok = ((e, epoch), val, e, False, fin)
            inc = (self._sem(e, epoch), 1)
        self.q[e].append((self._wl(waits), fn, inc))
        for r in reads:
            if not r.r or r.r[-1] is not tok:
                r.r.append(tok)
        for w in writes:
            w.w = tok
            w.r = []
        return tok

    def finish_wait(self, e, toks):
        waits = {}
        for t in toks:
            self._need(e, t, waits)
        self.q[e].append((self._wl(waits), None, None))

    def run(self):
        nc = self.nc
        with nc.Block() as block:
            def play(eng, items):
                for wl, fn, inc in items:
                    for s, v in wl:
                        eng.wait_ge(s, v)
                    if fn is not None:
                        fn(eng).then_inc(inc[0], inc[1])

            @block.tensor
            def _(eng):
                play(eng, self.q["pe"])

            @block.scalar
            def _(eng):
                play(eng, self.q["act"])

            @block.vector
            def _(eng):
                play(eng, self.q["dve"])

            @block.gpsimd
            def _(eng):
                play(eng, self.q["pool"])

            @block.sync
            def _(eng):
                play(eng, self.q["sp"])


class Slots:
    def __init__(self, items):
        self.free = list(items)
        self.n = len(items)

    def get(self):
        if not self.free:
            raise RuntimeError("slot pool exhausted")
        return self.free.pop(0)

    def put(self, it):
        self.free.append(it)


def _win_perm():
    p = list(range(0, 2048))
    p += list(range(2056, 3592))
    cq0 = 3592
    for c in range(4):
        p += list(range(cq0 + c * 64, cq0 + (c + 1) * 64))
        p += list(range(cq0 + (4 + c) * 64, cq0 + (5 + c) * 64))
    p += list(range(4104, 4360)) + list(range(2048, 2056)) + [-1] * 248
    p += list(range(4360, 12040))
    p = np.array(p, dtype=np.int64)
    assert p.size == NCOLP
    return p


def _t5_bucket(dist):
    n = np.maximum(dist, 0)
    max_exact = 16
    large = max_exact + (np.log(np.maximum(n, 1) / max_exact) / np.log(128 / max_exact) * (32 - max_exact)).astype(np.int32)
    large = np.minimum(large, 31)
    return np.where(n < max_exact, n, large).astype(np.int32)


def _param_layout(L):
    lay = [("ident", 128), ("triu", 128), ("masku", 128), ("masklneg", 128), ("bd01", 128), ("off01", 128),
           ("ones", 128), ("maskc", 256), ("bandb", 2048), ("flags", 6)]
    for l in range(L):
        lay += [(f"gpre{l}", 8), (f"gmem{l}", 8), (f"bfm{l}", 96), (f"bba{l}", 8), (f"bv{l}", 128),
                (f"aconv{l}", 48), (f"alog{l}", 4), (f"adt{l}", 4), (f"anorm{l}", 1),
                (f"bdw{l}", 124), (f"bdwb{l}", 4), (f"blng{l}", 4), (f"blnb{l}", 4),
                (f"sinks{l}", 8), (f"dconv{l}", 16), (f"dconvb{l}", 4), (f"dba{l}", 4), (f"dbx{l}", 4),
                (f"dlam{l}", 4)]
    off = {}
    o = 0
    for n, w in lay:
        off[n] = (o, w)
        o += w
    return off, o


def _host_params(inp, L, stage=0):
    off, tot = _param_layout(L)
    prm = np.zeros((128, tot), np.float32)
    fA, fB = (1.0, 0.0) if stage == 0 else (0.0, 1.0)
    o_, _w = off["flags"]
    big = 3.0e38 if stage == 0 else 0.0
    prm[:, o_:o_ + 6] = np.array([fA, fB, NEG if stage == 1 else 0.0, fA, -big, big], np.float32)[None, :]

    def put(name, a):
        o, w = off[name]
        prm[:, o:o + w] = np.asarray(a, np.float32).reshape(128, w)

    def fm(v, c):
        return np.asarray(v).reshape(c, 128).T

    def row(v):
        v = np.asarray(v).reshape(1, -1)
        return np.broadcast_to(v, (128, v.shape[1]))

    i = np.arange(128)
    put("ident", np.eye(128))
    put("triu", (i[:, None] <= i[None, :]))
    put("masku", np.where(i[None, :] >= i[:, None], 0.0, NEG))
    put("masklneg", np.where(i[:, None] > i[None, :], 0.0, NEG))
    blk = (i[:, None] // 64) == (i[None, :] // 64)
    put("bd01", blk)
    put("off01", ~blk)
    put("ones", np.ones((128, 128)))
    ii = np.arange(128)[:, None]
    jj = np.arange(256)[None, :]
    dist = ii + 128 - jj
    put("maskc", np.where((dist >= 0) & (dist < 128), 0.0, NEG))
    bucket = _t5_bucket(dist)
    bb = inp["rel_bias"][bucket]
    put("bandb", np.transpose(bb, (0, 2, 1)))
    perm = _win_perm()
    for l in range(L):
        put(f"gpre{l}", fm(inp["g_pre"][l], 8))
        put(f"gmem{l}", fm(inp["g_mem"][l], 8))
        b = inp["b_in"][l]
        bp = np.where(perm >= 0, b[np.maximum(perm, 0)], 0.0)
        put(f"bfm{l}", fm(bp, 96))
        put(f"bba{l}", row(b[2048:2056]))
        put(f"bv{l}", row(b[4232:4360]))
        put(f"aconv{l}", np.transpose(inp["a_conv_w"][l].reshape(4, 12, 128), (2, 1, 0)))
        put(f"alog{l}", row(inp["a_log"][l]))
        put(f"adt{l}", row(inp["a_dt_bias"][l]))
        put(f"anorm{l}", inp["a_norm_g"][l].reshape(128, 1))
        put(f"bdw{l}", np.transpose(inp["b_dw_w"][l].reshape(31, 4, 128), (2, 1, 0)))
        put(f"bdwb{l}", fm(inp["b_dw_b"][l], 4))
        put(f"blng{l}", fm(inp["b_ln_g"][l], 4))
        put(f"blnb{l}", fm(inp["b_ln_b"][l], 4))
        put(f"sinks{l}", row(inp["c_sinks"][l]))
        put(f"dconv{l}", np.transpose(inp["d_conv_w"][l].reshape(4, 4, 128), (2, 1, 0)))
        put(f"dconvb{l}", fm(inp["d_conv_b"][l], 4))
        put(f"dba{l}", fm(inp["d_b_a"][l], 4))
        put(f"dbx{l}", fm(inp["d_b_x"][l], 4))
        put(f"dlam{l}", fm(inp["d_lambda"][l], 4))
    w_in = inp["w_in"][:L]
    winp = np.zeros((L, D_MODEL, NCOLP), np.float32)
    valid = perm >= 0
    winp[:, :, valid] = w_in[:, :, perm[valid]]
    bdw = np.zeros((L, 2, 4, 128, 128), np.float32)
    for l in range(L):
        for t, nm in enumerate(("d_w_a", "d_w_x")):
            w = inp[nm][l]
            for c in range(4):
                bdw[l, t, c, 0:64, 0:64] = w[2 * c]
                bdw[l, t, c, 64:128, 64:128] = w[2 * c + 1]
    g_post = np.ascontiguousarray(inp["g_post"][:L], np.float32)
    return prm, winp, bdw, g_post


def build(T, L, taps=(), pipe=False):
    NT = T // 512
    nc = bass.Bass("TRN2", target_bir_lowering=False)
    poff, ptot = _param_layout(L)
    x_d = nc.dram_tensor("x", [T, D_MODEL], F32, kind="ExternalInput").ap()
    mem_d = nc.dram_tensor("mem", [MEM_LEN, D_MODEL], F32, kind="ExternalInput").ap()
    win_d = nc.dram_tensor("winp", [L, D_MODEL, NCOLP], F32, kind="ExternalInput").ap()
    wbr_d = nc.dram_tensor("wbr", [L, 5, 512, D_MODEL], F32, kind="ExternalInput").ap()
    wout_d = nc.dram_tensor("wout", [L, D_MODEL, D_MODEL], F32, kind="ExternalInput").ap()
    wmem_d = nc.dram_tensor("wmem", [L, D_MODEL, D_MODEL], F32, kind="ExternalInput").ap()
    prm_d = nc.dram_tensor("prm", [128, ptot], F32, kind="ExternalInput").ap()
    bdw_d = nc.dram_tensor("bdw", [L, 2, 4, 128, 128], F32, kind="ExternalInput").ap()
    gpost_d = nc.dram_tensor("gpost", [L, D_MODEL], F32, kind="ExternalInput").ap()
    out_d = nc.dram_tensor("out", [T, D_MODEL], F32, kind="ExternalOutput").ap()
    NSCR = 26 * L
    scr_d = nc.dram_tensor("wscr", [NSCR, 128, 8 * 512], BF16, kind="Internal").ap()
    scrb_d = nc.dram_tensor("wscrb", [L * 10, 128, 4 * 512], BF16, kind="Internal").ap()
    tap_d = {}
    for name, shape in taps:
        tap_d[name] = nc.dram_tensor("tap_" + name, list(shape), F32, kind="ExternalOutput").ap()

    with ExitStack() as st:
        P = Prog(nc, st)

        def sb(name, shape, dt):
            return st.enter_context(nc.sbuf_tensor("sb_" + name, list(shape), dt))

        def _fsize(ap):
            n = 1
            for d in ap.shape[1:]:
                n *= d
            return n

        def op(eng, meth, R=(), W=(), **kw):
            o = kw.get("out", kw.get("ap"))
            n = _fsize(o) if o is not None else 128
            if eng == "pe":
                if meth == "transpose":
                    c = 64.0 + 128 * 0.5
                else:
                    nn = _fsize(kw["rhs"])
                    f = 4.0 if kw["rhs"].dtype in (F32, F32R) else 1.0
                    c = 40.0 + nn * 0.52 * f
            elif eng == "act":
                c = 220.0 + n * 0.9
            elif eng == "dve":
                c = 120.0 + n * 0.75
            else:
                c = 250.0 + n * 2.0
            return P.emit(eng, lambda e: getattr(e, meth)(**kw), reads=R, writes=W, selfdep=(eng != "pe"), cost=c)

        def mm(out, lhsT, rhs, start, stop, R, W):
            return op("pe", "matmul", R, W, out=out, lhsT=lhsT, rhs=rhs, start=start, stop=stop)

        def tr(out, in_, ident, R, W):
            return op("pe", "transpose", R, W, out=out, in_=in_, identity=ident)

        def act(out, in_, func, R, W, **kw):
            return op("act", "activation", R, W, out=out, in_=in_, func=func, **kw)

        def dma(eng, out, in_, R=(), W=()):
            nb = 128 * _fsize(out) * 4
            return P.emit(eng, lambda e: e.dma_start(out=out, in_=in_), reads=R, writes=W, dma=True, cost=2500.0 + nb / 150.0)

        def ts(eng, out, in0, s1, s2, op0, op1, R, W):
            if s2 is None:
                return op(eng, "tensor_scalar", R, W, out=out, in0=in0, scalar1=s1, scalar2=None, op0=op0)
            return op(eng, "tensor_scalar", R, W, out=out, in0=in0, scalar1=s1, scalar2=s2, op0=op0, op1=op1)

        def tt(eng, out, in0, in1, o, R, W):
            return op(eng, "tensor_tensor", R, W, out=out, in0=in0, in1=in1, op=o)

        def stt(out, in0, scalar, in1, op0, op1, R, W):
            return op("dve", "scalar_tensor_tensor", R, W, out=out, in0=in0, scalar=scalar, in1=in1, op0=op0, op1=op1)

        prm = sb("prm", [128, ptot], F32)
        Rprm = Reg("prm")

        def pp(name, lo=0, hi=None):
            o, w = poff[name]
            hi = w if hi is None else hi
            return prm[:, o + lo:o + hi]

        cst = sb("cst", [128, 6, 128], BF16)
        cstr = sb("cstr", [128, 2, 128], F32)
        Rcst = Reg("cst")
        xt = sb("xt", [128, 4, D_MODEL], F32)
        Rxt = Reg("xt")
        hT = sb("hT", [128, 8, 512], BF16)
        RhT = Reg("hT")
        yT = sb("yT", [128, 5, 4, 512], BF16)
        RyT = [Reg(f"yT{n}") for n in range(5)]
        mgb = sb("mgb", [128, 8, 512], BF16)
        Rmgb = Reg("mgb")
        sm = sb("sm", [128, 256], F32)
        Rsm = {}

        def smr(name):
            if name not in Rsm:
                Rsm[name] = Reg("sm_" + name)
            return Rsm[name]

        NW = 5
        WB = Slots([(sb(f"wb{i}", [128, 8, 512], BF16), Reg(f"wb{i}")) for i in range(NW)])
        PS = Slots([(st.enter_context(nc.psum_tensor(f"ps{i}", [128, 512], F32)), Reg(f"ps{i}")) for i in range(8)])
        FS = Slots([(sb(f"f{i}", [128, 512], F32), Reg(f"f{i}")) for i in range(7)])
        HS = Slots([(sb(f"h{i}", [128, 544], BF16), Reg(f"h{i}")) for i in range(10)])
        GS = Slots([(sb(f"g{i}", [128, 4, 512], BF16), Reg(f"g{i}")) for i in range(8)])
        FR = Slots([(sb(f"fr{i}", [128, 512], F32), Reg(f"fr{i}")) for i in range(5)])
        S16 = Slots([(sb(f"s16_{i}", [128, 128], BF16), Reg(f"s16_{i}")) for i in range(4)])
        DG = Slots([(sb(f"dg{i}", [128, 4, 128], BF16), Reg(f"dg{i}")) for i in range(2)])

        haloA = sb("haloA", [128, L, 12, 4], BF16)
        haloB = sb("haloB", [128, L, 4, 32], BF16)
        haloD = sb("haloD", [128, L, 4, 4], BF16)
        Rhalo = [Reg(f"halo{l}") for l in range(L)]
        Sst = sb("Sst", [128, L, 4, 128], F32)
        Sbf = sb("Sbf", [128, L, 4, 128], BF16)
        RS = [[Reg(f"S{l}_{h}") for h in range(4)] for l in range(L)]
        hst = sb("hst", [128, L, 4], F32)
        Rhst = [Reg(f"hst{l}") for l in range(L)]
        kTc = sb("kTc", [128, L, 640], BF16)
        vtc = sb("vtc", [128, L, 5, 128], BF16)
        Rkv = [Reg(f"kv{l}") for l in range(L)]
        mkT = sb("mkT", [128, L, 4, 256], BF16)
        mvt = sb("mvt", [128, L, 2, 512], BF16)
        Rmem = [Reg(f"mem{l}") for l in range(L)]
        bdws = sb("bdws", [128, L, 2, 4, 128], BF16)
        Rbdw = Reg("bdw")
        lruc = sb("lruc", [128, L, 8], F32)
        nA = sb("nA", [128, L, 4], F32)
        Rlc = Reg("lruc")

        ident_f = pp("ident")
        ident_b = cst[:, 0, :]
        ones_b = cst[:, 1, :]
        ones_r = cstr[:, 0, :].bitcast(F32R)

        dma("sp", prm[:], prm_d, W=[Rprm])
        op("dve", "tensor_copy", [Rprm], [Rcst], out=cst[:, 0, :], in_=pp("ident"))
        op("dve", "tensor_copy", [Rprm], [Rcst], out=cst[:, 1, :], in_=pp("ones"))
        op("dve", "tensor_copy", [Rprm], [Rcst], out=cstr[:, 0, :].bitcast(F32R), in_=pp("ones"))
        bo_, bw_ = poff["bandb"]
        bandv = prm[:, bo_:bo_ + bw_].rearrange("p (h j) -> p h j", h=8)
        tt("dve", bandv, bandv, pp("maskc").unsqueeze(1).broadcast_to([128, 8, 256]), ALU.add, [Rprm], [Rprm])
        for l in range(L):
            dma("pool", bdws[:, l].rearrange("p t c m -> p (t c) m"),
                bdw_d[l].rearrange("t c p m -> p (t c) m"), W=[Rbdw])
            act(lruc[:, l, 0:4], pp(f"dlam{l}"), AF.Exp, [Rprm], [Rlc], scale=-1.0)
            act(lruc[:, l, 0:4], lruc[:, l, 0:4], AF.Ln, [Rlc], [Rlc], bias=1.0)
            ts("dve", lruc[:, l, 4:8], lruc[:, l, 0:4], -16.0, None, ALU.mult, None, [Rlc], [Rlc])
            ts("dve", lruc[:, l, 0:4], lruc[:, l, 0:4], -8.0, None, ALU.mult, None, [Rlc], [Rlc])
            act(nA[:, l, :], pp(f"alog{l}"), AF.Exp, [Rprm], [Rlc])
            ts("dve", nA[:, l, :], nA[:, l, :], -1.0, None, ALU.mult, None, [Rlc], [Rlc])
            op("dve", "memset", [], [Rhalo[l]], ap=haloA[:, l], constant=0.0)
            op("dve", "memset", [], [Rhalo[l]], ap=haloB[:, l], constant=0.0)
            op("dve", "memset", [], [Rhalo[l]], ap=haloD[:, l], constant=0.0)
            for h in range(4):
                op("dve", "memset", [], [RS[l][h]], ap=Sst[:, l, h, :], constant=0.0)
                op("dve", "memset", [], [RS[l][h]], ap=Sbf[:, l, h, :], constant=0.0)
            op("dve", "memset", [], [Rhst[l]], ap=hst[:, l, :], constant=0.0)
            op("dve", "memset", [], [Rkv[l]], ap=kTc[:, l, :], constant=0.0)
            op("dve", "memset", [], [Rkv[l]], ap=vtc[:, l], constant=0.0)

        def gget(pool):
            while not pool.free:
                P.blocked = True
                yield
            P.progress = True
            return pool.get()

        def run(g):
            idle = 0
            P.progress = False
            for _ in g:
                if P.progress:
                    idle = 0
                else:
                    idle += 1
                    if idle > 10000:
                        raise RuntimeError("build-time scheduling deadlock")
                P.progress = False

        def _step_best(active, rdy, bias=None):
            g = min(active, key=lambda x: rdy[id(x)] - (bias.get(id(x), 0.0) if bias else 0.0))
            save = P.step_fin
            P.step_fin = 0.0
            P.blocked = False
            try:
                next(g)
                if P.step_fin > 0.0:
                    rdy[id(g)] = P.step_fin
                else:
                    others = [rdy[id(x)] for x in active if x is not g]
                    rdy[id(g)] = (min(others) if others else rdy[id(g)]) + 50.0
            except StopIteration:
                active.remove(g)
                P.progress = True
            P.step_fin = max(save, P.step_fin)

        def par(*gens, prio=None):
            active = list(gens)
            rdy = {id(g): 0.0 for g in active}
            bias = {id(g): (prio[i] if prio else 0.0) for i, g in enumerate(active)}
            while active:
                _step_best(active, rdy, bias)
                yield

        def pipeline(gens, width):
            it = iter(gens)
            active = []
            rdy = {}
            done = False
            while True:
                while not done and len(active) < width:
                    try:
                        active.append(next(it))
                        P.progress = True
                    except StopIteration:
                        done = True
                if not active:
                    return
                for g in active:
                    rdy.setdefault(id(g), 0.0)
                _step_best(active, rdy)
                yield

        Rscr = {}
        scr_seen = set()

        class WStream:
            def __init__(self, srcs):
                self.srcs = srcs
                self.i = 0
                self.pend = {}

            def _issue(self, i, slot):
                wt, Rw = slot
                src, ncols, key = self.srcs[i]
                if key is None or ncols != 512:
                    dma("pool", wt[:, :, 0:ncols], src.rearrange("(kc p) n -> p kc n", p=128), W=[Rw])
                elif key not in scr_seen:
                    scr_seen.add(key)
                    Rscr[key] = Reg(f"scr{key}")
                    dma("pool", wt[:, :, 0:ncols], src.rearrange("(kc p) n -> p kc n", p=128), W=[Rw])
                    dma("sp", scr_d[key], wt[:].rearrange("p a b -> p (a b)"), R=[Rw], W=[Rscr[key]])
                else:
                    dma("sp", wt[:].rearrange("p a b -> p (a b)"), scr_d[key], R=[Rscr[key]], W=[Rw])
                self.pend[i] = slot

            def prefetch(self):
                if self.i < len(self.srcs) and self.i not in self.pend and WB.free:
                    self._issue(self.i, WB.get())

            def take(self):
                i = self.i
                self.i += 1
                if i not in self.pend:
                    slot = yield from gget(WB)
                    self._issue(i, slot)
                cur = self.pend.pop(i)
                self.prefetch()
                return cur

        def winblk(l, blk):
            return (win_d[l][:, blk * 512:(blk + 1) * 512], 512, l * 26 + blk)

        def norm_T(src, Rsrc, nsub, gname, dst, Rdst, inv_ap, Rinv):
            for s in range(nsub):
                jt, Rj = yield from gget(FS)
                act(jt[:].bitcast(BF16), src[:, s, :], AF.Square, [Rsrc], [Rj, Rinv], accum_out=inv_ap[:, s:s + 1])
                FS.put((jt, Rj))
            rs = sm[:, 8:8 + nsub]
            Rrs = smr("rs")
            act(inv_ap, inv_ap, AF.Ln, [Rinv], [Rinv], scale=1.0 / D_MODEL, bias=EPS)
            act(rs, inv_ap, AF.Exp, [Rinv], [Rrs], scale=-0.5)
            act(inv_ap, inv_ap, AF.Exp, [Rinv], [Rinv], scale=0.5)
            for s in range(nsub):
                ts("dve", src[:, s, :], src[:, s, :], rs[:, s:s + 1], None, ALU.mult, None, [Rsrc, Rrs], [Rsrc])
            for c in range(8):
                b, Rb = yield from gget(PS)
                for s in range(nsub):
                    tr(b[:, s * 128:(s + 1) * 128], src[:, s, c * 128:(c + 1) * 128], ident_f, [Rsrc, Rprm], [Rb])
                act(dst[:, c, 0:nsub * 128], b[:, 0:nsub * 128], AF.Copy, [Rb, Rprm], [Rdst], scale=pp(gname, c, c + 1))
                PS.put((b, Rb))
                yield

        def proj(wt, Rw, cofs, M, evac, ntok=512, rhs=None, Rrhs=None):
            rhs = hT if rhs is None else rhs
            Rrhs = RhT if Rrhs is None else Rrhs
            b, Rb = yield from gget(PS)
            for kc in range(8):
                mm(b[0:M, 0:ntok], wt[:, kc, cofs:cofs + M], rhs[:, kc, 0:ntok], kc == 0, kc == 7, [Rw, Rrhs], [Rb])
            yield
            evac(b, Rb)
            PS.put((b, Rb))

        mem_ready = {}
        if pipe:
            memt = sb("memt", [128, 2, D_MODEL], F32)
            memT = sb("memT", [128, 8, 256], BF16)
            Rmemt, RmemT = Reg("memt"), Reg("memT")
            minv, Rminv = sm[:, 200:202], smr("minv")
        else:
            memt, Rmemt, memT, RmemT = xt, Rxt, hT, RhT
            minv, Rminv = sm[:, 0:2], smr("inv")

        def mem_phase():
            for l in range(L):
                dma("sp", memt[:, 0:2, :], mem_d.rearrange("(s p) d -> p s d", p=128), W=[Rmemt])
                yield from norm_T(memt, Rmemt, 2, f"gmem{l}", memT, RmemT, minv, Rminv)
                ws = WStream([(wmem_d[l][:, 0:512], 512, None), (wmem_d[l][:, 512:1024], 512, None)])
                wt, Rw = yield from ws.take()
                for h in range(4):
                    def ev(b, Rb, h=h, l=l):
                        act(mkT[:, l, h, :], b[:, 0:256], AF.Copy, [Rb], [Rmem[l]])
                    yield from proj(wt, Rw, h * 128, 128, ev, ntok=256, rhs=memT, Rrhs=RmemT)
                WB.put((wt, Rw))
                wt, Rw = yield from ws.take()
                for s in range(2):
                    b, Rb = yield from gget(PS)
                    for kc in range(8):
                        mm(b[:, :], memT[:, kc, s * 128:(s + 1) * 128], wt[:, kc, :], kc == 0, kc == 7, [Rw, RmemT], [Rb])
                    yield
                    act(mvt[:, l, s, :], b[:, :], AF.Copy, [Rb], [Rmem[l]])
                    PS.put((b, Rb))
                WB.put((wt, Rw))
                mem_ready[l] = True
        if not pipe:
            run(mem_phase())

        def convA_item(l, grp, c, wt, Rw, qnT, RqnT, knT, RknT, vtok, Rvtok):
            ch = grp * 4 + c
            bfm = lambda q: pp(f"bfm{l}", q, q + 1)
            pre, Rpre = yield from gget(HS)
            op("pool", "tensor_copy", [Rhalo[l]], [Rpre], out=pre[:, 0:3], in_=haloA[:, l, ch, 0:3])

            def ev(b, Rb):
                act(pre[:, 3:515], b[:, :], AF.Identity, [Rb, Rprm], [Rpre], bias=bfm(ch))
            yield from proj(wt, Rw, c * 128, 128, ev)
            op("pool", "tensor_copy", [Rpre], [Rhalo[l]], out=haloA[:, l, ch, 0:3], in_=pre[:, 512:515])
            dg, Rdg = yield from gget(DG)
            o_, _w = poff[f"aconv{l}"]
            for k in range(4):
                ts("pool", dg[:, k, :], ident_b, prm[:, o_ + ch * 4 + k:o_ + ch * 4 + k + 1], 0.0, ALU.mult, ALU.add,
                   [Rcst, Rprm], [Rdg])
            yield
            b, Rb = yield from gget(PS)
            for k in range(4):
                mm(b[:, :], dg[:, k, :], pre[:, k:k + 512], k == 0, k == 3, [Rdg, Rpre], [Rb])
            DG.put((dg, Rdg))
            HS.put((pre, Rpre))
            yield
            if grp == 2:
                vT, RvT = yield from gget(HS)
                act(vT[:, 0:512], b[:, :], AF.Silu, [Rb], [RvT])
                PS.put((b, Rb))
                yield
                b, Rb = yield from gget(PS)
                bb = b[:].bitcast(BF16)
                for s in range(4):
                    tr(bb[:, s * 128:(s + 1) * 128], vT[:, s * 128:(s + 1) * 128], ident_b, [RvT, Rcst], [Rb])
                HS.put((vT, RvT))
                yield
                act(vtok[:, :, c * 128:(c + 1) * 128], bb[:, 0:512].rearrange("p (s d) -> p s d", s=4), AF.Copy, [Rb], [Rvtok])
                PS.put((b, Rb))
                return
            cs, Rcs = yield from gget(FS)
            act(cs[:, :], b[:, :], AF.Silu, [Rb], [Rcs])
            PS.put((b, Rb))
            sq, Rsq = yield from gget(HS)
            act(sq[:, 0:512], cs[:, :], AF.Square, [Rcs], [Rsq])
            yield
            b, Rb = yield from gget(PS)
            mm(b[:, :], ones_b, sq[:, 0:512], True, True, [Rcst, Rsq], [Rb])
            HS.put((sq, Rsq))
            yield
            rs, Rrs = yield from gget(FS)
            act(rs[:, :], b[:, :], AF.Ln, [Rb], [Rrs], bias=EPS)
            PS.put((b, Rb))
            act(rs[:, :], rs[:, :], AF.Exp, [Rrs], [Rrs], scale=-0.5)
            dst, Rdst = (qnT, RqnT) if grp == 0 else (knT, RknT)
            stt(dst[:, c, :], cs[:, :], (128 ** -0.5) if grp == 0 else 1.0, rs[:, :], ALU.mult, ALU.mult,
                [Rcs, Rrs], [Rdst])
            FS.put((cs, Rcs))
            FS.put((rs, Rrs))
            yield

        def chain_A(l, ti):
            bfm = lambda q: pp(f"bfm{l}", q, q + 1)
            qnT, RqnT = yield from gget(GS)
            knT, RknT = yield from gget(GS)
            sza, Rsza = yield from gget(GS)
            ktok, Rktok = yield from gget(GS)
            vtok, Rvtok = yield from gget(GS)
            ws = WStream([winblk(l, 8), winblk(l, 0), winblk(l, 1), winblk(l, 2), winblk(l, 3)])
            wt, Rw = yield from ws.take()
            yield from stage_C_kv(l, ti, wt, Rw)
            bba, Rbba = yield from gget(PS)
            for s in range(4):
                for kc in range(8):
                    mm(bba[:, s * 8:(s + 1) * 8], hT[:, kc, s * 128:(s + 1) * 128], wt[:, kc, 256:264], kc == 0, kc == 7,
                       [Rw, RhT], [Rbba])
            WB.put((wt, Rw))
            kv_ready[(l, ti)] = True
            yield
            bg = sm[:, 16:48].rearrange("p (s e) -> p s e", s=4)
            Rbg = smr("bg")
            tt("dve", bg, bba[:, 0:32].rearrange("p (s e) -> p s e", s=4),
               pp(f"bba{l}").unsqueeze(1).broadcast_to([128, 4, 8]), ALU.add, [Rbba, Rprm], [Rbg])
            PS.put((bba, Rbba))
            for grp in range(3):
                wt, Rw = yield from ws.take()
                yield from pipeline([convA_item(l, grp, c, wt, Rw, qnT, RqnT, knT, RknT, vtok, Rvtok) for c in range(4)], W_CONVA)
                WB.put((wt, Rw))
            wt, Rw = yield from ws.take()
            for c in range(4):
                def ev(b, Rb, c=c):
                    act(sza[:, c, :], b[:, :], AF.Silu, [Rb, Rprm], [Rsza], bias=bfm(12 + c))
                yield from proj(wt, Rw, c * 128, 128, ev)
            WB.put((wt, Rw))
            bet = sm[:, 48:64].rearrange("p (s e) -> p s e", s=4)
            gg = sm[:, 64:80].rearrange("p (s e) -> p s e", s=4)
            act(bet, bg[:, :, 0:4], AF.Sigmoid, [Rbg], [smr("bet")])
            tt("dve", gg, bg[:, :, 4:8], pp(f"adt{l}").unsqueeze(1).broadcast_to([128, 4, 4]), ALU.add, [Rbg, Rprm], [smr("gg")])
            act(gg, gg, AF.Exp, [smr("gg")], [smr("gg")])
            act(gg, gg, AF.Ln, [smr("gg")], [smr("gg")], bias=1.0)
            tt("dve", gg, gg, nA[:, l, :].unsqueeze(1).broadcast_to([128, 4, 4]), ALU.mult, [smr("gg"), Rlc], [smr("gg")])
            for s in range(4):
                b, Rb = yield from gget(PS)
                bb = b[:].bitcast(BF16)
                for h in range(4):
                    tr(bb[:, h * 128:(h + 1) * 128], knT[:, h, s * 128:(s + 1) * 128], ident_b, [RknT, Rcst], [Rb])
                yield
                act(ktok[:, s, :], bb[:, 0:512], AF.Copy, [Rb], [Rktok])
                PS.put((b, Rb))
            marks.append(("A_prologue_end", ti, l, P.cnt["pe"]))
            for s in range(4):
                yield from gdn_chunk(l, ti, s, qnT, RqnT, knT, RknT, ktok, Rktok, vtok, Rvtok, sza, Rsza, bet, gg)
                marks.append((f"A_chunk{s}_end", ti, l, P.cnt["pe"]))
            for it in ((qnT, RqnT), (knT, RknT), (ktok, Rktok), (vtok, Rvtok), (sza, Rsza)):
                GS.put(it)

        def gdn_chunk(l, ti, s, qnT, RqnT, knT, RknT, ktok, Rktok, vtok, Rvtok, sza, Rsza, bet, gg):
            tsl = slice(s * 128, (s + 1) * 128)
            Rbet, Rgg = smr("bet"), smr("gg")
            gs = sm[:, 80:88]
            Rgs = smr("gs")
            H4 = lambda ap: ap.rearrange("p (h j) -> p h j", h=4)
            bc4 = lambda ap: ap.unsqueeze(2).broadcast_to([128, 4, 128])
            bcm = lambda ap: ap.unsqueeze(1).broadcast_to([128, 4, 128])
            hsl = lambda h: slice(h * 128, (h + 1) * 128)
            bG, RbG = yield from gget(PS)
            mm(bG[:, 0:4], pp("triu"), gg[:, s, :], True, True, [Rprm, Rgg], [RbG])
            mm(bG[:, 4:8], pp("ones"), gg[:, s, :], True, True, [Rprm, Rgg], [RbG])
            yield
            op("dve", "tensor_copy", [RbG], [Rgs], out=gs, in_=bG[:, 0:8])
            PS.put((bG, RbG))
            ex = sm[:, 88:104]
            Rex = smr("ex")
            act(ex[:, 0:4], gs[:, 0:4], AF.Exp, [Rgs], [Rex])
            yield
            tt("dve", ex[:, 0:4], ex[:, 0:4], bet[:, s, :], ALU.mult, [Rex, Rbet], [Rex])
            tt("dve", ex[:, 12:16], gs[:, 4:8], gs[:, 0:4], ALU.subtract, [Rgs], [Rex])
            yield
            act(ex[:, 4:8], ex[:, 12:16], AF.Exp, [Rex], [Rex])
            act(ex[:, 8:12], gs[:, 4:8], AF.Exp, [Rgs], [Rex])
            rg, Rrg = yield from gget(FS)
            tt("dve", H4(rg[:, :]), bcm(ident_f), bc4(gs[:, 0:4]), ALU.mult, [Rprm, Rgs], [Rrg])
            bGr, RbGr = yield from gget(PS)
            mm(bGr[:, :], pp("ones"), rg[:, :], True, True, [Rprm, Rrg], [RbGr])
            FS.put((rg, Rrg))
            bK, RbK = yield from gget(PS)
            bQ, RbQ = yield from gget(PS)
            for h in range(4):
                mm(bK[:, hsl(h)], knT[:, h, tsl], knT[:, h, tsl], True, True, [RknT], [RbK])
            for h in range(4):
                mm(bQ[:, hsl(h)], knT[:, h, tsl], qnT[:, h, tsl], True, True, [RknT, RqnT], [RbQ])
            yield
            dd, Rdd = yield from gget(FS)
            e1, Re1 = yield from gget(FS)
            e2, Re2 = yield from gget(FS)
            tt("dve", H4(dd[:, :]), H4(bGr[:, :]), bc4(gs[:, 0:4]), ALU.subtract, [RbGr, Rgs], [Rdd])
            yield
            tt("dve", H4(e2[:, :]), H4(dd[:, :]), bcm(pp("masku")), ALU.add, [Rdd, Rprm], [Re2])
            act(e2[:, :], e2[:, :], AF.Exp, [Re2], [Re2])
            yield
            tt("dve", H4(e1[:, :]), H4(dd[:, :]), bcm(pp("masklneg")), ALU.subtract, [Rdd, Rprm], [Re1])
            act(e1[:, :], e1[:, :], AF.Exp, [Re1], [Re1], scale=-1.0)
            yield
            act(dd[:, :], bGr[:, :], AF.Exp, [RbGr], [Rdd])
            PS.put((bGr, RbGr))
            qd, Rqd = yield from gget(HS)
            tt("dve", H4(qd[:, 0:512]), qnT[:, :, tsl], H4(dd[:, :]), ALU.mult, [RqnT, Rdd], [Rqd])
            yield
            tt("dve", e1[:, :], bK[:, :], e1[:, :], ALU.mult, [RbK, Re1], [Re1])
            yield
            PS.put((bK, RbK))
            tt("dve", H4(dd[:, :]), H4(e1[:, :]), bc4(bet[:, s, :]), ALU.mult, [Re1, Rbet], [Rdd])
            at, Rat = yield from gget(HS)
            tt("dve", at[:, 0:512], bQ[:, :], e2[:, :], ALU.mult, [RbQ, Re2], [Rat])
            yield
            PS.put((bQ, RbQ))
            FS.put((e1, Re1))
            FS.put((e2, Re2))
            ad, Rad = yield from gget(FR)
            ao, Rao = yield from gget(FR)
            tt("dve", H4(ad[:, :].bitcast(F32R)), H4(dd[:, :]), bcm(pp("bd01")), ALU.mult, [Rdd, Rprm], [Rad])
            yield
            tt("dve", H4(ao[:, :].bitcast(F32R)), H4(dd[:, :]), bcm(pp("off01")), ALU.mult, [Rdd, Rprm], [Rao])
            FS.put((dd, Rdd))
            bT, RbT = yield from gget(PS)
            for h in range(4):
                tr(bT[:, hsl(h)], ad[:, hsl(h)], ident_f, [Rad, Rprm], [RbT])
            yield
            bm, Rbm = yield from gget(FR)
            pm, Rpm = yield from gget(FR)
            op("dve", "tensor_copy", [RbT], [Rbm], out=bm[:, :].bitcast(F32R), in_=bT[:, :])
            tt("dve", H4(pm[:, :].bitcast(F32R)), bcm(ident_f), H4(bT[:, :]), ALU.subtract, [Rprm, RbT], [Rpm])
            PS.put((bT, RbT))
            adr, bmr, pmr = ad[:, :].bitcast(F32R), bm[:, :].bitcast(F32R), pm[:, :].bitcast(F32R)
            bA, RbA = yield from gget(PS)
            bB, RbB = yield from gget(PS)
            bP, RbP = yield from gget(PS)
            for k in range(1, 6):
                for h in range(4):
                    mm(bA[:, hsl(h)], bmr[:, hsl(h)], adr[:, hsl(h)], True, True, [Rbm, Rad], [RbA])
                if k < 5:
                    for h in range(4):
                        mm(bB[:, hsl(h)], adr[:, hsl(h)], bmr[:, hsl(h)], True, True, [Rbm, Rad], [RbB])
                yield
                op("dve", "tensor_copy", [RbA], [Rad], out=adr, in_=bA[:, :])
                if k < 5:
                    op("dve", "tensor_copy", [RbB], [Rbm], out=bmr, in_=bB[:, :])
                for h in range(4):
                    mm(bP[:, hsl(h)], adr[:, hsl(h)], pmr[:, hsl(h)], True, True, [Rad, Rpm], [RbP])
                yield
                tt("dve", pmr, pm[:, :], bP[:, :], ALU.add, [Rpm, RbP], [Rpm])
            FR.put((ad, Rad))
            for h in range(4):
                tr(bA[:, hsl(h)], pm[:, hsl(h)], ident_f, [Rpm, Rprm], [RbA])
            for h in range(4):
                mm(bB[:, hsl(h)], ao[:, hsl(h)].bitcast(F32R), pmr[:, hsl(h)], True, True, [Rao, Rpm], [RbB])
            yield
            op("dve", "tensor_copy", [RbA], [Rbm], out=bmr, in_=bA[:, :])
            ym, Rym = yield from gget(FR)
            op("dve", "tensor_copy", [RbB], [Rym], out=ym[:, :].bitcast(F32R), in_=bB[:, :])
            FR.put((ao, Rao))
            for h in range(4):
                mm(bP[:, hsl(h)], bmr[:, hsl(h)], ym[:, hsl(h)].bitcast(F32R), True, True, [Rbm, Rym], [RbP])
            yield
            ttm, Rttm = yield from gget(HS)
            tt("dve", ttm[:, 0:512], pm[:, :], bP[:, :], ALU.subtract, [Rpm, RbP], [Rttm])
            for it in ((bm, Rbm), (pm, Rpm), (ym, Rym)):
                FR.put(it)
            rv, Rrv = yield from gget(HS)
            rk, Rrk = yield from gget(HS)
            kd, Rkd = yield from gget(HS)
            tt("pool", H4(rv[:, 0:512]), H4(vtok[:, s, :]), bc4(bet[:, s, :]), ALU.mult, [Rvtok, Rbet], [Rrv])
            tt("pool", H4(rk[:, 0:512]), H4(ktok[:, s, :]), bc4(ex[:, 0:4]), ALU.mult, [Rktok, Rex], [Rrk])
            tt("pool", H4(kd[:, 0:512]), H4(ktok[:, s, :]), bc4(ex[:, 4:8]), ALU.mult, [Rktok, Rex], [Rkd])
            for h in range(4):
                mm(bA[:, hsl(h)], ttm[:, hsl(h)], rv[:, hsl(h)], True, True, [Rttm, Rrv], [RbA])
            for h in range(4):
                mm(bB[:, hsl(h)], rk[:, hsl(h)], ttm[:, hsl(h)], True, True, [Rttm, Rrk], [RbB])
            yield
            u, Ru = yield from gget(FS)
            wT, RwT = yield from gget(HS)
            act(u[:, :], bA[:, :], AF.Copy, [RbA], [Ru])
            act(wT[:, 0:512], bB[:, :], AF.Copy, [RbB], [RwT])
            for it in ((ttm, Rttm), (rv, Rrv), (rk, Rrk)):
                HS.put(it)
            for h in range(4):
                mm(bP[:, hsl(h)], wT[:, hsl(h)], Sbf[:, l, h, :], True, True, [RwT, RS[l][h]], [RbP])
            yield
            vn, Rvn = yield from gget(HS)
            tt("dve", vn[:, 0:512], u[:, :], bP[:, :], ALU.subtract, [Ru, RbP], [Rvn])
            yield
            FS.put((u, Ru))
            for h in range(4):
                mm(bA[:, hsl(h)], qd[:, hsl(h)], Sbf[:, l, h, :], True, False, [Rqd, RS[l][h]], [RbA])
                mm(bA[:, hsl(h)], at[:, hsl(h)], vn[:, hsl(h)], False, True, [Rat, Rvn], [RbA])
            for h in range(4):
                mm(bB[:, hsl(h)], kd[:, hsl(h)], vn[:, hsl(h)], True, True, [Rkd, Rvn], [RbB])
            yield
            Sall = Sst[:, l].rearrange("p h d -> p (h d)")
            tt("dve", H4(Sall), H4(Sall), bc4(ex[:, 8:12]), ALU.mult, [Rex] + RS[l], RS[l])
            yield
            tt("dve", Sall, Sall, bB[:, :], ALU.add, RS[l] + [RbB], RS[l])
            act(Sbf[:, l].rearrange("p h d -> p (h d)"), Sall, AF.Copy, RS[l], RS[l])
            PS.put((bB, RbB))
            PS.put((bP, RbP))
            for it in ((wT, RwT), (vn, Rvn), (qd, Rqd), (at, Rat), (kd, Rkd)):
                HS.put(it)
            ssq = sm[:, 104:108]
            Rssq = smr("ssq")
            sq, Rsq = yield from gget(FS)
            act(sq[:, :], bA[:, :], AF.Square, [RbA], [Rsq])
            yield
            op("dve", "tensor_reduce", [Rsq], [Rssq], out=ssq, in_=H4(sq[:, :]), axis=AX.X, op=ALU.add)
            yield
            FS.put((sq, Rsq))
            act(ssq, ssq, AF.Ln, [Rssq], [Rssq], scale=1.0 / 128, bias=EPS)
            act(ssq, ssq, AF.Exp, [Rssq], [Rssq], scale=-0.5)
            on, Ron = yield from gget(HS)
            tt("dve", H4(on[:, 0:512]), H4(bA[:, :]), bc4(ssq), ALU.mult, [RbA, Rssq], [Ron])
            PS.put((bA, RbA))
            b, Rb = yield from gget(PS)
            bb = b[:].bitcast(BF16)
            for h in range(4):
                tr(bb[:, hsl(h)], on[:, hsl(h)], ident_b, [Ron, Rcst], [Rb])
            HS.put((on, Ron))
            yield
            stt(yT[:, 0, :, tsl], H4(bb[:, 0:512]), pp(f"anorm{l}"), sza[:, :, tsl], ALU.mult, ALU.mult,
                [Rb, Rprm, Rsza], [RyT[0]])
            PS.put((b, Rb))

        kv_ready = {}

        def attn_head(h, hh, hd, bO, RbO, Rq, Rk, Rv, stc):
            Rst = smr(f"stc{h}")
            nk = hh["nk"]
            bS, RbS = yield from gget(PS)
            mm(bS[:, 0:nk], hh["q"], hh["k"], True, True, [Rq, Rk], [RbS])
            yield
            mx, m_, negm, rsum, es, rden = (stc[:, i, h:h + 1] for i in range(6))
            p, Rp = yield from gget(HS)
            if hh["bias"] is not None:
                sc, Rsc = yield from gget(FS)
                stt(sc[:, 0:nk], bS[:, 0:nk], hh["scale"], hh["bias"], ALU.mult, ALU.add, [RbS, Rprm], [Rsc])
                PS.put((bS, RbS))
                if hh.get("premask") is not None:
                    ts("dve", sc[:, 0:128], sc[:, 0:128], hh["premask"], None, ALU.add, None, [Rsc, Rprm], [Rsc])
                if FINE:
                    yield
                op("dve", "tensor_reduce", [Rsc], [Rst], out=mx, in_=sc[:, 0:nk], axis=AX.X, op=ALU.max)
                if FINE:
                    yield
                tt("dve", m_, mx, hh["sink"], ALU.max, [Rst, Rprm], [Rst])
                if FINE:
                    yield
                ts("dve", negm, m_, -1.0, None, ALU.mult, None, [Rst], [Rst])
                if FINE:
                    yield
                act(p[:, 0:nk], sc[:, 0:nk], AF.Exp, [Rsc, Rst], [Rp, Rst], bias=negm, accum_out=rsum)
                FS.put((sc, Rsc))
                act(es, hh["sink"], AF.Exp, [Rprm, Rst], [Rst], bias=negm)
                if FINE:
                    yield
                tt("dve", rden, rsum, es, ALU.add, [Rst], [Rst])
                if FINE:
                    yield
            else:
                op("dve", "tensor_reduce", [RbS], [Rst], out=mx, in_=bS[:, 0:nk], axis=AX.X, op=ALU.max)
                if FINE:
                    yield
                ts("dve", negm, mx, -hh["scale"], None, ALU.mult, None, [Rst], [Rst])
                if FINE:
                    yield
                act(p[:, 0:nk], bS[:, 0:nk], AF.Exp, [RbS, Rst], [Rp, Rst], bias=negm, scale=hh["scale"], accum_out=rden)
                PS.put((bS, RbS))
                if FINE:
                    yield
            op("dve", "reciprocal", [Rst], [Rst], out=rden, in_=rden)
            yield
            nkc = nk // 128
            bT, RbT = yield from gget(PS)
            bTb = bT[:].bitcast(BF16)
            for kc in range(nkc):
                tr(bTb[:, kc * 128:(kc + 1) * 128], p[:, kc * 128:(kc + 1) * 128], ident_b, [Rp, Rcst], [RbT])
            HS.put((p, Rp))
            yield
            pT, RpT = yield from gget(HS)
            act(pT[:, 0:nk], bTb[:, 0:nk], AF.Copy, [RbT], [RpT])
            PS.put((bT, RbT))
            for kc in range(nkc):
                mm(bO[:, h * hd:(h + 1) * hd], pT[:, kc * 128:(kc + 1) * 128], hh["v"][kc], kc == 0, kc == nkc - 1,
                   [RpT, Rv], [RbO])
            HS.put((pT, RpT))
            yield

        def attention(heads, hd, Rq, Rk, Rv, ydst, Ry, sz, Rsz, tsl):
            nh = len(heads)
            bO, RbO = yield from gget(PS)
            stc = sm[:, 112:112 + 6 * 8].rearrange("p (k h) -> p k h", k=6)
            yield from pipeline([attn_head(h, hh, hd, bO, RbO, Rq, Rk, Rv, stc) for h, hh in enumerate(heads)], W_ATT)
            on, Ron = yield from gget(HS)
            rd = stc[:, 5, 0:nh]
            tt("dve", on[:, 0:512].rearrange("p (h d) -> p h d", h=nh), bO[:, :].rearrange("p (h d) -> p h d", h=nh),
               rd.unsqueeze(2).broadcast_to([128, nh, hd]), ALU.mult, [RbO] + [smr(f"stc{h}") for h in range(nh)], [Ron])
            PS.put((bO, RbO))
            b, Rb = yield from gget(PS)
            bb = b[:].bitcast(BF16)
            for c in range(4):
                tr(bb[:, c * 128:(c + 1) * 128], on[:, c * 128:(c + 1) * 128], ident_b, [Ron, Rcst], [Rb])
            HS.put((on, Ron))
            yield
            for c in range(4):
                tt("dve", ydst[:, c, tsl], bb[:, c * 128:(c + 1) * 128], sz[:, c, tsl], ALU.mult, [Rb, Rsz], [Ry])
            PS.put((b, Rb))

        def stage_C_kv(l, ti, wt, Rw):
            def ev(b, Rb):
                act(kTc[:, l, 128:640], b[:, :], AF.Identity, [Rb, Rprm], [Rkv[l]], bias=pp(f"bfm{l}", 32, 33))
            yield from proj(wt, Rw, 0, 128, ev)
            for s in range(4):
                b, Rb = yield from gget(PS)
                for kc in range(8):
                    mm(b[:, 0:128], hT[:, kc, s * 128:(s + 1) * 128], wt[:, kc, 128:256], kc == 0, kc == 7, [Rw, RhT], [Rb])
                yield
                tt("dve", vtc[:, l, 1 + s, :], b[:, 0:128], pp(f"bv{l}"), ALU.add, [Rb, Rprm], [Rkv[l]])
                PS.put((b, Rb))

        def gated_fm(l, ws, chbase, dst, Rdst, func):
            wt, Rw = yield from ws.take()
            for c in range(4):
                def ev(b, Rb, c=c):
                    act(dst[:, c, :], b[:, :], func, [Rb, Rprm], [Rdst], bias=pp(f"bfm{l}", chbase + c, chbase + c + 1))
                yield from proj(wt, Rw, c * 128, 128, ev)
            WB.put((wt, Rw))

        def stage_C(l, ti, ws):
            qT, RqT = yield from gget(GS)
            szc, Rszc = yield from gget(GS)
            yield from gated_fm(l, ws, 28, qT, RqT, AF.Identity)
            yield from gated_fm(l, ws, 36, szc, Rszc, AF.Silu)
            while not kv_ready.get((l, ti)):
                yield
            so, _sw = poff[f"sinks{l}"]
            for s in range(4):
                first = (ti == 0 and s == 0)
                heads = []
                for h in range(8):
                    c, base = h % 4, (h // 4) * 64
                    kvh = h // 4
                    if first:
                        k_ap, nk, bias = kTc[base:base + 64, l, 128:256], 128, bandv[:, h, 128:256]
                        v = [vtc[:, l, 1, kvh * 64:(kvh + 1) * 64]]
                    else:
                        k_ap, nk, bias = kTc[base:base + 64, l, s * 128:s * 128 + 256], 256, bandv[:, h, :]
                        v = [vtc[:, l, s, kvh * 64:(kvh + 1) * 64], vtc[:, l, s + 1, kvh * 64:(kvh + 1) * 64]]
                    pm_ = None
                    if pipe and ti == 1 and s == 0:
                        fo_, _ = poff["flags"]
                        pm_ = prm[:, fo_ + 2:fo_ + 3]
                    heads.append(dict(q=qT[base:base + 64, c, s * 128:(s + 1) * 128], k=k_ap, nk=nk, bias=bias,
                                      scale=0.125, sink=prm[:, so + h:so + h + 1], v=v, premask=pm_))
                yield from attention(heads, 64, RqT, Rkv[l], Rkv[l], yT[:, 2], RyT[2], szc, Rszc, slice(s * 128, (s + 1) * 128))
            op("pool", "tensor_copy", [Rkv[l]], [Rkv[l]], out=kTc[:, l, 0:128], in_=kTc[:, l, 512:640])
            op("pool", "tensor_copy", [Rkv[l]], [Rkv[l]], out=vtc[:, l, 0, :], in_=vtc[:, l, 4, :])
            GS.put((qT, RqT))
            GS.put((szc, Rszc))

        def stage_E(l, ti, ws):
            eq, Req = yield from gget(GS)
            sze, Rsze = yield from gget(GS)
            yield from gated_fm(l, ws, 48, eq, Req, AF.Identity)
            yield from gated_fm(l, ws, 52, sze, Rsze, AF.Silu)
            while not mem_ready.get(l):
                yield
            for s in range(4):
                tsl = slice(s * 128, (s + 1) * 128)
                heads = [dict(q=eq[:, h, tsl], k=mkT[:, l, h, :], nk=256, bias=None, scale=128 ** -0.5, sink=None,
                              v=[mvt[:, l, 0, h * 128:(h + 1) * 128], mvt[:, l, 1, h * 128:(h + 1) * 128]])
                         for h in range(4)]
                yield from attention(heads, 128, Req, Rmem[l], Rmem[l], yT[:, 4], RyT[4], sze, Rsze, tsl)
            GS.put((eq, Req))
            GS.put((sze, Rsze))

        def convB_item(l, c, wa, Rwa, wb, Rwb, cvv, Rcv):
            bwo, _ = poff[f"bdw{l}"]
            sg, Rsg = yield from gget(HS)

            def evb(b, Rb):
                act(sg[:, 0:512], b[:, :], AF.Sigmoid, [Rb, Rprm], [Rsg], bias=pp(f"bfm{l}", 20 + c, 21 + c))
            yield from proj(wb, Rwb, c * 128, 128, evb)
            pre, Rpre = yield from gget(HS)
            op("pool", "tensor_copy", [Rhalo[l]], [Rpre], out=pre[:, 0:30], in_=haloB[:, l, c, 0:30])

            def eva(b, Rb):
                stt(pre[:, 30:542], b[:, :], pp(f"bfm{l}", 16 + c, 17 + c), sg[:, 0:512], ALU.add, ALU.mult,
                    [Rb, Rprm, Rsg], [Rpre])
            yield from proj(wa, Rwa, c * 128, 128, eva)
            HS.put((sg, Rsg))
            op("pool", "tensor_copy", [Rpre], [Rhalo[l]], out=haloB[:, l, c, 0:30], in_=pre[:, 512:542])
            b, Rb = yield from gget(PS)
            for k in range(31):
                dgk, Rdgk = yield from gget(S16)
                ts("pool", dgk[:, :], ident_b, prm[:, bwo + c * 31 + k:bwo + c * 31 + k + 1], 0.0, ALU.mult, ALU.add,
                   [Rcst, Rprm], [Rdgk])
                mm(b[:, :], dgk[:, :], pre[:, k:k + 512], k == 0, k == 30, [Rdgk, Rpre], [Rb])
                S16.put((dgk, Rdgk))
                if k % 4 == 3:
                    yield
            HS.put((pre, Rpre))
            yield
            act(cvv[c], b[:, :], AF.Identity, [Rb, Rprm], [Rcv[c]], bias=pp(f"bdwb{l}", c, c + 1))
            PS.put((b, Rb))

        def stage_B(l, ti, ws):
            szb, Rszb = yield from gget(GS)
            g0, Rg0 = yield from gget(GS)
            g1, Rg1 = yield from gget(GS)
            yield from gated_fm(l, ws, 24, szb, Rszb, AF.Silu)
            wa, Rwa = yield from ws.take()
            wb, Rwb = yield from ws.take()
            g0f = g0[:].rearrange("p a b -> p (a b)").bitcast(F32)
            g1f = g1[:].rearrange("p a b -> p (a b)").bitcast(F32)
            cvv = [g0f[:, 0:512], g0f[:, 512:1024], g1f[:, 0:512], g1f[:, 512:1024]]
            Rcv = [Rg0, Rg0, Rg1, Rg1]
            yield from pipeline([convB_item(l, c, wa, Rwa, wb, Rwb, cvv, Rcv) for c in range(4)], W_B)
            WB.put((wa, Rwa))
            WB.put((wb, Rwb))
            bM, RbM = yield from gget(PS)
            bQ, RbQ = yield from gget(PS)
            for c in range(4):
                mm(bM[:, :], pp("ones"), cvv[c], c == 0, c == 3, [Rprm, Rcv[c]], [RbM])
            for c in range(4):
                sq, Rsq = yield from gget(FS)
                act(sq[:, :], cvv[c], AF.Square, [Rcv[c]], [Rsq])
                mm(bQ[:, :], pp("ones"), sq[:, :], c == 0, c == 3, [Rprm, Rsq], [RbQ])
                FS.put((sq, Rsq))
                yield
            mean, Rmean = yield from gget(FS)
            rstd, Rrstd = yield from gget(FS)
            act(mean[:, :], bM[:, :], AF.Copy, [RbM], [Rmean], scale=1.0 / 512)
            act(rstd[:, :], bM[:, :], AF.Square, [RbM], [Rrstd], scale=1.0 / 512)
            PS.put((bM, RbM))
            stt(rstd[:, :], bQ[:, :], 1.0 / 512, rstd[:, :], ALU.mult, ALU.subtract, [RbQ, Rrstd], [Rrstd])
            PS.put((bQ, RbQ))
            ts("dve", rstd[:, :], rstd[:, :], 0.0, EPS, ALU.max, ALU.add, [Rrstd], [Rrstd])
            act(rstd[:, :], rstd[:, :], AF.Ln, [Rrstd], [Rrstd])
            act(rstd[:, :], rstd[:, :], AF.Exp, [Rrstd], [Rrstd], scale=-0.5)
            yield
            for c in range(4):
                tt("dve", cvv[c], cvv[c], mean[:, :], ALU.subtract, [Rcv[c], Rmean], [Rcv[c]])
                tt("pool", cvv[c], cvv[c], rstd[:, :], ALU.mult, [Rcv[c], Rrstd], [Rcv[c]])
                bn, Rbn = yield from gget(HS)
                act(bn[:, 0:512], cvv[c], AF.Silu, [Rcv[c], Rprm], [Rbn], scale=pp(f"blng{l}", c, c + 1),
                    bias=pp(f"blnb{l}", c, c + 1))
                tt("dve", yT[:, 1, c, :], bn[:, 0:512], szb[:, c, :], ALU.mult, [Rbn, Rszb], [RyT[1]])
                HS.put((bn, Rbn))
                yield
            FS.put((mean, Rmean))
            FS.put((rstd, Rrstd))
            for it in ((szb, Rszb), (g0, Rg0), (g1, Rg1)):
                GS.put(it)

        def lru_item(l, c, wt, Rw, szd, Rszd):
            dwo, _ = poff[f"dconv{l}"]
            pre, Rpre = yield from gget(HS)
            op("pool", "tensor_copy", [Rhalo[l]], [Rpre], out=pre[:, 0:3], in_=haloD[:, l, c, 0:3])

            def ev(b, Rb):
                act(pre[:, 3:515], b[:, :], AF.Identity, [Rb, Rprm], [Rpre], bias=pp(f"bfm{l}", 40 + c, 41 + c))
            yield from proj(wt, Rw, c * 128, 128, ev)
            op("pool", "tensor_copy", [Rpre], [Rhalo[l]], out=haloD[:, l, c, 0:3], in_=pre[:, 512:515])
            dg, Rdg = yield from gget(DG)
            for k in range(4):
                ts("pool", dg[:, k, :], ident_b, prm[:, dwo + c * 4 + k:dwo + c * 4 + k + 1], 0.0, ALU.mult, ALU.add,
                   [Rcst, Rprm], [Rdg])
            yield
            b, Rb = yield from gget(PS)
            for k in range(4):
                mm(b[:, :], dg[:, k, :], pre[:, k:k + 512], k == 0, k == 3, [Rdg, Rpre], [Rb])
            DG.put((dg, Rdg))
            HS.put((pre, Rpre))
            yield
            while len(FS.free) < 4:
                yield
            dx, Rdx = FS.get()
            r, Rr = FS.get()
            ig, Rig = FS.get()
            a, Ra = FS.get()
            dxb, Rdxb = yield from gget(HS)
            act(dx[:, :], b[:, :], AF.Identity, [Rb, Rprm], [Rdx], bias=pp(f"dconvb{l}", c, c + 1))
            PS.put((b, Rb))
            op("pool", "tensor_copy", [Rdx], [Rdxb], out=dxb[:, 0:512], in_=dx[:, :])
            yield
            bR, RbR = yield from gget(PS)
            mm(bR[:, :], bdws[:, l, 0, c, :], dxb[:, 0:512], True, True, [Rbdw, Rdxb], [RbR])
            bI, RbI = yield from gget(PS)
            mm(bI[:, :], bdws[:, l, 1, c, :], dxb[:, 0:512], True, True, [Rbdw, Rdxb], [RbI])
            HS.put((dxb, Rdxb))
            yield
            act(r[:, :], bR[:, :], AF.Sigmoid, [RbR, Rprm], [Rr], bias=pp(f"dba{l}", c, c + 1))
            act(ig[:, :], bI[:, :], AF.Sigmoid, [RbI, Rprm], [Rig], bias=pp(f"dbx{l}", c, c + 1))
            PS.put((bR, RbR))
            PS.put((bI, RbI))
            act(a[:, :], r[:, :], AF.Exp, [Rr, Rlc], [Ra], scale=lruc[:, l, c:c + 1])
            act(r[:, :], r[:, :], AF.Exp, [Rr, Rlc], [Rr], scale=lruc[:, l, 4 + c:5 + c])
            act(r[:, :], r[:, :], AF.Sqrt, [Rr], [Rr], scale=-1.0, bias=1.0)
            tt("dve", ig[:, :], ig[:, :], dx[:, :], ALU.mult, [Rig, Rdx], [Rig])
            tt("pool", ig[:, :], ig[:, :], r[:, :], ALU.mult, [Rig, Rr], [Rig])
            FS.put((dx, Rdx))
            yield
            op("dve", "tensor_tensor_scan", [Ra, Rig, Rhst[l]], [Rr], out=r[:, :], data0=a[:, :], data1=ig[:, :],
               initial=hst[:, l, c:c + 1], op0=ALU.mult, op1=ALU.add)
            op("dve", "tensor_copy", [Rr], [Rhst[l]], out=hst[:, l, c:c + 1], in_=r[:, 511:512])
            tt("dve", yT[:, 3, c, :], r[:, :], szd[:, c, :], ALU.mult, [Rr, Rszd], [RyT[3]])
            for it in ((r, Rr), (ig, Rig), (a, Ra)):
                FS.put(it)
            yield

        def stage_D(l, ti, ws):
            szd, Rszd = yield from gget(GS)
            yield from gated_fm(l, ws, 44, szd, Rszd, AF.Silu)
            wt, Rw = yield from ws.take()
            for c in range(4):
                yield from lru_item(l, c, wt, Rw, szd, Rszd)
            WB.put((wt, Rw))
            GS.put((szd, Rszd))

        def chain_rest(l, ti):
            ws = WStream([winblk(l, 12), winblk(l, 13),
                          winblk(l, 7), winblk(l, 9)])
            ws.prefetch()
            yield from stage_E(l, ti, ws)
            marks.append(("E_end", ti, l, P.cnt["pe"]))
            yield from stage_C(l, ti, ws)
            marks.append(("C_end", ti, l, P.cnt["pe"]))

        def chain_bd(l, ti):
            ws = WStream([winblk(l, 11), winblk(l, 10),
                          winblk(l, 6), winblk(l, 4), winblk(l, 5)])
            yield from stage_D(l, ti, ws)
            marks.append(("D_end", ti, l, P.cnt["pe"]))
            yield from stage_B(l, ti, ws)
            marks.append(("B_end", ti, l, P.cnt["pe"]))

        def merge_item(l, n, j, jj, wg, Rwg, wr, Rwr, mgf, Rmgs):
            gsb, Rgsb = yield from gget(HS)

            def ev(b, Rb):
                act(gsb[:, 0:512], b[:, :], AF.Sigmoid, [Rb, Rprm], [Rgsb], bias=pp(f"bfm{l}", 56 + n * 8 + j, 57 + n * 8 + j))
            yield from proj(wg, Rwg, jj * 128, 128, ev)
            b, Rb = yield from gget(PS)
            for kc in range(4):
                mm(b[:, :], wr[:, kc, jj * 128:(jj + 1) * 128], yT[:, n, kc, :], kc == 0, kc == 3, [Rwr, RyT[n]], [Rb])
            yield
            if n == 0:
                tt("dve", mgf[j], b[:, :], gsb[:, 0:512], ALU.mult, [Rb, Rgsb], [Rmgs[j]])
            else:
                tmp, Rtmp = yield from gget(FS)
                tt("dve", tmp[:, :], b[:, :], gsb[:, 0:512], ALU.mult, [Rb, Rgsb], [Rtmp])
                tt("pool", mgf[j], mgf[j], tmp[:, :], ALU.add, [Rmgs[j], Rtmp], [Rmgs[j]])
                FS.put((tmp, Rtmp))
            PS.put((b, Rb))
            HS.put((gsb, Rgsb))
            if n == 4:
                act(mgb[:, j, :], mgf[j], AF.Copy, [Rmgs[j]], [Rmgb])
            yield

        def stage_merge(l, ti):
            mgs = []
            for i in range(4):
                mgs.append((yield from gget(GS)))
            mgf, Rmgs = [], []
            for i in range(4):
                f = mgs[i][0][:].rearrange("p a b -> p (a b)").bitcast(F32)
                mgf += [f[:, 0:512], f[:, 512:1024]]
                Rmgs += [Reg(f"mgs{2 * i}"), Reg(f"mgs{2 * i + 1}")]
                for r_ in Rmgs[-2:]:
                    r_.w, r_.r = mgs[i][1].w, list(mgs[i][1].r)
            ws = WStream([winblk(l, 14 + q) for q in range(10)])
            ws.prefetch()
            for n in range(5):
                for half in range(2):
                    wr, Rwr = yield from gget(GS)
                    kb = ("b", l * 10 + n * 2 + half)
                    if kb not in scr_seen:
                        scr_seen.add(kb)
                        Rscr[kb] = Reg(f"scrb{kb[1]}")
                        dma("pool", wr[:, :, :], wbr_d[l, n][:, half * 512:(half + 1) * 512].rearrange("(kc p) d -> p kc d", p=128),
                            W=[Rwr])
                        dma("sp", scrb_d[kb[1]], wr[:].rearrange("p a b -> p (a b)"), R=[Rwr], W=[Rscr[kb]])
                    else:
                        dma("sp", wr[:].rearrange("p a b -> p (a b)"), scrb_d[kb[1]], R=[Rscr[kb]], W=[Rwr])
                    wg, Rwg = yield from ws.take()
                    yield from pipeline([merge_item(l, n, half * 4 + jj, jj, wg, Rwg, wr, Rwr, mgf, Rmgs) for jj in range(4)], W_MRG)
                    WB.put((wg, Rwg))
                    GS.put((wr, Rwr))
            for i in range(4):
                R_ = mgs[i][1]
                R_.w = Rmgs[2 * i + 1].w
                R_.r = list(Rmgs[2 * i].r) + list(Rmgs[2 * i + 1].r) + ([Rmgs[2 * i].w] if Rmgs[2 * i].w else [])
                GS.put(mgs[i])

        def out_item(l, s, wo, gp):
            inv = sm[:, 0:4]
            Rinv = smr("inv")
            bs = []
            for half in range(2):
                b, Rb = yield from gget(PS)
                wt, Rw = wo[half]
                for kc in range(8):
                    mm(b[:, :], mgb[:, kc, s * 128:(s + 1) * 128], wt[:, kc, :], kc == 0, kc == 7, [Rmgb, Rw], [Rb])
                bs.append((b, Rb))
                yield
            ss = sm[:, 160 + 4 * s:164 + 4 * s]
            Rss = smr(f"oss{s}")
            for half in range(2):
                jt, Rj = yield from gget(HS)
                act(jt[:, 0:512], bs[half][0][:, :], AF.Square, [bs[half][1]], [Rj, Rss], accum_out=ss[:, half:half + 1])
                HS.put((jt, Rj))
            tt("dve", ss[:, 2:3], ss[:, 0:1], ss[:, 1:2], ALU.add, [Rss], [Rss])
            act(ss[:, 2:3], ss[:, 2:3], AF.Ln, [Rss], [Rss], scale=1.0 / D_MODEL, bias=EPS)
            act(ss[:, 3:4], ss[:, 2:3], AF.Exp, [Rss], [Rss], scale=-0.5)
            yield
            for half in range(2):
                b, Rb = bs[half]
                tmp, Rtmp = yield from gget(FS)
                stt(tmp[:, :], b[:, :], ss[:, 3:4], gp[half][0][:, :], ALU.mult, ALU.mult, [Rb, Rss, gp[half][1]], [Rtmp])
                PS.put((b, Rb))
                xs = xt[:, s, half * 512:(half + 1) * 512]
                stt(xs, xs, inv[:, s:s + 1], tmp[:, :], ALU.mult, ALU.add, [Rxts[s], Rinv, Rtmp], [Rxts[s]])
                FS.put((tmp, Rtmp))
                yield

        def stage_out(l, ti):
            gp = []
            for half in range(2):
                g_, Rg_ = yield from gget(FS)
                dma("sp", g_[:, :], gpost_d[l:l + 1, half * 512:(half + 1) * 512].broadcast_to([128, 512]), W=[Rg_])
                gp.append((g_, Rg_))
            ws = WStream([(wout_d[l][:, 0:512], 512, l * 26 + 24), (wout_d[l][:, 512:1024], 512, l * 26 + 25)])
            wo = []
            for half in range(2):
                wo.append((yield from ws.take()))
            for s in range(4):
                Rxts[s].w, Rxts[s].r = Rxt.w, list(Rxt.r)
            yield from pipeline([out_item(l, s, wo, gp) for s in range(4)], 2)
            Rxt.w = Rxts[3].w
            Rxt.r = [t for s in range(4) for t in Rxts[s].r] + [Rxts[s].w for s in range(3)]
            for it in wo:
                WB.put(it)
            for it in gp:
                FS.put(it)

        Rxts = [Reg(f"xts{s}") for s in range(4)]
        marks = []

        out_toks = []

        def one_layer(l, ti, extra=()):
            marks.append(("norm", ti, l, P.cnt["pe"]))
            run(norm_T(xt, Rxt, 4, f"gpre{l}", hT, RhT, sm[:, 0:4], smr("inv")))
            marks.append(("branches", ti, l, P.cnt["pe"]))
            run(par(chain_A(l, ti), chain_rest(l, ti), chain_bd(l, ti), *extra))
            marks.append(("merge", ti, l, P.cnt["pe"]))
            run(stage_merge(l, ti))
            marks.append(("out", ti, l, P.cnt["pe"]))
            run(stage_out(l, ti))

        if not pipe:
            for ti in range(NT):
                dma("sp", xt[:], x_d[ti * 512:(ti + 1) * 512, :].rearrange("(s p) d -> p s d", p=128), W=[Rxt])
                for l in range(L):
                    one_layer(l, ti)
                    if ti == NT - 1 and l == 0 and "yT" in tap_d:
                        dma("pool", tap_d["yT"], yT[:], R=RyT)
                out_toks.append(dma("sp", out_d[ti * 512:(ti + 1) * 512, :].rearrange("(s p) d -> p s d", p=128), xt[:], R=[Rxt]))
        else:
            assert L == 1
            send_d = nc.dram_tensor("pp_send", [512, D_MODEL], F32)
            recv_d = nc.dram_tensor("pp_recv", [1024, D_MODEL], F32)
            Rsend, Rrecv = Reg("pp_send"), Reg("pp_recv")
            fo, _fw = poff["flags"]
            fA, fB = prm[:, fo:fo + 1], prm[:, fo + 1:fo + 2]
            klo, khi = prm[:, fo + 4:fo + 5], prm[:, fo + 5:fo + 6]
            gsl = None
            for step in range(NT + 1):
                ti_in = min(step, NT - 1)
                xsrc = x_d[ti_in * 512:(ti_in + 1) * 512, :].rearrange("(s p) d -> p s d", p=128)
                if step == 0:
                    dma("sp", xt[:], xsrc, W=[Rxt])
                else:
                    dma("sp", xt[:], recv_d.ap()[0:512, :].rearrange("(s p) d -> p s d", p=128), R=[Rrecv], W=[Rxt])
                    for s_ in range(4):
                        g_, Rg_ = gsl[s_]
                        gf = g_[:].rearrange("p a b -> p (a b)").bitcast(F32)
                        ts("dve", xt[:, s_, :], xt[:, s_, :], fB, None, ALU.mult, None, [Rxt, Rprm], [Rxt])
                        stt(xt[:, s_, :], gf, fA, xt[:, s_, :], ALU.mult, ALU.add, [Rg_, Rprm, Rxt], [Rxt])
                    for it in gsl:
                        GS.put(it)
                one_layer(0, step, extra=([mem_phase()] if step == 0 else ()))
                if step == 0:
                    def wipe(ap, R):
                        ts("dve", ap, ap, klo, khi, ALU.max, ALU.min, list(R) + [Rprm], list(R))
                    wipe(Sst[:, 0].rearrange("p h d -> p (h d)"), RS[0])
                    wipe(Sbf[:, 0].rearrange("p h d -> p (h d)"), RS[0])
                    wipe(hst[:, 0, :], [Rhst[0]])
                    wipe(haloA[:, 0].rearrange("p c k -> p (c k)"), [Rhalo[0]])
                    wipe(haloB[:, 0].rearrange("p c k -> p (c k)"), [Rhalo[0]])
                    wipe(haloD[:, 0].rearrange("p c k -> p (c k)"), [Rhalo[0]])
                    wipe(kTc[:, 0, 0:128], [Rkv[0]])
                    wipe(vtc[:, 0, 0, :], [Rkv[0]])
                if step < NT:
                    dma("sp", send_d.ap().rearrange("(s p) d -> p s d", p=128), xt[:], R=[Rxt], W=[Rsend])
                    if os.environ.get("PIPE_NOCC"):
                        dma("sp", recv_d.ap()[0:512, :], send_d.ap(), R=[Rsend], W=[Rrecv])
                    else:
                        P.emit("pool", lambda e: e.collective_compute(
                            "AllGather", ALU.bypass, replica_groups=[[0, 1], [2, 3], [4, 5], [6, 7]],
                            ins=[send_d.ap().opt()], outs=[recv_d.ap().opt()]),
                            reads=[Rsend], writes=[Rrecv], cc=True, cost=40000.0)
                extra = [Rsend] if step < NT else []
                if step >= 1:
                    to = step - 1
                    out_toks.append(dma("sp", out_d[to * 512:(to + 1) * 512, :].rearrange("(s p) d -> p s d", p=128), xt[:],
                                        R=[Rxt] + extra))
                if step < NT:
                    tn = min(step + 1, NT - 1)
                    xn = x_d[tn * 512:(tn + 1) * 512, :].rearrange("(s p) d -> p s d", p=128)
                    gsl = [GS.get() for _ in range(4)]
                    for s_ in range(4):
                        g_, Rg_ = gsl[s_]
                        dma("sp", g_[:].rearrange("p a b -> p (a b)").bitcast(F32), xn[:, s_, :], R=extra, W=[Rg_])
        P.finish_wait("sp", out_toks)
        P.run()
        build.stats = dict(cnt=dict(P.cnt), dmas=P.dma_n, marks=marks, model=dict(P.eng_time), busy=dict(P.busy), stall=dict(P.stall))
    return nc


_NC_CACHE = {}


_PER_LAYER = ("g_pre", "g_post", "w_in", "b_in", "a_conv_w", "a_log", "a_dt_bias", "a_norm_g", "b_dw_w", "b_dw_b",
              "b_ln_g", "b_ln_b", "c_sinks", "d_conv_w", "d_conv_b", "d_w_a", "d_b_a", "d_w_x", "d_b_x", "d_lambda",
              "g_mem", "w_mem_kv", "w_br", "w_out")


def kernel(**inputs):
    inp = {k: np.asarray(v) for k, v in inputs.items()}
    B, T, _ = inp["x"].shape
    L = inp["g_pre"].shape[0]
    assert L == 2 and 2 * B <= 8
    key = (T, "pipe")
    if key not in _NC_CACHE:
        _NC_CACHE[key] = build(T, 1, pipe=True)
    nc = _NC_CACHE[key]
    per_stage = []
    for l in range(L):
        inp_l = {k: (v[l:l + 1] if k in _PER_LAYER else v) for k, v in inp.items()}
        prm, winp, bdw, g_post = _host_params(inp_l, 1, stage=l)
        per_stage.append(dict(winp=winp, wbr=np.ascontiguousarray(inp["w_br"][l:l + 1], np.float32),
                              wout=np.ascontiguousarray(inp["w_out"][l:l + 1], np.float32),
                              wmem=np.ascontiguousarray(inp["w_mem_kv"][l:l + 1], np.float32),
                              prm=prm, bdw=bdw, gpost=g_post))
    zx = np.zeros((T, D_MODEL), np.float32)
    in_maps = []
    for b in range(B):
        for stage in range(2):
            m = dict(per_stage[stage])
            m["x"] = np.ascontiguousarray(inp["x"][b], np.float32)
            m["mem"] = np.ascontiguousarray(inp["mem"][b], np.float32)
            in_maps.append(m)
    res = run_bass_kernel_spmd(nc, in_maps, core_ids=list(range(2 * B)))
    kernel.last_all = [np.asarray(r["out"]) for r in res.results]
    return np.stack([np.asarray(res.results[2 * b + 1]["out"]) for b in range(B)], axis=0).astype(np.float32)
```

```python
import os
import numpy as np
from contextlib import ExitStack
import concourse.bass as bass
import concourse.mybir as mybir
from concourse.bass_utils import run_bass_kernel_spmd

F32 = mybir.dt.float32
F32R = mybir.dt.float32r
BF16 = mybir.dt.bfloat16
AF = mybir.ActivationFunctionType
ALU = mybir.AluOpType
AX = mybir.AxisListType

D_MODEL = 1024
SEQ = 4096
DEPTH = 2
MEM_LEN = 256
IN_COLS = 12040
NCOLP = 12288
EPS = 1e-6
NEG = -30000.0

ENGS = ("pe", "act", "dve", "pool", "sp")
EPOCH = 12000
FINE = bool(int(os.environ.get('FINE', 1)))
WAW_SELF = bool(int(os.environ.get('WAW_SELF', 0)))
W_CONVA = int(os.environ.get('W_CONVA', 2))
W_GDN = int(os.environ.get('W_GDN', 2))
W_ATT = int(os.environ.get('W_ATT', 3))
W_MRG = int(os.environ.get('W_MRG', 3))
W_B = int(os.environ.get('W_B', 2))
NDMASEM = 24


class Reg:
    __slots__ = ("name", "w", "r")

    def __init__(self, name):
        self.name = name
        self.w = None
        self.r = []


class Prog:
    def __init__(self, nc, stack):
        self.nc = nc
        self.stack = stack
        self.q = {e: [] for e in ENGS}
        self.cnt = {e: 0 for e in ENGS}
        self.sems = {e: [] for e in ENGS}
        self.seen = {e: {} for e in ENGS}
        self.dma_sems = [stack.enter_context(nc.semaphore(f"dq{i}")) for i in range(NDMASEM)]
        self.dma_n = 0
        self.dma_tok = {}
        self.cc_sems = [stack.enter_context(nc.semaphore(f"cc{i}")) for i in range(2)]
        self.cc_n = 0
        self.cc_tok = {}
        self.eng_time = {e: 0.0 for e in ENGS}
        self.step_fin = 0.0

    def _sem(self, e, epoch):
        while len(self.sems[e]) <= epoch:
            self.sems[e].append(self.stack.enter_context(
                self.nc.semaphore(f"s_{e}_{len(self.sems[e])}")))
        return self.sems[e][epoch]

    def _need(self, e, tok, waits):
        key, val = tok[0], tok[1]
        se = self.seen[e]
        if se.get(key, 0) >= val:
            return
        se[key] = val
        waits[key] = max(waits.get(key, 0), val)
        for k2, v2 in tok[5].items():
            if se.get(k2, 0) < v2:
                se[k2] = v2

    def _wl(self, waits):
        wl = []
        for key, val in waits.items():
            if key[0] == "d":
                wl.append((self.dma_sems[key[1]], val))
            elif key[0] == "c":
                wl.append((self.cc_sems[key[1]], val))
            else:
                wl.append((self._sem(key[0], key[1]), val))
        return wl

    def emit(self, e, fn, reads=(), writes=(), dma=False, selfdep=True, cost=300.0, cc=False):
        waits = {}
        ready = 0.0
        for r in reads:
            t = r.w
            if t is not None:
                ready = max(ready, t[4])
                if (selfdep or t[2] != e or dma or t[3]):
                    self._need(e, t, waits)
        for w in writes:
            t = w.w
            if t is not None:
                ready = max(ready, t[4])
                if (t[2] != e or dma or t[3] or (selfdep and WAW_SELF)):
                    self._need(e, t, waits)
            for t in w.r:
                ready = max(ready, t[4])
                if t[2] != e or dma or t[3]:
                    self._need(e, t, waits)
        cost = cost * float(os.environ.get("CS_" + ("dma" if dma else e), 1.0))
        if dma:
            t0 = max(ready + 100.0, self.eng_time[e])
            self.eng_time[e] = t0 + 60.0
            fin = t0 + cost
        else:
            t0 = max(ready + float(os.environ.get('CS_lat', 150.0)), self.eng_time[e]) if ready > self.eng_time[e] - 1e-9 and waits else max(ready, self.eng_time[e])
            fin = t0 + cost
            self.eng_time[e] = fin
        self.step_fin = max(self.step_fin, fin)
        self.busy = getattr(self, "busy", {})
        self.busy[e] = self.busy.get(e, 0.0) + cost
        self.stall = getattr(self, "stall", {})
        self.stall[e] = self.stall.get(e, 0.0) + max(0.0, t0 - max(self.eng_time[e] - (cost if not dma else 60.0), 0.0))
        if cc:
            k = self.cc_n
            self.cc_n += 1
            si = k % 2
            val = k // 2 + 1
            if k >= 2:
                self._need(e, self.cc_tok[k - 2], waits)
            tok = (("c", si), val, e, True, fin, dict(self.seen[e]))
            self.cc_tok[k] = tok
            inc = (self.cc_sems[si], 1)
        elif dma:
            k = self.dma_n
            self.dma_n += 1
            si = k % NDMASEM
            val = 16 * (k // NDMASEM + 1)
            if k >= NDMASEM:
                self._need(e, self.dma_tok[k - NDMASEM], waits)
            tok = (("d", si), val, e, True, fin, dict(self.seen[e]))
            self.dma_tok[k] = tok
            inc = (self.dma_sems[si], 16)
        else:
            self.cnt[e] += 1
            c = self.cnt[e]
            epoch, val = (c - 1) // EPOCH, (c - 1) % EPOCH + 1
            tok = ((e, epoch), val, e, False, fin, dict(self.seen[e]))
            inc = (self._sem(e, epoch), 1)
        self.q[e].append((self._wl(waits), fn, inc))
        for r in reads:
            if not r.r or r.r[-1] is not tok:
                r.r.append(tok)
        for w in writes:
            w.w = tok
            w.r = []
        return tok

    def finish_wait(self, e, toks):
        waits = {}
        for t in toks:
            self._need(e, t, waits)
        self.q[e].append((self._wl(waits), None, None))

    def run(self):
        nc = self.nc
        with nc.Block() as block:
            def play(eng, items):
                for wl, fn, inc in items:
                    for s, v in wl:
                        eng.wait_ge(s, v)
                    if fn is not None:
                        fn(eng).then_inc(inc[0], inc[1])

            @block.tensor
            def _(eng):
                play(eng, self.q["pe"])

            @block.scalar
            def _(eng):
                play(eng, self.q["act"])

            @block.vector
            def _(eng):
                play(eng, self.q["dve"])

            @block.gpsimd
            def _(eng):
                play(eng, self.q["pool"])

            @block.sync
            def _(eng):
                play(eng, self.q["sp"])


class Slots:
    def __init__(self, items):
        self.free = list(items)
        self.n = len(items)

    def get(self):
        if not self.free:
            raise RuntimeError("slot pool exhausted")
        return self.free.pop(0)

    def put(self, it):
        self.free.append(it)


def _win_perm():
    p = list(range(0, 2048))
    p += list(range(2056, 3592))
    cq0 = 3592
    for c in range(4):
        p += list(range(cq0 + c * 64, cq0 + (c + 1) * 64))
        p += list(range(cq0 + (4 + c) * 64, cq0 + (5 + c) * 64))
    p += list(range(4104, 4360)) + list(range(2048, 2056)) + [-1] * 248
    p += list(range(4360, 12040))
    p = np.array(p, dtype=np.int64)
    assert p.size == NCOLP
    return p


def _t5_bucket(dist):
    n = np.maximum(dist, 0)
    max_exact = 16
    large = max_exact + (np.log(np.maximum(n, 1) / max_exact) / np.log(128 / max_exact) * (32 - max_exact)).astype(np.int32)
    large = np.minimum(large, 31)
    return np.where(n < max_exact, n, large).astype(np.int32)


def _param_layout(L):
    lay = [("ident", 128), ("triu", 128), ("masku", 128), ("masklneg", 128), ("bd01", 128), ("off01", 128),
           ("ones", 128), ("maskc", 256), ("bandb", 2048), ("flags", 6)]
    for l in range(L):
        lay += [(f"gpre{l}", 8), (f"gmem{l}", 8), (f"bfm{l}", 96), (f"bba{l}", 8), (f"bv{l}", 128),
                (f"aconv{l}", 48), (f"alog{l}", 4), (f"adt{l}", 4), (f"anorm{l}", 1),
                (f"bdw{l}", 124), (f"bdwb{l}", 4), (f"blng{l}", 4), (f"blnb{l}", 4),
                (f"sinks{l}", 8), (f"dconv{l}", 16), (f"dconvb{l}", 4), (f"dba{l}", 4), (f"dbx{l}", 4),
                (f"dlam{l}", 4)]
    off = {}
    o = 0
    for n, w in lay:
        off[n] = (o, w)
        o += w
    return off, o


def _host_params(inp, L, stage=0):
    off, tot = _param_layout(L)
    prm = np.zeros((128, tot), np.float32)
    fA, fB = (1.0, 0.0) if stage == 0 else (0.0, 1.0)
    o_, _w = off["flags"]
    big = 3.0e38 if stage == 0 else 0.0
    prm[:, o_:o_ + 6] = np.array([fA, fB, NEG if stage == 1 else 0.0, fA, -big, big], np.float32)[None, :]

    def put(name, a):
        o, w = off[name]
        prm[:, o:o + w] = np.asarray(a, np.float32).reshape(128, w)

    def fm(v, c):
        return np.asarray(v).reshape(c, 128).T

    def row(v):
        v = np.asarray(v).reshape(1, -1)
        return np.broadcast_to(v, (128, v.shape[1]))

    i = np.arange(128)
    put("ident", np.eye(128))
    put("triu", (i[:, None] <= i[None, :]))
    put("masku", np.where(i[None, :] >= i[:, None], 0.0, NEG))
    put("masklneg", np.where(i[:, None] > i[None, :], 0.0, NEG))
    blk = (i[:, None] // 64) == (i[None, :] // 64)
    put("bd01", blk)
    put("off01", ~blk)
    put("ones", np.ones((128, 128)))
    ii = np.arange(128)[:, None]
    jj = np.arange(256)[None, :]
    dist = ii + 128 - jj
    put("maskc", np.where((dist >= 0) & (dist < 128), 0.0, NEG))
    bucket = _t5_bucket(dist)
    bb = inp["rel_bias"][bucket]
    put("bandb", np.transpose(bb, (0, 2, 1)))
    perm = _win_perm()
    for l in range(L):
        put(f"gpre{l}", fm(inp["g_pre"][l], 8))
        put(f"gmem{l}", fm(inp["g_mem"][l], 8))
        b = inp["b_in"][l]
        bp = np.where(perm >= 0, b[np.maximum(perm, 0)], 0.0)
        put(f"bfm{l}", fm(bp, 96))
        put(f"bba{l}", row(b[2048:2056]))
        put(f"bv{l}", row(b[4232:4360]))
        put(f"aconv{l}", np.transpose(inp["a_conv_w"][l].reshape(4, 12, 128), (2, 1, 0)))
        put(f"alog{l}", row(inp["a_log"][l]))
        put(f"adt{l}", row(inp["a_dt_bias"][l]))
        put(f"anorm{l}", inp["a_norm_g"][l].reshape(128, 1))
        put(f"bdw{l}", np.transpose(inp["b_dw_w"][l].reshape(31, 4, 128), (2, 1, 0)))
        put(f"bdwb{l}", fm(inp["b_dw_b"][l], 4))
        put(f"blng{l}", fm(inp["b_ln_g"][l], 4))
        put(f"blnb{l}", fm(inp["b_ln_b"][l], 4))
        put(f"sinks{l}", row(inp["c_sinks"][l]))
        put(f"dconv{l}", np.transpose(inp["d_conv_w"][l].reshape(4, 4, 128), (2, 1, 0)))
        put(f"dconvb{l}", fm(inp["d_conv_b"][l], 4))
        put(f"dba{l}", fm(inp["d_b_a"][l], 4))
        put(f"dbx{l}", fm(inp["d_b_x"][l], 4))
        put(f"dlam{l}", fm(inp["d_lambda"][l], 4))
    w_in = inp["w_in"][:L]
    winp = np.zeros((L, D_MODEL, NCOLP), np.float32)
    valid = perm >= 0
    winp[:, :, valid] = w_in[:, :, perm[valid]]
    bdw = np.zeros((L, 2, 4, 128, 128), np.float32)
    for l in range(L):
        for t, nm in enumerate(("d_w_a", "d_w_x")):
            w = inp[nm][l]
            for c in range(4):
                bdw[l, t, c, 0:64, 0:64] = w[2 * c]
                bdw[l, t, c, 64:128, 64:128] = w[2 * c + 1]
    g_post = np.ascontiguousarray(inp["g_post"][:L], np.float32)
    return prm, winp, bdw, g_post


def build(T, L, taps=(), pipe=False):
    NT = T // 512
    nc = bass.Bass("TRN2", target_bir_lowering=False)
    poff, ptot = _param_layout(L)
    x_d = nc.dram_tensor("x", [T, D_MODEL], F32, kind="ExternalInput").ap()
    mem_d = nc.dram_tensor("mem", [MEM_LEN, D_MODEL], F32, kind="ExternalInput").ap()
    win_d = nc.dram_tensor("winp", [L, D_MODEL, NCOLP], F32, kind="ExternalInput").ap()
    wbr_d = nc.dram_tensor("wbr", [L, 5, 512, D_MODEL], F32, kind="ExternalInput").ap()
    wout_d = nc.dram_tensor("wout", [L, D_MODEL, D_MODEL], F32, kind="ExternalInput").ap()
    wmem_d = nc.dram_tensor("wmem", [L, D_MODEL, D_MODEL], F32, kind="ExternalInput").ap()
    prm_d = nc.dram_tensor("prm", [128, ptot], F32, kind="ExternalInput").ap()
    bdw_d = nc.dram_tensor("bdw", [L, 2, 4, 128, 128], F32, kind="ExternalInput").ap()
    gpost_d = nc.dram_tensor("gpost", [L, D_MODEL], F32, kind="ExternalInput").ap()
    out_d = nc.dram_tensor("out", [T, D_MODEL], F32, kind="ExternalOutput").ap()
    NSCR = 26 * L
    scr_d = nc.dram_tensor("wscr", [NSCR, 128, 8 * 512], BF16, kind="Internal").ap()
    scrb_d = nc.dram_tensor("wscrb", [L * 10, 128, 4 * 512], BF16, kind="Internal").ap()
    tap_d = {}
    for name, shape in taps:
        tap_d[name] = nc.dram_tensor("tap_" + name, list(shape), F32, kind="ExternalOutput").ap()

    with ExitStack() as st:
        P = Prog(nc, st)

        def sb(name, shape, dt):
            return st.enter_context(nc.sbuf_tensor("sb_" + name, list(shape), dt))

        def _fsize(ap):
            n = 1
            for d in ap.shape[1:]:
                n *= d
            return n

        def op(eng, meth, R=(), W=(), **kw):
            o = kw.get("out", kw.get("ap"))
            n = _fsize(o) if o is not None else 128
            if eng == "pe":
                if meth == "transpose":
                    c = 64.0 + 128 * 0.5
                else:
                    nn = _fsize(kw["rhs"])
                    f = 4.0 if kw["rhs"].dtype in (F32, F32R) else 1.0
                    c = 40.0 + nn * 0.52 * f
            elif eng == "act":
                c = 220.0 + n * 0.9
            elif eng == "dve":
                c = 120.0 + n * 0.75
            else:
                c = 250.0 + n * 2.0
            return P.emit(eng, lambda e: getattr(e, meth)(**kw), reads=R, writes=W, selfdep=(eng != "pe"), cost=c)

        def mm(out, lhsT, rhs, start, stop, R, W):
            return op("pe", "matmul", R, W, out=out, lhsT=lhsT, rhs=rhs, start=start, stop=stop)

        def tr(out, in_, ident, R, W):
            return op("pe", "transpose", R, W, out=out, in_=in_, identity=ident)

        def act(out, in_, func, R, W, **kw):
            return op("act", "activation", R, W, out=out, in_=in_, func=func, **kw)

        def dma(eng, out, in_, R=(), W=()):
            nb = 128 * _fsize(out) * 4
            return P.emit(eng, lambda e: e.dma_start(out=out, in_=in_), reads=R, writes=W, dma=True, cost=2500.0 + nb / 150.0)

        def ts(eng, out, in0, s1, s2, op0, op1, R, W):
            if s2 is None:
                return op(eng, "tensor_scalar", R, W, out=out, in0=in0, scalar1=s1, scalar2=None, op0=op0)
            return op(eng, "tensor_scalar", R, W, out=out, in0=in0, scalar1=s1, scalar2=s2, op0=op0, op1=op1)

        def tt(eng, out, in0, in1, o, R, W):
            return op(eng, "tensor_tensor", R, W, out=out, in0=in0, in1=in1, op=o)

        def stt(out, in0, scalar, in1, op0, op1, R, W):
            return op("dve", "scalar_tensor_tensor", R, W, out=out, in0=in0, scalar=scalar, in1=in1, op0=op0, op1=op1)

        prm = sb("prm", [128, ptot], F32)
        Rprm = Reg("prm")

        def pp(name, lo=0, hi=None):
            o, w = poff[name]
            hi = w if hi is None else hi
            return prm[:, o + lo:o + hi]

        cst = sb("cst", [128, 6, 128], BF16)
        cstr = sb("cstr", [128, 2, 128], F32)
        Rcst = Reg("cst")
        xt = sb("xt", [128, 4, D_MODEL], F32)
        Rxt = Reg("xt")
        hT = sb("hT", [128, 8, 512], BF16)
        RhT = Reg("hT")
        yT = sb("yT", [128, 5, 4, 512], BF16)
        RyT = [Reg(f"yT{n}") for n in range(5)]
        mgb = sb("mgb", [128, 8, 512], BF16)
        Rmgb = Reg("mgb")
        sm = sb("sm", [128, 256], F32)
        Rsm = {}

        def smr(name):
            if name not in Rsm:
                Rsm[name] = Reg("sm_" + name)
            return Rsm[name]

        NW = 5
        WB = Slots([(sb(f"wb{i}", [128, 8, 512], BF16), Reg(f"wb{i}")) for i in range(NW)])
        PS = Slots([(st.enter_context(nc.psum_tensor(f"ps{i}", [128, 512], F32)), Reg(f"ps{i}")) for i in range(8)])
        FS = Slots([(sb(f"f{i}", [128, 512], F32), Reg(f"f{i}")) for i in range(7)])
        HS = Slots([(sb(f"h{i}", [128, 544], BF16), Reg(f"h{i}")) for i in range(10)])
        GS = Slots([(sb(f"g{i}", [128, 4, 512], BF16), Reg(f"g{i}")) for i in range(8)])
        FR = Slots([(sb(f"fr{i}", [128, 512], F32), Reg(f"fr{i}")) for i in range(5)])
        S16 = Slots([(sb(f"s16_{i}", [128, 128], BF16), Reg(f"s16_{i}")) for i in range(4)])
        DG = Slots([(sb(f"dg{i}", [128, 4, 128], BF16), Reg(f"dg{i}")) for i in range(2)])

        haloA = sb("haloA", [128, L, 12, 4], BF16)
        haloB = sb("haloB", [128, L, 4, 32], BF16)
        haloD = sb("haloD", [128, L, 4, 4], BF16)
        Rhalo = [Reg(f"halo{l}") for l in range(L)]
        Sst = sb("Sst", [128, L, 4, 128], F32)
        Sbf = sb("Sbf", [128, L, 4, 128], BF16)
        RS = [[Reg(f"S{l}_{h}") for h in range(4)] for l in range(L)]
        hst = sb("hst", [128, L, 4], F32)
        Rhst = [Reg(f"hst{l}") for l in range(L)]
        kTc = sb("kTc", [128, L, 640], BF16)
        vtc = sb("vtc", [128, L, 5, 128], BF16)
        Rkv = [Reg(f"kv{l}") for l in range(L)]
        mkT = sb("mkT", [128, L, 4, 256], BF16)
        mvt = sb("mvt", [128, L, 2, 512], BF16)
        Rmem = [Reg(f"mem{l}") for l in range(L)]
        bdws = sb("bdws", [128, L, 2, 4, 128], BF16)
        Rbdw = Reg("bdw")
        lruc = sb("lruc", [128, L, 8], F32)
        nA = sb("nA", [128, L, 4], F32)
        Rlc = Reg("lruc")

        ident_f = pp("ident")
        ident_b = cst[:, 0, :]
        ones_b = cst[:, 1, :]
        ones_r = cstr[:, 0, :].bitcast(F32R)

        dma("sp", prm[:], prm_d, W=[Rprm])
        op("dve", "tensor_copy", [Rprm], [Rcst], out=cst[:, 0, :], in_=pp("ident"))
        op("dve", "tensor_copy", [Rprm], [Rcst], out=cst[:, 1, :], in_=pp("ones"))
        op("dve", "tensor_copy", [Rprm], [Rcst], out=cstr[:, 0, :].bitcast(F32R), in_=pp("ones"))
        bo_, bw_ = poff["bandb"]
        bandv = prm[:, bo_:bo_ + bw_].rearrange("p (h j) -> p h j", h=8)
        tt("dve", bandv, bandv, pp("maskc").unsqueeze(1).broadcast_to([128, 8, 256]), ALU.add, [Rprm], [Rprm])
        for l in range(L):
            dma("pool", bdws[:, l].rearrange("p t c m -> p (t c) m"),
                bdw_d[l].rearrange("t c p m -> p (t c) m"), W=[Rbdw])
            act(lruc[:, l, 0:4], pp(f"dlam{l}"), AF.Exp, [Rprm], [Rlc], scale=-1.0)
            act(lruc[:, l, 0:4], lruc[:, l, 0:4], AF.Ln, [Rlc], [Rlc], bias=1.0)
            ts("dve", lruc[:, l, 4:8], lruc[:, l, 0:4], -16.0, None, ALU.mult, None, [Rlc], [Rlc])
            ts("dve", lruc[:, l, 0:4], lruc[:, l, 0:4], -8.0, None, ALU.mult, None, [Rlc], [Rlc])
            act(nA[:, l, :], pp(f"alog{l}"), AF.Exp, [Rprm], [Rlc])
            ts("dve", nA[:, l, :], nA[:, l, :], -1.0, None, ALU.mult, None, [Rlc], [Rlc])
            op("dve", "memset", [], [Rhalo[l]], ap=haloA[:, l], constant=0.0)
            op("dve", "memset", [], [Rhalo[l]], ap=haloB[:, l], constant=0.0)
            op("dve", "memset", [], [Rhalo[l]], ap=haloD[:, l], constant=0.0)
            for h in range(4):
                op("dve", "memset", [], [RS[l][h]], ap=Sst[:, l, h, :], constant=0.0)
                op("dve", "memset", [], [RS[l][h]], ap=Sbf[:, l, h, :], constant=0.0)
            op("dve", "memset", [], [Rhst[l]], ap=hst[:, l, :], constant=0.0)
            op("dve", "memset", [], [Rkv[l]], ap=kTc[:, l, :], constant=0.0)
            op("dve", "memset", [], [Rkv[l]], ap=vtc[:, l], constant=0.0)

        def gget(pool):
            while not pool.free:
                P.blocked = True
                yield
            P.progress = True
            return pool.get()

        def run(g):
            idle = 0
            P.progress = False
            for _ in g:
                if P.progress:
                    idle = 0
                else:
                    idle += 1
                    if idle > 10000:
                        raise RuntimeError("build-time scheduling deadlock")
                P.progress = False

        def _step_best(active, rdy, bias=None):
            g = min(active, key=lambda x: rdy[id(x)] - (bias.get(id(x), 0.0) if bias else 0.0))
            save = P.step_fin
            P.step_fin = 0.0
            P.blocked = False
            try:
                next(g)
                if P.step_fin > 0.0:
                    rdy[id(g)] = P.step_fin
                else:
                    others = [rdy[id(x)] for x in active if x is not g]
                    rdy[id(g)] = (min(others) if others else rdy[id(g)]) + 50.0
            except StopIteration:
                active.remove(g)
                P.progress = True
            P.step_fin = max(save, P.step_fin)

        def par(*gens, prio=None):
            active = list(gens)
            rdy = {id(g): 0.0 for g in active}
            bias = {id(g): (prio[i] if prio else 0.0) for i, g in enumerate(active)}
            while active:
                _step_best(active, rdy, bias)
                yield

        def pipeline(gens, width):
            it = iter(gens)
            active = []
            rdy = {}
            done = False
            while True:
                while not done and len(active) < width:
                    try:
                        active.append(next(it))
                        P.progress = True
                    except StopIteration:
                        done = True
                if not active:
                    return
                for g in active:
                    rdy.setdefault(id(g), 0.0)
                _step_best(active, rdy)
                yield

        Rscr = {}
        scr_seen = set()

        class WStream:
            def __init__(self, srcs):
                self.srcs = srcs
                self.i = 0
                self.pend = {}

            def _issue(self, i, slot):
                wt, Rw = slot
                src, ncols, key = self.srcs[i]
                if key is None or ncols != 512:
                    dma("pool", wt[:, :, 0:ncols], src.rearrange("(kc p) n -> p kc n", p=128), W=[Rw])
                elif key not in scr_seen:
                    scr_seen.add(key)
                    Rscr[key] = Reg(f"scr{key}")
                    dma("pool", wt[:, :, 0:ncols], src.rearrange("(kc p) n -> p kc n", p=128), W=[Rw])
                    dma("sp", scr_d[key], wt[:].rearrange("p a b -> p (a b)"), R=[Rw], W=[Rscr[key]])
                else:
                    dma("sp", wt[:].rearrange("p a b -> p (a b)"), scr_d[key], R=[Rscr[key]], W=[Rw])
                self.pend[i] = slot

            def prefetch(self):
                if self.i < len(self.srcs) and self.i not in self.pend and WB.free:
                    self._issue(self.i, WB.get())

            def take(self):
                i = self.i
                self.i += 1
                if i not in self.pend:
                    slot = yield from gget(WB)
                    self._issue(i, slot)
                cur = self.pend.pop(i)
                self.prefetch()
                return cur

        def winblk(l, blk):
            return (win_d[l][:, blk * 512:(blk + 1) * 512], 512, l * 26 + blk)

        def norm_T(src, Rsrc, nsub, gname, dst, Rdst, inv_ap, Rinv):
            for s in range(nsub):
                jt, Rj = yield from gget(FS)
                act(jt[:].bitcast(BF16), src[:, s, :], AF.Square, [Rsrc], [Rj, Rinv], accum_out=inv_ap[:, s:s + 1])
                FS.put((jt, Rj))
            rs = sm[:, 8:8 + nsub]
            Rrs = smr("rs")
            act(inv_ap, inv_ap, AF.Ln, [Rinv], [Rinv], scale=1.0 / D_MODEL, bias=EPS)
            act(rs, inv_ap, AF.Exp, [Rinv], [Rrs], scale=-0.5)
            act(inv_ap, inv_ap, AF.Exp, [Rinv], [Rinv], scale=0.5)
            for s in range(nsub):
                ts("dve", src[:, s, :], src[:, s, :], rs[:, s:s + 1], None, ALU.mult, None, [Rsrc, Rrs], [Rsrc])
            for c in range(8):
                b, Rb = yield from gget(PS)
                for s in range(nsub):
                    tr(b[:, s * 128:(s + 1) * 128], src[:, s, c * 128:(c + 1) * 128], ident_f, [Rsrc, Rprm], [Rb])
                act(dst[:, c, 0:nsub * 128], b[:, 0:nsub * 128], AF.Copy, [Rb, Rprm], [Rdst], scale=pp(gname, c, c + 1))
                PS.put((b, Rb))
                yield

        def proj(wt, Rw, cofs, M, evac, ntok=512, rhs=None, Rrhs=None):
            rhs = hT if rhs is None else rhs
            Rrhs = RhT if Rrhs is None else Rrhs
            b, Rb = yield from gget(PS)
            for kc in range(8):
                mm(b[0:M, 0:ntok], wt[:, kc, cofs:cofs + M], rhs[:, kc, 0:ntok], kc == 0, kc == 7, [Rw, Rrhs], [Rb])
            yield
            evac(b, Rb)
            PS.put((b, Rb))

        mem_ready = {}
        if pipe:
            memt = sb("memt", [128, 2, D_MODEL], F32)
            memT = sb("memT", [128, 8, 256], BF16)
            Rmemt, RmemT = Reg("memt"), Reg("memT")
            minv, Rminv = sm[:, 200:202], smr("minv")
        else:
            memt, Rmemt, memT, RmemT = xt, Rxt, hT, RhT
            minv, Rminv = sm[:, 0:2], smr("inv")

        def mem_phase():
            for l in range(L):
                dma("sp", memt[:, 0:2, :], mem_d.rearrange("(s p) d -> p s d", p=128), W=[Rmemt])
                yield from norm_T(memt, Rmemt, 2, f"gmem{l}", memT, RmemT, minv, Rminv)
                ws = WStream([(wmem_d[l][:, 0:512], 512, None), (wmem_d[l][:, 512:1024], 512, None)])
                wt, Rw = yield from ws.take()
                for h in range(4):
                    def ev(b, Rb, h=h, l=l):
                        act(mkT[:, l, h, :], b[:, 0:256], AF.Copy, [Rb], [Rmem[l]])
                    yield from proj(wt, Rw, h * 128, 128, ev, ntok=256, rhs=memT, Rrhs=RmemT)
                WB.put((wt, Rw))
                wt, Rw = yield from ws.take()
                for s in range(2):
                    b, Rb = yield from gget(PS)
                    for kc in range(8):
                        mm(b[:, :], memT[:, kc, s * 128:(s + 1) * 128], wt[:, kc, :], kc == 0, kc == 7, [Rw, RmemT], [Rb])
                    yield
                    act(mvt[:, l, s, :], b[:, :], AF.Copy, [Rb], [Rmem[l]])
                    PS.put((b, Rb))
                WB.put((wt, Rw))
                mem_ready[l] = True
        if not pipe:
            run(mem_phase())

        def convA_item(l, grp, c, wt, Rw, qnT, RqnT, knT, RknT, vtok, Rvtok):
            ch = grp * 4 + c
            bfm = lambda q: pp(f"bfm{l}", q, q + 1)
            pre, Rpre = yield from gget(HS)
            op("pool", "tensor_copy", [Rhalo[l]], [Rpre], out=pre[:, 0:3], in_=haloA[:, l, ch, 0:3])

            def ev(b, Rb):
                act(pre[:, 3:515], b[:, :], AF.Identity, [Rb, Rprm], [Rpre], bias=bfm(ch))
            yield from proj(wt, Rw, c * 128, 128, ev)
            op("pool", "tensor_copy", [Rpre], [Rhalo[l]], out=haloA[:, l, ch, 0:3], in_=pre[:, 512:515])
            dg, Rdg = yield from gget(DG)
            o_, _w = poff[f"aconv{l}"]
            for k in range(4):
                ts("pool", dg[:, k, :], ident_b, prm[:, o_ + ch * 4 + k:o_ + ch * 4 + k + 1], 0.0, ALU.mult, ALU.add,
                   [Rcst, Rprm], [Rdg])
            yield
            b, Rb = yield from gget(PS)
            for k in range(4):
                mm(b[:, :], dg[:, k, :], pre[:, k:k + 512], k == 0, k == 3, [Rdg, Rpre], [Rb])
            DG.put((dg, Rdg))
            HS.put((pre, Rpre))
            yield
            if grp == 2:
                vT, RvT = yield from gget(HS)
                act(vT[:, 0:512], b[:, :], AF.Silu, [Rb], [RvT])
                PS.put((b, Rb))
                yield
                b, Rb = yield from gget(PS)
                bb = b[:].bitcast(BF16)
                for s in range(4):
                    tr(bb[:, s * 128:(s + 1) * 128], vT[:, s * 128:(s + 1) * 128], ident_b, [RvT, Rcst], [Rb])
                HS.put((vT, RvT))
                yield
                act(vtok[:, :, c * 128:(c + 1) * 128], bb[:, 0:512].rearrange("p (s d) -> p s d", s=4), AF.Copy, [Rb], [Rvtok])
                PS.put((b, Rb))
                return
            cs, Rcs = yield from gget(FS)
            act(cs[:, :], b[:, :], AF.Silu, [Rb], [Rcs])
            PS.put((b, Rb))
            sq, Rsq = yield from gget(HS)
            act(sq[:, 0:512], cs[:, :], AF.Square, [Rcs], [Rsq])
            yield
            b, Rb = yield from gget(PS)
            mm(b[:, :], ones_b, sq[:, 0:512], True, True, [Rcst, Rsq], [Rb])
            HS.put((sq, Rsq))
            yield
            rs, Rrs = yield from gget(FS)
            act(rs[:, :], b[:, :], AF.Ln, [Rb], [Rrs], bias=EPS)
            PS.put((b, Rb))
            act(rs[:, :], rs[:, :], AF.Exp, [Rrs], [Rrs], scale=-0.5)
            dst, Rdst = (qnT, RqnT) if grp == 0 else (knT, RknT)
            stt(dst[:, c, :], cs[:, :], (128 ** -0.5) if grp == 0 else 1.0, rs[:, :], ALU.mult, ALU.mult,
                [Rcs, Rrs], [Rdst])
            FS.put((cs, Rcs))
            FS.put((rs, Rrs))
            yield

        def chain_A(l, ti):
            bfm = lambda q: pp(f"bfm{l}", q, q + 1)
            qnT, RqnT = yield from gget(GS)
            knT, RknT = yield from gget(GS)
            sza, Rsza = yield from gget(GS)
            ktok, Rktok = yield from gget(GS)
            vtok, Rvtok = yield from gget(GS)
            ws = WStream([winblk(l, 8), winblk(l, 0), winblk(l, 1), winblk(l, 2), winblk(l, 3)])
            wt, Rw = yield from ws.take()
            yield from stage_C_kv(l, ti, wt, Rw)
            bba, Rbba = yield from gget(PS)
            for s in range(4):
                for kc in range(8):
                    mm(bba[:, s * 8:(s + 1) * 8], hT[:, kc, s * 128:(s + 1) * 128], wt[:, kc, 256:264], kc == 0, kc == 7,
                       [Rw, RhT], [Rbba])
            WB.put((wt, Rw))
            kv_ready[(l, ti)] = True
            yield
            bg = sm[:, 16:48].rearrange("p (s e) -> p s e", s=4)
            Rbg = smr("bg")
            tt("dve", bg, bba[:, 0:32].rearrange("p (s e) -> p s e", s=4),
               pp(f"bba{l}").unsqueeze(1).broadcast_to([128, 4, 8]), ALU.add, [Rbba, Rprm], [Rbg])
            PS.put((bba, Rbba))
            for grp in range(3):
                wt, Rw = yield from ws.take()
                yield from pipeline([convA_item(l, grp, c, wt, Rw, qnT, RqnT, knT, RknT, vtok, Rvtok) for c in range(4)], W_CONVA)
                WB.put((wt, Rw))
            wt, Rw = yield from ws.take()
            for c in range(4):
                def ev(b, Rb, c=c):
                    act(sza[:, c, :], b[:, :], AF.Silu, [Rb, Rprm], [Rsza], bias=bfm(12 + c))
                yield from proj(wt, Rw, c * 128, 128, ev)
            WB.put((wt, Rw))
            bet = sm[:, 48:64].rearrange("p (s e) -> p s e", s=4)
            gg = sm[:, 64:80].rearrange("p (s e) -> p s e", s=4)
            act(bet, bg[:, :, 0:4], AF.Sigmoid, [Rbg], [smr("bet")])
            tt("dve", gg, bg[:, :, 4:8], pp(f"adt{l}").unsqueeze(1).broadcast_to([128, 4, 4]), ALU.add, [Rbg, Rprm], [smr("gg")])
            act(gg, gg, AF.Exp, [smr("gg")], [smr("gg")])
            act(gg, gg, AF.Ln, [smr("gg")], [smr("gg")], bias=1.0)
            tt("dve", gg, gg, nA[:, l, :].unsqueeze(1).broadcast_to([128, 4, 4]), ALU.mult, [smr("gg"), Rlc], [smr("gg")])
            for s in range(4):
                b, Rb = yield from gget(PS)
                bb = b[:].bitcast(BF16)
                for h in range(4):
                    tr(bb[:, h * 128:(h + 1) * 128], knT[:, h, s * 128:(s + 1) * 128], ident_b, [RknT, Rcst], [Rb])
                yield
                act(ktok[:, s, :], bb[:, 0:512], AF.Copy, [Rb], [Rktok])
                PS.put((b, Rb))
            marks.append(("A_prologue_end", ti, l, P.cnt["pe"]))
            for s in range(4):
                yield from gdn_chunk(l, ti, s, qnT, RqnT, knT, RknT, ktok, Rktok, vtok, Rvtok, sza, Rsza, bet, gg)
                marks.append((f"A_chunk{s}_end", ti, l, P.cnt["pe"]))
            for it in ((qnT, RqnT), (knT, RknT), (ktok, Rktok), (vtok, Rvtok), (sza, Rsza)):
                GS.put(it)

        def gdn_chunk(l, ti, s, qnT, RqnT, knT, RknT, ktok, Rktok, vtok, Rvtok, sza, Rsza, bet, gg):
            tsl = slice(s * 128, (s + 1) * 128)
            Rbet, Rgg = smr("bet"), smr("gg")
            gs = sm[:, 80:88]
            Rgs = smr("gs")
            H4 = lambda ap: ap.rearrange("p (h j) -> p h j", h=4)
            bc4 = lambda ap: ap.unsqueeze(2).broadcast_to([128, 4, 128])
            bcm = lambda ap: ap.unsqueeze(1).broadcast_to([128, 4, 128])
            hsl = lambda h: slice(h * 128, (h + 1) * 128)
            bG, RbG = yield from gget(PS)
            mm(bG[:, 0:4], pp("triu"), gg[:, s, :], True, True, [Rprm, Rgg], [RbG])
            mm(bG[:, 4:8], pp("ones"), gg[:, s, :], True, True, [Rprm, Rgg], [RbG])
            yield
            op("dve", "tensor_copy", [RbG], [Rgs], out=gs, in_=bG[:, 0:8])
            PS.put((bG, RbG))
            ex = sm[:, 88:104]
            Rex = smr("ex")
            act(ex[:, 0:4], gs[:, 0:4], AF.Exp, [Rgs], [Rex])
            yield
            tt("dve", ex[:, 0:4], ex[:, 0:4], bet[:, s, :], ALU.mult, [Rex, Rbet], [Rex])
            tt("dve", ex[:, 12:16], gs[:, 4:8], gs[:, 0:4], ALU.subtract, [Rgs], [Rex])
            yield
            act(ex[:, 4:8], ex[:, 12:16], AF.Exp, [Rex], [Rex])
            act(ex[:, 8:12], gs[:, 4:8], AF.Exp, [Rgs], [Rex])
            rg, Rrg = yield from gget(FS)
            tt("dve", H4(rg[:, :]), bcm(ident_f), bc4(gs[:, 0:4]), ALU.mult, [Rprm, Rgs], [Rrg])
            bGr, RbGr = yield from gget(PS)
            mm(bGr[:, :], pp("ones"), rg[:, :], True, True, [Rprm, Rrg], [RbGr])
            FS.put((rg, Rrg))
            bK, RbK = yield from gget(PS)
            bQ, RbQ = yield from gget(PS)
            for h in range(4):
                mm(bK[:, hsl(h)], knT[:, h, tsl], knT[:, h, tsl], True, True, [RknT], [RbK])
            for h in range(4):
                mm(bQ[:, hsl(h)], knT[:, h, tsl], qnT[:, h, tsl], True, True, [RknT, RqnT], [RbQ])
            yield
            dd, Rdd = yield from gget(FS)
            e1, Re1 = yield from gget(FS)
            e2, Re2 = yield from gget(FS)
            tt("dve", H4(dd[:, :]), H4(bGr[:, :]), bc4(gs[:, 0:4]), ALU.subtract, [RbGr, Rgs], [Rdd])
            yield
            tt("dve", H4(e2[:, :]), H4(dd[:, :]), bcm(pp("masku")), ALU.add, [Rdd, Rprm], [Re2])
            act(e2[:, :], e2[:, :], AF.Exp, [Re2], [Re2])
            yield
            tt("dve", H4(e1[:, :]), H4(dd[:, :]), bcm(pp("masklneg")), ALU.subtract, [Rdd, Rprm], [Re1])
            act(e1[:, :], e1[:, :], AF.Exp, [Re1], [Re1], scale=-1.0)
            yield
            act(dd[:, :], bGr[:, :], AF.Exp, [RbGr], [Rdd])
            PS.put((bGr, RbGr))
            qd, Rqd = yield from gget(HS)
            tt("dve", H4(qd[:, 0:512]), qnT[:, :, tsl], H4(dd[:, :]), ALU.mult, [RqnT, Rdd], [Rqd])
            yield
            tt("dve", e1[:, :], bK[:, :], e1[:, :], ALU.mult, [RbK, Re1], [Re1])
            yield
            PS.put((bK, RbK))
            tt("dve", H4(dd[:, :]), H4(e1[:, :]), bc4(bet[:, s, :]), ALU.mult, [Re1, Rbet], [Rdd])
            at, Rat = yield from gget(HS)
            tt("dve", at[:, 0:512], bQ[:, :], e2[:, :], ALU.mult, [RbQ, Re2], [Rat])
            yield
            PS.put((bQ, RbQ))
            FS.put((e1, Re1))
            FS.put((e2, Re2))
            ad, Rad = yield from gget(FR)
            ao, Rao = yield from gget(FR)
            tt("dve", H4(ad[:, :].bitcast(F32R)), H4(dd[:, :]), bcm(pp("bd01")), ALU.mult, [Rdd, Rprm], [Rad])
            yield
            tt("dve", H4(ao[:, :].bitcast(F32R)), H4(dd[:, :]), bcm(pp("off01")), ALU.mult, [Rdd, Rprm], [Rao])
            FS.put((dd, Rdd))
            bT, RbT = yield from gget(PS)
            for h in range(4):
                tr(bT[:, hsl(h)], ad[:, hsl(h)], ident_f, [Rad, Rprm], [RbT])
            yield
            bm, Rbm = yield from gget(FR)
            pm, Rpm = yield from gget(FR)
            op("dve", "tensor_copy", [RbT], [Rbm], out=bm[:, :].bitcast(F32R), in_=bT[:, :])
            tt("dve", H4(pm[:, :].bitcast(F32R)), bcm(ident_f), H4(bT[:, :]), ALU.subtract, [Rprm, RbT], [Rpm])
            PS.put((bT, RbT))
            adr, bmr, pmr = ad[:, :].bitcast(F32R), bm[:, :].bitcast(F32R), pm[:, :].bitcast(F32R)
            bA, RbA = yield from gget(PS)
            bB, RbB = yield from gget(PS)
            bP, RbP = yield from gget(PS)
            for k in range(1, 6):
                for h in range(4):
                    mm(bA[:, hsl(h)], bmr[:, hsl(h)], adr[:, hsl(h)], True, True, [Rbm, Rad], [RbA])
                if k < 5:
                    for h in range(4):
                        mm(bB[:, hsl(h)], adr[:, hsl(h)], bmr[:, hsl(h)], True, True, [Rbm, Rad], [RbB])
                yield
                op("dve", "tensor_copy", [RbA], [Rad], out=adr, in_=bA[:, :])
                if k < 5:
                    op("dve", "tensor_copy", [RbB], [Rbm], out=bmr, in_=bB[:, :])
                for h in range(4):
                    mm(bP[:, hsl(h)], adr[:, hsl(h)], pmr[:, hsl(h)], True, True, [Rad, Rpm], [RbP])
                yield
                tt("dve", pmr, pm[:, :], bP[:, :], ALU.add, [Rpm, RbP], [Rpm])
            FR.put((ad, Rad))
            for h in range(4):
                tr(bA[:, hsl(h)], pm[:, hsl(h)], ident_f, [Rpm, Rprm], [RbA])
            for h in range(4):
                mm(bB[:, hsl(h)], ao[:, hsl(h)].bitcast(F32R), pmr[:, hsl(h)], True, True, [Rao, Rpm], [RbB])
            yield
            op("dve", "tensor_copy", [RbA], [Rbm], out=bmr, in_=bA[:, :])
            ym, Rym = yield from gget(FR)
            op("dve", "tensor_copy", [RbB], [Rym], out=ym[:, :].bitcast(F32R), in_=bB[:, :])
            FR.put((ao, Rao))
            for h in range(4):
                mm(bP[:, hsl(h)], bmr[:, hsl(h)], ym[:, hsl(h)].bitcast(F32R), True, True, [Rbm, Rym], [RbP])
            yield
            ttm, Rttm = yield from gget(HS)
            tt("dve", ttm[:, 0:512], pm[:, :], bP[:, :], ALU.subtract, [Rpm, RbP], [Rttm])
            for it in ((bm, Rbm), (pm, Rpm), (ym, Rym)):
                FR.put(it)
            rv, Rrv = yield from gget(HS)
            rk, Rrk = yield from gget(HS)
            kd, Rkd = yield from gget(HS)
            tt("pool", H4(rv[:, 0:512]), H4(vtok[:, s, :]), bc4(bet[:, s, :]), ALU.mult, [Rvtok, Rbet], [Rrv])
            tt("pool", H4(rk[:, 0:512]), H4(ktok[:, s, :]), bc4(ex[:, 0:4]), ALU.mult, [Rktok, Rex], [Rrk])
            tt("pool", H4(kd[:, 0:512]), H4(ktok[:, s, :]), bc4(ex[:, 4:8]), ALU.mult, [Rktok, Rex], [Rkd])
            for h in range(4):
                mm(bA[:, hsl(h)], ttm[:, hsl(h)], rv[:, hsl(h)], True, True, [Rttm, Rrv], [RbA])
            for h in range(4):
                mm(bB[:, hsl(h)], rk[:, hsl(h)], ttm[:, hsl(h)], True, True, [Rttm, Rrk], [RbB])
            yield
            u, Ru = yield from gget(FS)
            wT, RwT = yield from gget(HS)
            act(u[:, :], bA[:, :], AF.Copy, [RbA], [Ru])
            act(wT[:, 0:512], bB[:, :], AF.Copy, [RbB], [RwT])
            for it in ((ttm, Rttm), (rv, Rrv), (rk, Rrk)):
                HS.put(it)
            for h in range(4):
                mm(bP[:, hsl(h)], wT[:, hsl(h)], Sbf[:, l, h, :], True, True, [RwT, RS[l][h]], [RbP])
            yield
            vn, Rvn = yield from gget(HS)
            tt("dve", vn[:, 0:512], u[:, :], bP[:, :], ALU.subtract, [Ru, RbP], [Rvn])
            yield
            FS.put((u, Ru))
            for h in range(4):
                mm(bA[:, hsl(h)], qd[:, hsl(h)], Sbf[:, l, h, :], True, False, [Rqd, RS[l][h]], [RbA])
                mm(bA[:, hsl(h)], at[:, hsl(h)], vn[:, hsl(h)], False, True, [Rat, Rvn], [RbA])
            for h in range(4):
                mm(bB[:, hsl(h)], kd[:, hsl(h)], vn[:, hsl(h)], True, True, [Rkd, Rvn], [RbB])
            yield
            Sall = Sst[:, l].rearrange("p h d -> p (h d)")
            tt("dve", H4(Sall), H4(Sall), bc4(ex[:, 8:12]), ALU.mult, [Rex] + RS[l], RS[l])
            yield
            tt("dve", Sall, Sall, bB[:, :], ALU.add, RS[l] + [RbB], RS[l])
            act(Sbf[:, l].rearrange("p h d -> p (h d)"), Sall, AF.Copy, RS[l], RS[l])
            PS.put((bB, RbB))
            PS.put((bP, RbP))
            for it in ((wT, RwT), (vn, Rvn), (qd, Rqd), (at, Rat), (kd, Rkd)):
                HS.put(it)
            ssq = sm[:, 104:108]
            Rssq = smr("ssq")
            sq, Rsq = yield from gget(FS)
            act(sq[:, :], bA[:, :], AF.Square, [RbA], [Rsq])
            yield
            op("dve", "tensor_reduce", [Rsq], [Rssq], out=ssq, in_=H4(sq[:, :]), axis=AX.X, op=ALU.add)
            yield
            FS.put((sq, Rsq))
            act(ssq, ssq, AF.Ln, [Rssq], [Rssq], scale=1.0 / 128, bias=EPS)
            act(ssq, ssq, AF.Exp, [Rssq], [Rssq], scale=-0.5)
            on, Ron = yield from gget(HS)
            tt("dve", H4(on[:, 0:512]), H4(bA[:, :]), bc4(ssq), ALU.mult, [RbA, Rssq], [Ron])
            PS.put((bA, RbA))
            b, Rb = yield from gget(PS)
            bb = b[:].bitcast(BF16)
            for h in range(4):
                tr(bb[:, hsl(h)], on[:, hsl(h)], ident_b, [Ron, Rcst], [Rb])
            HS.put((on, Ron))
            yield
            stt(yT[:, 0, :, tsl], H4(bb[:, 0:512]), pp(f"anorm{l}"), sza[:, :, tsl], ALU.mult, ALU.mult,
                [Rb, Rprm, Rsza], [RyT[0]])
            PS.put((b, Rb))

        kv_ready = {}

        def attn_head(h, hh, hd, bO, RbO, Rq, Rk, Rv, stc):
            Rst = smr(f"stc{h}")
            nk = hh["nk"]
            bS, RbS = yield from gget(PS)
            mm(bS[:, 0:nk], hh["q"], hh["k"], True, True, [Rq, Rk], [RbS])
            yield
            mx, m_, negm, rsum, es, rden = (stc[:, i, h:h + 1] for i in range(6))
            p, Rp = yield from gget(HS)
            if hh["bias"] is not None:
                sc, Rsc = yield from gget(FS)
                stt(sc[:, 0:nk], bS[:, 0:nk], hh["scale"], hh["bias"], ALU.mult, ALU.add, [RbS, Rprm], [Rsc])
                PS.put((bS, RbS))
                if hh.get("premask") is not None:
                    ts("dve", sc[:, 0:128], sc[:, 0:128], hh["premask"], None, ALU.add, None, [Rsc, Rprm], [Rsc])
                if FINE:
                    yield
                op("dve", "tensor_reduce", [Rsc], [Rst], out=mx, in_=sc[:, 0:nk], axis=AX.X, op=ALU.max)
                if FINE:
                    yield
                tt("dve", m_, mx, hh["sink"], ALU.max, [Rst, Rprm], [Rst])
                if FINE:
                    yield
                ts("dve", negm, m_, -1.0, None, ALU.mult, None, [Rst], [Rst])
                if FINE:
                    yield
                act(p[:, 0:nk], sc[:, 0:nk], AF.Exp, [Rsc, Rst], [Rp, Rst], bias=negm, accum_out=rsum)
                FS.put((sc, Rsc))
                act(es, hh["sink"], AF.Exp, [Rprm, Rst], [Rst], bias=negm)
                if FINE:
                    yield
                tt("dve", rden, rsum, es, ALU.add, [Rst], [Rst])
                if FINE:
                    yield
            else:
                op("dve", "tensor_reduce", [RbS], [Rst], out=mx, in_=bS[:, 0:nk], axis=AX.X, op=ALU.max)
                if FINE:
                    yield
                ts("dve", negm, mx, -hh["scale"], None, ALU.mult, None, [Rst], [Rst])
                if FINE:
                    yield
                act(p[:, 0:nk], bS[:, 0:nk], AF.Exp, [RbS, Rst], [Rp, Rst], bias=negm, scale=hh["scale"], accum_out=rden)
                PS.put((bS, RbS))
                if FINE:
                    yield
            op("dve", "reciprocal", [Rst], [Rst], out=rden, in_=rden)
            yield
            nkc = nk // 128
            bT, RbT = yield from gget(PS)
            bTb = bT[:].bitcast(BF16)
            for kc in range(nkc):
                tr(bTb[:, kc * 128:(kc + 1) * 128], p[:, kc * 128:(kc + 1) * 128], ident_b, [Rp, Rcst], [RbT])
            HS.put((p, Rp))
            yield
            pT, RpT = yield from gget(HS)
            act(pT[:, 0:nk], bTb[:, 0:nk], AF.Copy, [RbT], [RpT])
            PS.put((bT, RbT))
            for kc in range(nkc):
                mm(bO[:, h * hd:(h + 1) * hd], pT[:, kc * 128:(kc + 1) * 128], hh["v"][kc], kc == 0, kc == nkc - 1,
                   [RpT, Rv], [RbO])
            HS.put((pT, RpT))
            yield

        def attention(heads, hd, Rq, Rk, Rv, ydst, Ry, sz, Rsz, tsl):
            nh = len(heads)
            bO, RbO = yield from gget(PS)
            stc = sm[:, 112:112 + 6 * 8].rearrange("p (k h) -> p k h", k=6)
            yield from pipeline([attn_head(h, hh, hd, bO, RbO, Rq, Rk, Rv, stc) for h, hh in enumerate(heads)], W_ATT)
            on, Ron = yield from gget(HS)
            rd = stc[:, 5, 0:nh]
            tt("dve", on[:, 0:512].rearrange("p (h d) -> p h d", h=nh), bO[:, :].rearrange("p (h d) -> p h d", h=nh),
               rd.unsqueeze(2).broadcast_to([128, nh, hd]), ALU.mult, [RbO] + [smr(f"stc{h}") for h in range(nh)], [Ron])
            PS.put((bO, RbO))
            b, Rb = yield from gget(PS)
            bb = b[:].bitcast(BF16)
            for c in range(4):
                tr(bb[:, c * 128:(c + 1) * 128], on[:, c * 128:(c + 1) * 128], ident_b, [Ron, Rcst], [Rb])
            HS.put((on, Ron))
            yield
            for c in range(4):
                tt("dve", ydst[:, c, tsl], bb[:, c * 128:(c + 1) * 128], sz[:, c, tsl], ALU.mult, [Rb, Rsz], [Ry])
            PS.put((b, Rb))

        def stage_C_kv(l, ti, wt, Rw):
            def ev(b, Rb):
                act(kTc[:, l, 128:640], b[:, :], AF.Identity, [Rb, Rprm], [Rkv[l]], bias=pp(f"bfm{l}", 32, 33))
            yield from proj(wt, Rw, 0, 128, ev)
            for s in range(4):
                b, Rb = yield from gget(PS)
                for kc in range(8):
                    mm(b[:, 0:128], hT[:, kc, s * 128:(s + 1) * 128], wt[:, kc, 128:256], kc == 0, kc == 7, [Rw, RhT], [Rb])
                yield
                tt("dve", vtc[:, l, 1 + s, :], b[:, 0:128], pp(f"bv{l}"), ALU.add, [Rb, Rprm], [Rkv[l]])
                PS.put((b, Rb))

        def gated_fm(l, ws, chbase, dst, Rdst, func):
            wt, Rw = yield from ws.take()
            for c in range(4):
                def ev(b, Rb, c=c):
                    act(dst[:, c, :], b[:, :], func, [Rb, Rprm], [Rdst], bias=pp(f"bfm{l}", chbase + c, chbase + c + 1))
                yield from proj(wt, Rw, c * 128, 128, ev)
            WB.put((wt, Rw))

        def stage_C(l, ti, ws):
            qT, RqT = yield from gget(GS)
            szc, Rszc = yield from gget(GS)
            yield from gated_fm(l, ws, 28, qT, RqT, AF.Identity)
            yield from gated_fm(l, ws, 36, szc, Rszc, AF.Silu)
            while not kv_ready.get((l, ti)):
                yield
            so, _sw = poff[f"sinks{l}"]
            for s in range(4):
                first = (ti == 0 and s == 0)
                heads = []
                for h in range(8):
                    c, base = h % 4, (h // 4) * 64
                    kvh = h // 4
                    if first:
                        k_ap, nk, bias = kTc[base:base + 64, l, 128:256], 128, bandv[:, h, 128:256]
                        v = [vtc[:, l, 1, kvh * 64:(kvh + 1) * 64]]
                    else:
                        k_ap, nk, bias = kTc[base:base + 64, l, s * 128:s * 128 + 256], 256, bandv[:, h, :]
                        v = [vtc[:, l, s, kvh * 64:(kvh + 1) * 64], vtc[:, l, s + 1, kvh * 64:(kvh + 1) * 64]]
                    pm_ = None
                    if pipe and ti == 1 and s == 0:
                        fo_, _ = poff["flags"]
                        pm_ = prm[:, fo_ + 2:fo_ + 3]
                    heads.append(dict(q=qT[base:base + 64, c, s * 128:(s + 1) * 128], k=k_ap, nk=nk, bias=bias,
                                      scale=0.125, sink=prm[:, so + h:so + h + 1], v=v, premask=pm_))
                yield from attention(heads, 64, RqT, Rkv[l], Rkv[l], yT[:, 2], RyT[2], szc, Rszc, slice(s * 128, (s + 1) * 128))
            op("pool", "tensor_copy", [Rkv[l]], [Rkv[l]], out=kTc[:, l, 0:128], in_=kTc[:, l, 512:640])
            op("pool", "tensor_copy", [Rkv[l]], [Rkv[l]], out=vtc[:, l, 0, :], in_=vtc[:, l, 4, :])
            GS.put((qT, RqT))
            GS.put((szc, Rszc))

        def stage_E(l, ti, ws):
            eq, Req = yield from gget(GS)
            sze, Rsze = yield from gget(GS)
            yield from gated_fm(l, ws, 48, eq, Req, AF.Identity)
            yield from gated_fm(l, ws, 52, sze, Rsze, AF.Silu)
            while not mem_ready.get(l):
                yield
            for s in range(4):
                tsl = slice(s * 128, (s + 1) * 128)
                heads = [dict(q=eq[:, h, tsl], k=mkT[:, l, h, :], nk=256, bias=None, scale=128 ** -0.5, sink=None,
                              v=[mvt[:, l, 0, h * 128:(h + 1) * 128], mvt[:, l, 1, h * 128:(h + 1) * 128]])
                         for h in range(4)]
                yield from attention(heads, 128, Req, Rmem[l], Rmem[l], yT[:, 4], RyT[4], sze, Rsze, tsl)
            GS.put((eq, Req))
            GS.put((sze, Rsze))

        def convB_item(l, c, wa, Rwa, wb, Rwb, cvv, Rcv):
            bwo, _ = poff[f"bdw{l}"]
            sg, Rsg = yield from gget(HS)

            def evb(b, Rb):
                act(sg[:, 0:512], b[:, :], AF.Sigmoid, [Rb, Rprm], [Rsg], bias=pp(f"bfm{l}", 20 + c, 21 + c))
            yield from proj(wb, Rwb, c * 128, 128, evb)
            pre, Rpre = yield from gget(HS)
            op("pool", "tensor_copy", [Rhalo[l]], [Rpre], out=pre[:, 0:30], in_=haloB[:, l, c, 0:30])

            def eva(b, Rb):
                stt(pre[:, 30:542], b[:, :], pp(f"bfm{l}", 16 + c, 17 + c), sg[:, 0:512], ALU.add, ALU.mult,
                    [Rb, Rprm, Rsg], [Rpre])
            yield from proj(wa, Rwa, c * 128, 128, eva)
            HS.put((sg, Rsg))
            op("pool", "tensor_copy", [Rpre], [Rhalo[l]], out=haloB[:, l, c, 0:30], in_=pre[:, 512:542])
            b, Rb = yield from gget(PS)
            for k in range(31):
                dgk, Rdgk = yield from gget(S16)
                ts("pool", dgk[:, :], ident_b, prm[:, bwo + c * 31 + k:bwo + c * 31 + k + 1], 0.0, ALU.mult, ALU.add,
                   [Rcst, Rprm], [Rdgk])
                mm(b[:, :], dgk[:, :], pre[:, k:k + 512], k == 0, k == 30, [Rdgk, Rpre], [Rb])
                S16.put((dgk, Rdgk))
                if k % 4 == 3:
                    yield
            HS.put((pre, Rpre))
            yield
            act(cvv[c], b[:, :], AF.Identity, [Rb, Rprm], [Rcv[c]], bias=pp(f"bdwb{l}", c, c + 1))
            PS.put((b, Rb))

        def stage_B(l, ti, ws):
            szb, Rszb = yield from gget(GS)
            g0, Rg0 = yield from gget(GS)
            g1, Rg1 = yield from gget(GS)
            yield from gated_fm(l, ws, 24, szb, Rszb, AF.Silu)
            wa, Rwa = yield from ws.take()
            wb, Rwb = yield from ws.take()
            g0f = g0[:].rearrange("p a b -> p (a b)").bitcast(F32)
            g1f = g1[:].rearrange("p a b -> p (a b)").bitcast(F32)
            cvv = [g0f[:, 0:512], g0f[:, 512:1024], g1f[:, 0:512], g1f[:, 512:1024]]
            Rcv = [Rg0, Rg0, Rg1, Rg1]
            yield from pipeline([convB_item(l, c, wa, Rwa, wb, Rwb, cvv, Rcv) for c in range(4)], W_B)
            WB.put((wa, Rwa))
            WB.put((wb, Rwb))
            bM, RbM = yield from gget(PS)
            bQ, RbQ = yield from gget(PS)
            for c in range(4):
                mm(bM[:, :], pp("ones"), cvv[c], c == 0, c == 3, [Rprm, Rcv[c]], [RbM])
            for c in range(4):
                sq, Rsq = yield from gget(FS)
                act(sq[:, :], cvv[c], AF.Square, [Rcv[c]], [Rsq])
                mm(bQ[:, :], pp("ones"), sq[:, :], c == 0, c == 3, [Rprm, Rsq], [RbQ])
                FS.put((sq, Rsq))
                yield
            mean, Rmean = yield from gget(FS)
            rstd, Rrstd = yield from gget(FS)
            act(mean[:, :], bM[:, :], AF.Copy, [RbM], [Rmean], scale=1.0 / 512)
            act(rstd[:, :], bM[:, :], AF.Square, [RbM], [Rrstd], scale=1.0 / 512)
            PS.put((bM, RbM))
            stt(rstd[:, :], bQ[:, :], 1.0 / 512, rstd[:, :], ALU.mult, ALU.subtract, [RbQ, Rrstd], [Rrstd])
            PS.put((bQ, RbQ))
            ts("dve", rstd[:, :], rstd[:, :], 0.0, EPS, ALU.max, ALU.add, [Rrstd], [Rrstd])
            act(rstd[:, :], rstd[:, :], AF.Ln, [Rrstd], [Rrstd])
            act(rstd[:, :], rstd[:, :], AF.Exp, [Rrstd], [Rrstd], scale=-0.5)
            yield
            for c in range(4):
                tt("dve", cvv[c], cvv[c], mean[:, :], ALU.subtract, [Rcv[c], Rmean], [Rcv[c]])
                tt("pool", cvv[c], cvv[c], rstd[:, :], ALU.mult, [Rcv[c], Rrstd], [Rcv[c]])
                bn, Rbn = yield from gget(HS)
                act(bn[:, 0:512], cvv[c], AF.Silu, [Rcv[c], Rprm], [Rbn], scale=pp(f"blng{l}", c, c + 1),
                    bias=pp(f"blnb{l}", c, c + 1))
                tt("dve", yT[:, 1, c, :], bn[:, 0:512], szb[:, c, :], ALU.mult, [Rbn, Rszb], [RyT[1]])
                HS.put((bn, Rbn))
                yield
            FS.put((mean, Rmean))
            FS.put((rstd, Rrstd))
            for it in ((szb, Rszb), (g0, Rg0), (g1, Rg1)):
                GS.put(it)

        def lru_item(l, c, wt, Rw, szd, Rszd):
            dwo, _ = poff[f"dconv{l}"]
            pre, Rpre = yield from gget(HS)
            op("pool", "tensor_copy", [Rhalo[l]], [Rpre], out=pre[:, 0:3], in_=haloD[:, l, c, 0:3])

            def ev(b, Rb):
                act(pre[:, 3:515], b[:, :], AF.Identity, [Rb, Rprm], [Rpre], bias=pp(f"bfm{l}", 40 + c, 41 + c))
            yield from proj(wt, Rw, c * 128, 128, ev)
            op("pool", "tensor_copy", [Rpre], [Rhalo[l]], out=haloD[:, l, c, 0:3], in_=pre[:, 512:515])
            dg, Rdg = yield from gget(DG)
            for k in range(4):
                ts("pool", dg[:, k, :], ident_b, prm[:, dwo + c * 4 + k:dwo + c * 4 + k + 1], 0.0, ALU.mult, ALU.add,
                   [Rcst, Rprm], [Rdg])
            yield
            b, Rb = yield from gget(PS)
            for k in range(4):
                mm(b[:, :], dg[:, k, :], pre[:, k:k + 512], k == 0, k == 3, [Rdg, Rpre], [Rb])
            DG.put((dg, Rdg))
            HS.put((pre, Rpre))
            yield
            while len(FS.free) < 4:
                yield
            dx, Rdx = FS.get()
            r, Rr = FS.get()
            ig, Rig = FS.get()
            a, Ra = FS.get()
            dxb, Rdxb = yield from gget(HS)
            act(dx[:, :], b[:, :], AF.Identity, [Rb, Rprm], [Rdx], bias=pp(f"dconvb{l}", c, c + 1))
            PS.put((b, Rb))
            op("pool", "tensor_copy", [Rdx], [Rdxb], out=dxb[:, 0:512], in_=dx[:, :])
            yield
            bR, RbR = yield from gget(PS)
            mm(bR[:, :], bdws[:, l, 0, c, :], dxb[:, 0:512], True, True, [Rbdw, Rdxb], [RbR])
            bI, RbI = yield from gget(PS)
            mm(bI[:, :], bdws[:, l, 1, c, :], dxb[:, 0:512], True, True, [Rbdw, Rdxb], [RbI])
            HS.put((dxb, Rdxb))
            yield
            act(r[:, :], bR[:, :], AF.Sigmoid, [RbR, Rprm], [Rr], bias=pp(f"dba{l}", c, c + 1))
            act(ig[:, :], bI[:, :], AF.Sigmoid, [RbI, Rprm], [Rig], bias=pp(f"dbx{l}", c, c + 1))
            PS.put((bR, RbR))
            PS.put((bI, RbI))
            act(a[:, :], r[:, :], AF.Exp, [Rr, Rlc], [Ra], scale=lruc[:, l, c:c + 1])
            act(r[:, :], r[:, :], AF.Exp, [Rr, Rlc], [Rr], scale=lruc[:, l, 4 + c:5 + c])
            act(r[:, :], r[:, :], AF.Sqrt, [Rr], [Rr], scale=-1.0, bias=1.0)
            tt("dve", ig[:, :], ig[:, :], dx[:, :], ALU.mult, [Rig, Rdx], [Rig])
            tt("pool", ig[:, :], ig[:, :], r[:, :], ALU.mult, [Rig, Rr], [Rig])
            FS.put((dx, Rdx))
            yield
            op("dve", "tensor_tensor_scan", [Ra, Rig, Rhst[l]], [Rr], out=r[:, :], data0=a[:, :], data1=ig[:, :],
               initial=hst[:, l, c:c + 1], op0=ALU.mult, op1=ALU.add)
            op("dve", "tensor_copy", [Rr], [Rhst[l]], out=hst[:, l, c:c + 1], in_=r[:, 511:512])
            tt("dve", yT[:, 3, c, :], r[:, :], szd[:, c, :], ALU.mult, [Rr, Rszd], [RyT[3]])
            for it in ((r, Rr), (ig, Rig), (a, Ra)):
                FS.put(it)
            yield

        def stage_D(l, ti, ws):
            szd, Rszd = yield from gget(GS)
            yield from gated_fm(l, ws, 44, szd, Rszd, AF.Silu)
            wt, Rw = yield from ws.take()
            for c in range(4):
                yield from lru_item(l, c, wt, Rw, szd, Rszd)
            WB.put((wt, Rw))
            GS.put((szd, Rszd))

        def chain_rest(l, ti):
            ws = WStream([winblk(l, 12), winblk(l, 13),
                          winblk(l, 7), winblk(l, 9)])
            ws.prefetch()
            yield from stage_E(l, ti, ws)
            marks.append(("E_end", ti, l, P.cnt["pe"]))
            yield from stage_C(l, ti, ws)
            marks.append(("C_end", ti, l, P.cnt["pe"]))

        def chain_bd(l, ti):
            ws = WStream([winblk(l, 11), winblk(l, 10),
                          winblk(l, 6), winblk(l, 4), winblk(l, 5)])
            yield from stage_D(l, ti, ws)
            marks.append(("D_end", ti, l, P.cnt["pe"]))
            yield from stage_B(l, ti, ws)
            marks.append(("B_end", ti, l, P.cnt["pe"]))

        def merge_item(l, n, j, jj, wg, Rwg, wr, Rwr, mgf, Rmgs):
            gsb, Rgsb = yield from gget(HS)

            def ev(b, Rb):
                act(gsb[:, 0:512], b[:, :], AF.Sigmoid, [Rb, Rprm], [Rgsb], bias=pp(f"bfm{l}", 56 + n * 8 + j, 57 + n * 8 + j))
            yield from proj(wg, Rwg, jj * 128, 128, ev)
            b, Rb = yield from gget(PS)
            for kc in range(4):
                mm(b[:, :], wr[:, kc, jj * 128:(jj + 1) * 128], yT[:, n, kc, :], kc == 0, kc == 3, [Rwr, RyT[n]], [Rb])
            yield
            if n == 0:
                tt("dve", mgf[j], b[:, :], gsb[:, 0:512], ALU.mult, [Rb, Rgsb], [Rmgs[j]])
            else:
                tmp, Rtmp = yield from gget(FS)
                tt("dve", tmp[:, :], b[:, :], gsb[:, 0:512], ALU.mult, [Rb, Rgsb], [Rtmp])
                tt("pool", mgf[j], mgf[j], tmp[:, :], ALU.add, [Rmgs[j], Rtmp], [Rmgs[j]])
                FS.put((tmp, Rtmp))
            PS.put((b, Rb))
            HS.put((gsb, Rgsb))
            if n == 4:
                act(mgb[:, j, :], mgf[j], AF.Copy, [Rmgs[j]], [Rmgb])
            yield

        def stage_merge(l, ti):
            mgs = []
            for i in range(4):
                mgs.append((yield from gget(GS)))
            mgf, Rmgs = [], []
            for i in range(4):
                f = mgs[i][0][:].rearrange("p a b -> p (a b)").bitcast(F32)
                mgf += [f[:, 0:512], f[:, 512:1024]]
                Rmgs += [Reg(f"mgs{2 * i}"), Reg(f"mgs{2 * i + 1}")]
                for r_ in Rmgs[-2:]:
                    r_.w, r_.r = mgs[i][1].w, list(mgs[i][1].r)
            ws = WStream([winblk(l, 14 + q) for q in range(10)])
            ws.prefetch()
            for n in range(5):
                for half in range(2):
                    wr, Rwr = yield from gget(GS)
                    kb = ("b", l * 10 + n * 2 + half)
                    if kb not in scr_seen:
                        scr_seen.add(kb)
                        Rscr[kb] = Reg(f"scrb{kb[1]}")
                        dma("pool", wr[:, :, :], wbr_d[l, n][:, half * 512:(half + 1) * 512].rearrange("(kc p) d -> p kc d", p=128),
                            W=[Rwr])
                        dma("sp", scrb_d[kb[1]], wr[:].rearrange("p a b -> p (a b)"), R=[Rwr], W=[Rscr[kb]])
                    else:
                        dma("sp", wr[:].rearrange("p a b -> p (a b)"), scrb_d[kb[1]], R=[Rscr[kb]], W=[Rwr])
                    wg, Rwg = yield from ws.take()
                    yield from pipeline([merge_item(l, n, half * 4 + jj, jj, wg, Rwg, wr, Rwr, mgf, Rmgs) for jj in range(4)], W_MRG)
                    WB.put((wg, Rwg))
                    GS.put((wr, Rwr))
            for i in range(4):
                R_ = mgs[i][1]
                R_.w = Rmgs[2 * i + 1].w
                R_.r = list(Rmgs[2 * i].r) + list(Rmgs[2 * i + 1].r) + ([Rmgs[2 * i].w] if Rmgs[2 * i].w else [])
                GS.put(mgs[i])

        def out_item(l, s, wo, gp):
            inv = sm[:, 0:4]
            Rinv = smr("inv")
            bs = []
            for half in range(2):
                b, Rb = yield from gget(PS)
                wt, Rw = wo[half]
                for kc in range(8):
                    mm(b[:, :], mgb[:, kc, s * 128:(s + 1) * 128], wt[:, kc, :], kc == 0, kc == 7, [Rmgb, Rw], [Rb])
                bs.append((b, Rb))
                yield
            ss = sm[:, 160 + 4 * s:164 + 4 * s]
            Rss = smr(f"oss{s}")
            for half in range(2):
                jt, Rj = yield from gget(HS)
                act(jt[:, 0:512], bs[half][0][:, :], AF.Square, [bs[half][1]], [Rj, Rss], accum_out=ss[:, half:half + 1])
                HS.put((jt, Rj))
            tt("dve", ss[:, 2:3], ss[:, 0:1], ss[:, 1:2], ALU.add, [Rss], [Rss])
            act(ss[:, 2:3], ss[:, 2:3], AF.Ln, [Rss], [Rss], scale=1.0 / D_MODEL, bias=EPS)
            act(ss[:, 3:4], ss[:, 2:3], AF.Exp, [Rss], [Rss], scale=-0.5)
            yield
            for half in range(2):
                b, Rb = bs[half]
                tmp, Rtmp = yield from gget(FS)
                stt(tmp[:, :], b[:, :], ss[:, 3:4], gp[half][0][:, :], ALU.mult, ALU.mult, [Rb, Rss, gp[half][1]], [Rtmp])
                PS.put((b, Rb))
                xs = xt[:, s, half * 512:(half + 1) * 512]
                stt(xs, xs, inv[:, s:s + 1], tmp[:, :], ALU.mult, ALU.add, [Rxts[s], Rinv, Rtmp], [Rxts[s]])
                FS.put((tmp, Rtmp))
                yield

        def stage_out(l, ti):
            gp = []
            for half in range(2):
                g_, Rg_ = yield from gget(FS)
                dma("sp", g_[:, :], gpost_d[l:l + 1, half * 512:(half + 1) * 512].broadcast_to([128, 512]), W=[Rg_])
                gp.append((g_, Rg_))
            ws = WStream([(wout_d[l][:, 0:512], 512, l * 26 + 24), (wout_d[l][:, 512:1024], 512, l * 26 + 25)])
            wo = []
            for half in range(2):
                wo.append((yield from ws.take()))
            for s in range(4):
                Rxts[s].w, Rxts[s].r = Rxt.w, list(Rxt.r)
            yield from pipeline([out_item(l, s, wo, gp) for s in range(4)], 2)
            Rxt.w = Rxts[3].w
            Rxt.r = [t for s in range(4) for t in Rxts[s].r] + [Rxts[s].w for s in range(3)]
            for it in wo:
                WB.put(it)
            for it in gp:
                FS.put(it)

        Rxts = [Reg(f"xts{s}") for s in range(4)]
        marks = []

        out_toks = []

        def one_layer(l, ti, extra=()):
            marks.append(("norm", ti, l, P.cnt["pe"]))
            run(norm_T(xt, Rxt, 4, f"gpre{l}", hT, RhT, sm[:, 0:4], smr("inv")))
            marks.append(("branches", ti, l, P.cnt["pe"]))
            run(par(chain_A(l, ti), chain_rest(l, ti), chain_bd(l, ti), *extra))
            marks.append(("merge", ti, l, P.cnt["pe"]))
            run(stage_merge(l, ti))
            marks.append(("out", ti, l, P.cnt["pe"]))
            run(stage_out(l, ti))

        if not pipe:
            for ti in range(NT):
                dma("sp", xt[:], x_d[ti * 512:(ti + 1) * 512, :].rearrange("(s p) d -> p s d", p=128), W=[Rxt])
                for l in range(L):
                    one_layer(l, ti)
                    if ti == NT - 1 and l == 0 and "yT" in tap_d:
                        dma("pool", tap_d["yT"], yT[:], R=RyT)
                out_toks.append(dma("sp", out_d[ti * 512:(ti + 1) * 512, :].rearrange("(s p) d -> p s d", p=128), xt[:], R=[Rxt]))
        else:
            assert L == 1
            send_d = nc.dram_tensor("pp_send", [512, D_MODEL], F32)
            recv_d = nc.dram_tensor("pp_recv", [1024, D_MODEL], F32)
            Rsend, Rrecv = Reg("pp_send"), Reg("pp_recv")
            fo, _fw = poff["flags"]
            fA, fB = prm[:, fo:fo + 1], prm[:, fo + 1:fo + 2]
            klo, khi = prm[:, fo + 4:fo + 5], prm[:, fo + 5:fo + 6]
            gsl = None
            for step in range(NT + 1):
                ti_in = min(step, NT - 1)
                xsrc = x_d[ti_in * 512:(ti_in + 1) * 512, :].rearrange("(s p) d -> p s d", p=128)
                if step == 0:
                    dma("sp", xt[:], xsrc, W=[Rxt])
                else:
                    dma("sp", xt[:], recv_d.ap()[0:512, :].rearrange("(s p) d -> p s d", p=128), R=[Rrecv], W=[Rxt])
                    for s_ in range(4):
                        g_, Rg_ = gsl[s_]
                        gf = g_[:].rearrange("p a b -> p (a b)").bitcast(F32)
                        ts("dve", xt[:, s_, :], xt[:, s_, :], fB, None, ALU.mult, None, [Rxt, Rprm], [Rxt])
                        stt(xt[:, s_, :], gf, fA, xt[:, s_, :], ALU.mult, ALU.add, [Rg_, Rprm, Rxt], [Rxt])
                    for it in gsl:
                        GS.put(it)
                one_layer(0, step, extra=([mem_phase()] if step == 0 else ()))
                if step == 0:
                    def wipe(ap, R):
                        ts("dve", ap, ap, klo, khi, ALU.max, ALU.min, list(R) + [Rprm], list(R))
                    wipe(Sst[:, 0].rearrange("p h d -> p (h d)"), RS[0])
                    wipe(Sbf[:, 0].rearrange("p h d -> p (h d)"), RS[0])
                    wipe(hst[:, 0, :], [Rhst[0]])
                    wipe(haloA[:, 0].rearrange("p c k -> p (c k)"), [Rhalo[0]])
                    wipe(haloB[:, 0].rearrange("p c k -> p (c k)"), [Rhalo[0]])
                    wipe(haloD[:, 0].rearrange("p c k -> p (c k)"), [Rhalo[0]])
                    wipe(kTc[:, 0, 0:128], [Rkv[0]])
                    wipe(vtc[:, 0, 0, :], [Rkv[0]])
                if step < NT:
                    dma("sp", send_d.ap().rearrange("(s p) d -> p s d", p=128), xt[:], R=[Rxt], W=[Rsend])
                    if os.environ.get("PIPE_NOCC"):
                        dma("sp", recv_d.ap()[0:512, :], send_d.ap(), R=[Rsend], W=[Rrecv])
                    else:
                        P.emit("pool", lambda e: e.collective_compute(
                            "AllGather", ALU.bypass, replica_groups=[[0, 1], [2, 3], [4, 5], [6, 7]],
                            ins=[send_d.ap().opt()], outs=[recv_d.ap().opt()]),
                            reads=[Rsend], writes=[Rrecv], cc=True, cost=40000.0)
                extra = [Rsend] if step < NT else []
                if step >= 1:
                    to = step - 1
                    out_toks.append(dma("sp", out_d[to * 512:(to + 1) * 512, :].rearrange("(s p) d -> p s d", p=128), xt[:],
                                        R=[Rxt] + extra))
                if step < NT:
                    tn = min(step + 1, NT - 1)
                    xn = x_d[tn * 512:(tn + 1) * 512, :].rearrange("(s p) d -> p s d", p=128)
                    gsl = [GS.get() for _ in range(4)]
                    for s_ in range(4):
                        g_, Rg_ = gsl[s_]
                        dma("sp", g_[:].rearrange("p a b -> p (a b)").bitcast(F32), xn[:, s_, :], R=extra, W=[Rg_])
        P.finish_wait("sp", out_toks)
        P.run()
        build.stats = dict(cnt=dict(P.cnt), dmas=P.dma_n, marks=marks, model=dict(P.eng_time), busy=dict(P.busy), stall=dict(P.stall))
    return nc


_NC_CACHE = {}


_PER_LAYER = ("g_pre", "g_post", "w_in", "b_in", "a_conv_w", "a_log", "a_dt_bias", "a_norm_g", "b_dw_w", "b_dw_b",
              "b_ln_g", "b_ln_b", "c_sinks", "d_conv_w", "d_conv_b", "d_w_a", "d_b_a", "d_w_x", "d_b_x", "d_lambda",
              "g_mem", "w_mem_kv", "w_br", "w_out")


def kernel(**inputs):
    inp = {k: np.asarray(v) for k, v in inputs.items()}
    B, T, _ = inp["x"].shape
    L = inp["g_pre"].shape[0]
    assert L == 2 and 2 * B <= 8
    key = (T, "pipe")
    if key not in _NC_CACHE:
        _NC_CACHE[key] = build(T, 1, pipe=True)
    nc = _NC_CACHE[key]
    per_stage = []
    for l in range(L):
        inp_l = {k: (v[l:l + 1] if k in _PER_LAYER else v) for k, v in inp.items()}
        prm, winp, bdw, g_post = _host_params(inp_l, 1, stage=l)
        per_stage.append(dict(winp=winp, wbr=np.ascontiguousarray(inp["w_br"][l:l + 1], np.float32),
                              wout=np.ascontiguousarray(inp["w_out"][l:l + 1], np.float32),
                              wmem=np.ascontiguousarray(inp["w_mem_kv"][l:l + 1], np.float32),
                              prm=prm, bdw=bdw, gpost=g_post))
    zx = np.zeros((T, D_MODEL), np.float32)
    in_maps = []
    for b in range(B):
        for stage in range(2):
            m = dict(per_stage[stage])
            m["x"] = np.ascontiguousarray(inp["x"][b], np.float32)
            m["mem"] = np.ascontiguousarray(inp["mem"][b], np.float32)
            in_maps.append(m)
    res = run_bass_kernel_spmd(nc, in_maps, core_ids=list(range(2 * B)))
    kernel.last_all = [np.asarray(r["out"]) for r in res.results]
    return np.stack([np.asarray(res.results[2 * b + 1]["out"]) for b in range(B)], axis=0).astype(np.float32)
```
